# Optimizing a Trainium2 kernel written in Bass

```python
import math
import jax
import jax.numpy as jnp
from jax import lax
import numpy as np

D_MODEL = 1024
BATCH = 8
SEQ = 2048
DEPTH = 2
DEC_BATCH = 32
DEC_SEQ = 1
PAST_LEN = 8192
PAGE_SIZE = 128

N_A_LAYERS = (DEPTH + 1) // 2
N_C_LAYERS = DEPTH // 2
HEAD_DIM = 64
N_HEADS = D_MODEL // 128
N_KV = 2
Q_PER_KV = N_HEADS // N_KV
ROPE_THETA = 10000.0
CMP_STRIDE = 16
L_CMP = 2 * CMP_STRIDE
CMP_HIDDEN = 256
SLC_BLOCK = 64
N_SELECT = 16
WINDOW = 512
WIN_BLOCK = 128
SLC_Q_BLOCK = 64
FORCE_SCORE = 1e9
NEG_BIG = -1e30
SSD_HEAD_DIM = 64
SSD_HEADS = D_MODEL // 128
SSD_INNER = SSD_HEADS * SSD_HEAD_DIM
SSD_GROUPS = 2
SSD_STATE = 128
SSD_CHUNK = 128
SSD_CONV_DIM = SSD_INNER + 2 * SSD_GROUPS * SSD_STATE
CONV_WIDTH = 4
LRU_WIDTH = (4 * D_MODEL // 3) // 128 * 128
LRU_BLOCK = 128
LRU_HEADS = LRU_WIDTH // LRU_BLOCK
LRU_C = 8.0
FFN_HIDDEN = -(-(8 * D_MODEL // 3) // 256) * 256
Q_DIM = N_HEADS * HEAD_DIM
KV_DIM = 2 * N_KV * HEAD_DIM
GATE_DIM = 3 * N_HEADS
A_SPLITS = (Q_DIM, KV_DIM, KV_DIM, KV_DIM, GATE_DIM, SSD_INNER, SSD_CONV_DIM, SSD_HEADS)
IN_A_DIM = Q_DIM + 3 * KV_DIM + GATE_DIM + SSD_INNER + SSD_CONV_DIM + SSD_HEADS
RMS_EPS = 1e-6

kernel_name = 'nsa_ssd_rglru_hybrid_step'


def split_last(x, sizes):
    points = [int(p) for p in np.cumsum(sizes)[:-1]]
    return jnp.split(x, points, axis=-1)


def rmsnorm(x, g):
    xf = x.astype(jnp.float32)
    y = xf * lax.rsqrt(jnp.mean(xf * xf, axis=-1, keepdims=True) + RMS_EPS)
    return (y * g.astype(jnp.float32)).astype(x.dtype)


def masked_softmax(s, mask):
    s = jnp.where(mask, s.astype(jnp.float32), NEG_BIG)
    p = jax.nn.softmax(s, axis=-1)
    return jnp.where(mask, p, 0.0)


def rope(x, pos):
    half = HEAD_DIM // 2
    inv_freq = ROPE_THETA ** (-jnp.arange(half, dtype=jnp.float32) / half)
    ang = pos.astype(jnp.float32)[:, None] * inv_freq[None, :]
    shape = (1, pos.shape[0]) + (1,) * (x.ndim - 3) + (half,)
    cos = jnp.cos(ang).reshape(shape)
    sin = jnp.sin(ang).reshape(shape)
    xf = x.astype(jnp.float32)
    x1, x2 = xf[..., :half], xf[..., half:]
    return jnp.concatenate([x1 * cos - x2 * sin, x2 * cos + x1 * sin], axis=-1).astype(x.dtype)


def causal_conv(u_full, w, b_conv):
    T = u_full.shape[1] - (CONV_WIDTH - 1)
    out = u_full[:, 0:T] * w[0]
    for k in range(1, CONV_WIDTH):
        out = out + u_full[:, k:k + T] * w[k]
    return out + b_conv


def compress_rows(rows, pe, w1, w2):
    b, t_keys = rows.shape[:2]
    n_half = t_keys // CMP_STRIDE
    halves = rows[:, :n_half * CMP_STRIDE].reshape(b, n_half, CMP_STRIDE, N_KV, HEAD_DIM)
    w1 = w1.reshape(2, CMP_STRIDE, HEAD_DIM, CMP_HIDDEN)
    pe = pe.reshape(2, CMP_STRIDE, HEAD_DIM)
    lead = jnp.einsum('bnsgd,sdf->bngf', halves[:, :-1] + pe[0][:, None, :], w1[0])
    trail = jnp.einsum('bnsgd,sdf->bngf', halves[:, 1:] + pe[1][:, None, :], w1[1])
    return jnp.einsum('bngf,fd->bngd', jax.nn.silu(lead + trail), w2)


def nsa_compressed(q, pos_q, k_cmp, v_cmp):
    n_cmp = k_cmp.shape[1]
    s = jnp.einsum('bqgrd,bngd->bgrqn', q, k_cmp) * HEAD_DIM ** -0.5
    end = jnp.arange(n_cmp) * CMP_STRIDE + (L_CMP - 1)
    mask = end[None, :] <= pos_q[:, None]
    p = masked_softmax(s, mask)
    o = jnp.einsum('bgrqn,bngd->bqgrd', p.astype(v_cmp.dtype), v_cmp)
    return o, p


def select_blocks(p_cmp, pos_q, n_sel):
    n_cmp = p_cmp.shape[-1]
    c_start = jnp.arange(n_cmp) * CMP_STRIDE
    s_start = jnp.arange(n_sel) * SLC_BLOCK
    overlap = ((c_start[:, None] < s_start[None, :] + SLC_BLOCK)
               & (c_start[:, None] + L_CMP > s_start[None, :])).astype(jnp.float32)
    imp = jnp.einsum('bgrqn,nj->bgqj', p_cmp, overlap)
    j = jnp.arange(n_sel)[None, :]
    cur = (pos_q // SLC_BLOCK)[:, None]
    valid = s_start[None, :] <= pos_q[:, None]
    forced = (j == 0) | (j == cur) | (j == cur - 1)
    imp = jnp.where(valid & forced, FORCE_SCORE, imp)
    imp = jnp.where(valid, imp, -FORCE_SCORE)
    _, idx = lax.top_k(imp, min(N_SELECT, n_sel))
    return idx


def nsa_selected(q, pos_q, idx, k_blk, v_blk):
    b, g, tq, k = idx.shape
    bi = jnp.arange(b)[:, None, None, None]
    gi = jnp.arange(g)[None, :, None, None]
    kg = k_blk[bi, gi, idx].reshape(b, g, tq, k * SLC_BLOCK, HEAD_DIM)
    vg = v_blk[bi, gi, idx].reshape(b, g, tq, k * SLC_BLOCK, HEAD_DIM)
    key_pos = (idx[..., None] * SLC_BLOCK + jnp.arange(SLC_BLOCK)).reshape(b, g, tq, k * SLC_BLOCK)
    mask = key_pos <= pos_q[None, None, :, None]
    s = jnp.einsum('bqgrd,bgqkd->bgrqk', q, kg) * HEAD_DIM ** -0.5
    p = masked_softmax(s, mask[:, :, None])
    return jnp.einsum('bgrqk,bgqkd->bqgrd', p.astype(vg.dtype), vg)


def window_band(q, k, v, q_pos, k_pos):
    s = jnp.einsum('bnqgrd,bnkgd->bngrqk', q, k) * HEAD_DIM ** -0.5
    dist = q_pos[:, :, None] - k_pos[:, None, :]
    mask = (dist >= 0) & (dist < WINDOW) & (k_pos >= 0)[:, None, :]
    p = masked_softmax(s, mask[None, :, None, None])
    return jnp.einsum('bngrqk,bnkgd->bnqgrd', p.astype(v.dtype), v)


def segsum(a):
    l = a.shape[-1]
    cs = jnp.cumsum(a, axis=-1)
    diff = cs[..., :, None] - cs[..., None, :]
    return jnp.where(jnp.tril(jnp.ones((l, l), dtype=bool)), diff, -jnp.inf)


def ssd_scan(x, a, bm, cm, h0):
    b, T, h, pd = x.shape
    lc = SSD_CHUNK if T % SSD_CHUNK == 0 else T
    nc = T // lc
    xc = x.reshape(b, nc, lc, h, pd)
    bc = bm.reshape(b, nc, lc, h, SSD_STATE)
    cc = cm.reshape(b, nc, lc, h, SSD_STATE)
    ac = a.reshape(b, nc, lc, h).transpose(0, 3, 1, 2)
    a_cs = jnp.cumsum(ac, axis=-1)
    scores = jnp.einsum('bclhn,bcshn->bhcls', cc, bc) * jnp.exp(segsum(ac))
    y_diag = jnp.einsum('bhcls,bcshp->bclhp', scores, xc)
    decay_states = jnp.exp(a_cs[..., -1:] - a_cs)
    states = jnp.einsum('bclhn,bhcl,bclhp->bchpn', bc, decay_states, xc)
    states = jnp.concatenate([h0[:, None], states], axis=1)
    decay_chunk = jnp.exp(segsum(jnp.pad(a_cs[..., -1], ((0, 0), (0, 0), (1, 0)))))
    new_states = jnp.einsum('bhzc,bchpn->bzhpn', decay_chunk, states)
    states, h_final = new_states[:, :-1], new_states[:, -1]
    y_off = jnp.einsum('bclhn,bchpn,bhcl->bclhp', cc, states, jnp.exp(a_cs))
    return (y_diag + y_off).reshape(b, T, h, pd), h_final


def linear_scan(a, bx, h0):
    def combine(left, right):
        return (left[0] * right[0], right[0] * left[1] + right[1])
    a_cum, b_cum = lax.associative_scan(combine, (a, bx), axis=1)
    return a_cum * h0[:, None] + b_cum


def swiglu(x, w_gate, w_up, w_down):
    return (jax.nn.silu(x @ w_gate) * (x @ w_up)) @ w_down


def mixer_a(xn, pos, past, w_in, w_out, cmp_pe_k, cmp_w1_k, cmp_w2_k, cmp_pe_v, cmp_w1_v, cmp_w2_v,
            conv_w, conv_b, dt_bias, a_log, d_skip, ssd_norm):
    b, T, _ = xn.shape
    dtype = xn.dtype
    q, kvc, kvs, kvw, gate, z, xbc, dt = split_last(xn @ w_in, A_SPLITS)
    q = q.reshape(b, T, N_KV, Q_PER_KV, HEAD_DIM)
    q_rot = rope(q, pos)
    kvc = kvc.reshape(b, T, 2, N_KV, HEAD_DIM)
    kvs = kvs.reshape(b, T, 2, N_KV, HEAD_DIM)
    kvs = jnp.stack([rope(kvs[:, :, 0], pos), kvs[:, :, 1]], axis=2)
    kvw = kvw.reshape(b, T, 2, N_KV, HEAD_DIM)
    kvw = jnp.stack([rope(kvw[:, :, 0], pos), kvw[:, :, 1]], axis=2)
    if past is None:
        kvc_all, kvs_all = kvc, kvs
        conv_buf = jnp.zeros((b, CONV_WIDTH - 1, SSD_CONV_DIM), dtype)
        h0 = jnp.zeros((b, SSD_HEADS, SSD_HEAD_DIM, SSD_STATE), jnp.float32)
    else:
        past_c, past_s, win_buf, h0, conv_buf = past
        kvc_all = jnp.concatenate([past_c, kvc], axis=1)
        kvs_all = jnp.concatenate([past_s, kvs], axis=1)
    t_keys = kvc_all.shape[1]
    k_cmp = compress_rows(kvc_all[:, :, 0], cmp_pe_k, cmp_w1_k, cmp_w2_k)
    v_cmp = compress_rows(kvc_all[:, :, 1], cmp_pe_v, cmp_w1_v, cmp_w2_v)
    o_cmp, p_cmp = nsa_compressed(q, pos, k_cmp, v_cmp)
    n_sel = -(-t_keys // SLC_BLOCK)
    idx = select_blocks(p_cmp, pos, n_sel)
    kvs_pad = jnp.pad(kvs_all, ((0, 0), (0, n_sel * SLC_BLOCK - t_keys), (0, 0), (0, 0), (0, 0)))
    blocks = kvs_pad.reshape(b, n_sel, SLC_BLOCK, 2, N_KV, HEAD_DIM).transpose(3, 0, 4, 1, 2, 5)
    k_blk, v_blk = blocks[0], blocks[1]
    if past is None:
        nqb = T // SLC_Q_BLOCK
        qb = q_rot.reshape(b, nqb, SLC_Q_BLOCK, N_KV, Q_PER_KV, HEAD_DIM).transpose(1, 0, 2, 3, 4, 5)
        pb = pos.reshape(nqb, SLC_Q_BLOCK)
        ib = idx.reshape(b, N_KV, nqb, SLC_Q_BLOCK, idx.shape[-1]).transpose(2, 0, 1, 3, 4)
        o_slc = lax.map(lambda a: nsa_selected(a[0], a[1], a[2], k_blk, v_blk), (qb, pb, ib))
        o_slc = o_slc.transpose(1, 0, 2, 3, 4, 5).reshape(b, T, N_KV, Q_PER_KV, HEAD_DIM)
        nb = T // WIN_BLOCK
        n_prev = WINDOW // WIN_BLOCK
        kvw_pad = jnp.pad(kvw, ((0, 0), (n_prev * WIN_BLOCK, 0), (0, 0), (0, 0), (0, 0)))
        kvw_pad = kvw_pad.reshape(b, nb + n_prev, WIN_BLOCK, 2, N_KV, HEAD_DIM)
        band = jnp.concatenate([kvw_pad[:, o:o + nb] for o in range(n_prev + 1)], axis=2)
        k_pos = (jnp.arange(nb)[:, None] - n_prev) * WIN_BLOCK + jnp.arange((n_prev + 1) * WIN_BLOCK)[None, :]
        o_win = window_band(q_rot.reshape(b, nb, WIN_BLOCK, N_KV, Q_PER_KV, HEAD_DIM),
                            band[:, :, :, 0], band[:, :, :, 1], pos.reshape(nb, WIN_BLOCK), k_pos)
        win_new = kvw[:, T - min(WINDOW, T):]
    else:
        o_slc = nsa_selected(q_rot, pos, idx, k_blk, v_blk)
        w_buf = win_buf.shape[1]
        kvw_all = jnp.concatenate([win_buf, kvw], axis=1)
        k_pos = pos[0] - w_buf + jnp.arange(w_buf + T)
        o_win = window_band(q_rot[:, None], kvw_all[:, None, :, 0], kvw_all[:, None, :, 1], pos[None], k_pos[None])
        win_new = kvw_all[:, T:]
    o_win = o_win.reshape(b, T, N_KV, Q_PER_KV, HEAD_DIM)
    g = jax.nn.sigmoid(gate.reshape(b, T, N_KV, Q_PER_KV, 3).astype(jnp.float32)).astype(dtype)
    o_nsa = (g[..., 0:1] * o_cmp + g[..., 1:2] * o_slc + g[..., 2:3] * o_win).reshape(b, T, Q_DIM)
    xbc_full = jnp.concatenate([conv_buf.astype(dtype), xbc], axis=1)
    conv_new = xbc_full[:, -(CONV_WIDTH - 1):]
    xbc_c = jax.nn.silu(causal_conv(xbc_full, conv_w, conv_b))
    xs, bm, cm = split_last(xbc_c, (SSD_INNER, SSD_GROUPS * SSD_STATE, SSD_GROUPS * SSD_STATE))
    xh = xs.reshape(b, T, SSD_HEADS, SSD_HEAD_DIM).astype(jnp.float32)
    rep = SSD_HEADS // SSD_GROUPS
    bm = jnp.repeat(bm.reshape(b, T, SSD_GROUPS, SSD_STATE), rep, axis=2).astype(jnp.float32)
    cm = jnp.repeat(cm.reshape(b, T, SSD_GROUPS, SSD_STATE), rep, axis=2).astype(jnp.float32)
    dt = jax.nn.softplus(dt.astype(jnp.float32) + dt_bias.astype(jnp.float32))
    a_head = -jnp.exp(a_log.astype(jnp.float32))
    y, h_new = ssd_scan(xh * dt[..., None], dt * a_head, bm, cm, h0.astype(jnp.float32))
    y = y + d_skip.astype(jnp.float32)[:, None] * xh
    y = y.reshape(b, T, SSD_INNER) * jax.nn.silu(z.astype(jnp.float32))
    y_ssd = rmsnorm(y, ssd_norm).astype(dtype)
    out = jnp.concatenate([o_nsa, y_ssd], axis=-1) @ w_out
    return out, (kvc, kvs, win_new, h_new, conv_new)


def mixer_c(xn, past, w_in_c, conv_w, conv_b, w_a, b_a, w_x, b_x, lam, w_out_c):
    b, T, _ = xn.shape
    dtype = xn.dtype
    gate_br, x_br = jnp.split(xn @ w_in_c, 2, axis=-1)
    if past is None:
        h0 = jnp.zeros((b, LRU_WIDTH), jnp.float32)
        conv_buf = jnp.zeros((b, CONV_WIDTH - 1, LRU_WIDTH), dtype)
    else:
        h0, conv_buf = past
    x_full = jnp.concatenate([conv_buf.astype(dtype), x_br], axis=1)
    conv_new = x_full[:, -(CONV_WIDTH - 1):]
    xc = causal_conv(x_full, conv_w, conv_b).astype(jnp.float32)
    xblk = xc.reshape(b, T, LRU_HEADS, LRU_BLOCK)
    r = jax.nn.sigmoid(jnp.einsum('bthi,hij->bthj', xblk, w_a.astype(jnp.float32)).reshape(b, T, LRU_WIDTH) + b_a)
    i = jax.nn.sigmoid(jnp.einsum('bthi,hij->bthj', xblk, w_x.astype(jnp.float32)).reshape(b, T, LRU_WIDTH) + b_x)
    log_a = -LRU_C * r * jax.nn.softplus(-lam.astype(jnp.float32))
    a = jnp.exp(log_a)
    gx = jnp.sqrt(-jnp.expm1(2.0 * log_a)) * (i * xc)
    h = linear_scan(a, gx, h0.astype(jnp.float32))
    y = (jax.nn.gelu(gate_br.astype(jnp.float32)) * h).astype(dtype) @ w_out_c
    return y, (h[:, -1], conv_new)


def setup_inputs(seed: int = 0) -> dict:
    key = jax.random.key(seed)
    keys = iter(jax.random.split(key, 64))

    def nrm(shape, scale):
        return jax.random.normal(next(keys), shape, jnp.float32) * scale

    def unif(shape, lo, hi):
        return jax.random.uniform(next(keys), shape, jnp.float32, lo, hi)

    n_pages = PAST_LEN // PAGE_SIZE
    n_used = DEC_BATCH * n_pages
    n_phys = n_used + max(1, n_used // 4)
    w_buf = min(WINDOW, PAST_LEN)
    page_table = jax.random.permutation(next(keys), n_phys)[:n_used].reshape(DEC_BATCH, n_pages).astype(jnp.int32)
    dt0 = jnp.exp(unif((N_A_LAYERS, SSD_HEADS), math.log(1e-3), math.log(1e-1)))
    a0 = unif((N_C_LAYERS, LRU_WIDTH), 0.9, 0.999)
    return {
        'x_prompt': nrm((BATCH, SEQ, D_MODEL), 1.0),
        'x_sample': nrm((DEC_BATCH, DEC_SEQ, D_MODEL), 1.0),
        'cache_kv_cmp': nrm((N_A_LAYERS, n_phys, PAGE_SIZE, 2, N_KV, HEAD_DIM), 1.0),
        'cache_kv_slc': nrm((N_A_LAYERS, n_phys, PAGE_SIZE, 2, N_KV, HEAD_DIM), 1.0),
        'cache_kv_win': nrm((N_A_LAYERS, DEC_BATCH, w_buf, 2, N_KV, HEAD_DIM), 1.0),
        'state_ssm': nrm((N_A_LAYERS, DEC_BATCH, SSD_HEADS, SSD_HEAD_DIM, SSD_STATE), 0.3),
        'state_ssd_conv': nrm((N_A_LAYERS, DEC_BATCH, CONV_WIDTH - 1, SSD_CONV_DIM), 1.0),
        'state_lru': nrm((N_C_LAYERS, DEC_BATCH, LRU_WIDTH), 0.5),
        'state_lru_conv': nrm((N_C_LAYERS, DEC_BATCH, CONV_WIDTH - 1, LRU_WIDTH), 1.0),
        'page_table': page_table,
        'norm_mix': 1.0 + nrm((DEPTH, D_MODEL), 0.01),
        'norm_ffn': 1.0 + nrm((DEPTH, D_MODEL), 0.01),
        'norm_final': 1.0 + nrm((D_MODEL,), 0.01),
        'w_ffn_gate': nrm((DEPTH, D_MODEL, FFN_HIDDEN), D_MODEL ** -0.5),
        'w_ffn_up': nrm((DEPTH, D_MODEL, FFN_HIDDEN), D_MODEL ** -0.5),
        'w_ffn_down': nrm((DEPTH, FFN_HIDDEN, D_MODEL), FFN_HIDDEN ** -0.5),
        'w_in_a': nrm((N_A_LAYERS, D_MODEL, IN_A_DIM), D_MODEL ** -0.5),
        'w_out_a': nrm((N_A_LAYERS, Q_DIM + SSD_INNER, D_MODEL), (Q_DIM + SSD_INNER) ** -0.5),
        'cmp_pe_k': nrm((N_A_LAYERS, L_CMP, HEAD_DIM), 0.1),
        'cmp_w1_k': nrm((N_A_LAYERS, L_CMP * HEAD_DIM, CMP_HIDDEN), (L_CMP * HEAD_DIM) ** -0.5),
        'cmp_w2_k': nrm((N_A_LAYERS, CMP_HIDDEN, HEAD_DIM), CMP_HIDDEN ** -0.5),
        'cmp_pe_v': nrm((N_A_LAYERS, L_CMP, HEAD_DIM), 0.1),
        'cmp_w1_v': nrm((N_A_LAYERS, L_CMP * HEAD_DIM, CMP_HIDDEN), (L_CMP * HEAD_DIM) ** -0.5),
        'cmp_w2_v': nrm((N_A_LAYERS, CMP_HIDDEN, HEAD_DIM), CMP_HIDDEN ** -0.5),
        'ssd_conv_w': nrm((N_A_LAYERS, CONV_WIDTH, SSD_CONV_DIM), CONV_WIDTH ** -0.5),
        'ssd_conv_b': nrm((N_A_LAYERS, SSD_CONV_DIM), 0.1),
        'ssd_dt_bias': dt0 + jnp.log(-jnp.expm1(-dt0)),
        'ssd_a_log': jnp.log(unif((N_A_LAYERS, SSD_HEADS), 1.0, 16.0)),
        'ssd_d': 1.0 + nrm((N_A_LAYERS, SSD_HEADS), 0.1),
        'ssd_norm': 1.0 + nrm((N_A_LAYERS, SSD_INNER), 0.01),
        'w_in_c': nrm((N_C_LAYERS, D_MODEL, 2 * LRU_WIDTH), D_MODEL ** -0.5),
        'lru_conv_w': nrm((N_C_LAYERS, CONV_WIDTH, LRU_WIDTH), CONV_WIDTH ** -0.5),
        'lru_conv_b': nrm((N_C_LAYERS, LRU_WIDTH), 0.1),
        'lru_w_a': nrm((N_C_LAYERS, LRU_HEADS, LRU_BLOCK, LRU_BLOCK), LRU_BLOCK ** -0.5),
        'lru_b_a': nrm((N_C_LAYERS, LRU_WIDTH), 0.1),
        'lru_w_x': nrm((N_C_LAYERS, LRU_HEADS, LRU_BLOCK, LRU_BLOCK), LRU_BLOCK ** -0.5),
        'lru_b_x': nrm((N_C_LAYERS, LRU_WIDTH), 0.1),
        'lru_lambda': jnp.log(a0) - jnp.log1p(-a0),
        'w_out_c': nrm((N_C_LAYERS, LRU_WIDTH, D_MODEL), LRU_WIDTH ** -0.5),
    }


def reference(x_prompt, x_sample, cache_kv_cmp, cache_kv_slc, cache_kv_win, state_ssm, state_ssd_conv,
              state_lru, state_lru_conv, page_table, norm_mix, norm_ffn, norm_final, w_ffn_gate, w_ffn_up,
              w_ffn_down, w_in_a, w_out_a, cmp_pe_k, cmp_w1_k, cmp_w2_k, cmp_pe_v, cmp_w1_v, cmp_w2_v,
              ssd_conv_w, ssd_conv_b, ssd_dt_bias, ssd_a_log, ssd_d, ssd_norm, w_in_c, lru_conv_w, lru_conv_b,
              lru_w_a, lru_b_a, lru_w_x, lru_b_x, lru_lambda, w_out_c):
    T = x_prompt.shape[1]
    db, S = x_sample.shape[:2]
    n_pages = page_table.shape[1]
    past_len = n_pages * PAGE_SIZE
    pos_p = jnp.arange(T, dtype=jnp.int32)
    pos_s = past_len + jnp.arange(S, dtype=jnp.int32)
    xp, xs = x_prompt, x_sample
    cmp_p, slc_p, win_p, ssm_p, sconv_p, lru_p, lconv_p = [], [], [], [], [], [], []
    cmp_s, slc_s, win_s, ssm_s, sconv_s, lru_s, lconv_s = [], [], [], [], [], [], []
    for layer in range(DEPTH):
        hp = rmsnorm(xp, norm_mix[layer])
        hs = rmsnorm(xs, norm_mix[layer])
        if layer % 2 == 0:
            ia = layer // 2
            wa = (w_in_a[ia], w_out_a[ia], cmp_pe_k[ia], cmp_w1_k[ia], cmp_w2_k[ia], cmp_pe_v[ia], cmp_w1_v[ia],
                  cmp_w2_v[ia], ssd_conv_w[ia], ssd_conv_b[ia], ssd_dt_bias[ia], ssd_a_log[ia], ssd_d[ia], ssd_norm[ia])
            mp, (c1, c2, c3, c4, c5) = mixer_a(hp, pos_p, None, *wa)
            past_c = cache_kv_cmp[ia][page_table].reshape(db, past_len, 2, N_KV, HEAD_DIM)
            past_s = cache_kv_slc[ia][page_table].reshape(db, past_len, 2, N_KV, HEAD_DIM)
            past = (past_c, past_s, cache_kv_win[ia], state_ssm[ia], state_ssd_conv[ia])
            ms, (d1, d2, d3, d4, d5) = mixer_a(hs, pos_s, past, *wa)
            cmp_p.append(c1); slc_p.append(c2); win_p.append(c3); ssm_p.append(c4); sconv_p.append(c5)
            cmp_s.append(d1); slc_s.append(d2); win_s.append(d3); ssm_s.append(d4); sconv_s.append(d5)
        else:
            ic = layer // 2
            wc = (w_in_c[ic], lru_conv_w[ic], lru_conv_b[ic], lru_w_a[ic], lru_b_a[ic], lru_w_x[ic], lru_b_x[ic],
                  lru_lambda[ic], w_out_c[ic])
            mp, (e1, e2) = mixer_c(hp, None, *wc)
            ms, (f1, f2) = mixer_c(hs, (state_lru[ic], state_lru_conv[ic]), *wc)
            lru_p.append(e1); lconv_p.append(e2)
            lru_s.append(f1); lconv_s.append(f2)
        xp = xp + mp
        xs = xs + ms
        xp = xp + swiglu(rmsnorm(xp, norm_ffn[layer]), w_ffn_gate[layer], w_ffn_up[layer], w_ffn_down[layer])
        xs = xs + swiglu(rmsnorm(xs, norm_ffn[layer]), w_ffn_gate[layer], w_ffn_up[layer], w_ffn_down[layer])
    y_prompt = rmsnorm(xp, norm_final)
    y_sample = rmsnorm(xs, norm_final)
    return (y_prompt, y_sample,
            jnp.stack(cmp_p), jnp.stack(slc_p), jnp.stack(win_p), jnp.stack(ssm_p), jnp.stack(sconv_p),
            jnp.stack(lru_p), jnp.stack(lconv_p),
            jnp.stack(cmp_s), jnp.stack(slc_s), jnp.stack(win_s), jnp.stack(ssm_s), jnp.stack(sconv_s),
            jnp.stack(lru_s), jnp.stack(lconv_s))
```

```python
import numpy as np
from contextlib import ExitStack
import concourse.bass as bass
import concourse.mybir as mybir
from concourse.bass_utils import run_bass_kernel_spmd

F32 = mybir.dt.float32
BF16 = mybir.dt.bfloat16
I32 = mybir.dt.int32
ALU = mybir.AluOpType
AF = mybir.ActivationFunctionType
AX = mybir.AxisListType

T = 2048
D = 1024
NT = 16
NS = 4
NPHYS = 2560
FFN = 2816
LRU = 1280
EPS = 1e-6
NEG = -1e30


def _region(ap):
    t = ap.tensor
    name = t.name
    dsz = mybir.dt.size(ap.dtype)
    pairs = [(int(s), int(c)) for s, c in ap.ap]
    off = int(ap.offset)
    if 'DRam' in type(t).__name__:
        lo = off + sum(min(0, s * (c - 1)) for s, c in pairs)
        hi = off + sum(max(0, s * (c - 1)) for s, c in pairs) + 1
        return (name, 0, 1, lo * dsz, hi * dsz)
    if 'PSum' in type(t).__name__:
        return (name, 0, 128, 0, 2048, True)
    R = 1
    for d in list(t.shape)[1:]:
        R *= int(d)
    ps, pc = pairs[0]
    p_lo = off // R
    pstep = max(1, ps // R) if ps else 0
    p_hi = p_lo + (pc - 1) * pstep + 1
    f0 = off % R
    lo = f0 + sum(min(0, s * (c - 1)) for s, c in pairs[1:])
    hi = f0 + sum(max(0, s * (c - 1)) for s, c in pairs[1:]) + 1
    return (name, p_lo, p_hi, lo * dsz, hi * dsz)


def _ovl(a, b):
    return a[1] < b[2] and b[1] < a[2] and a[3] < b[4] and b[3] < a[4]


def _cov(a, b):
    return a[1] <= b[1] and a[2] >= b[2] and a[3] <= b[3] and a[4] >= b[4]


class Op:
    __slots__ = ('i', 'eng', 'fn', 'dma', 'deps', 'signal', 'sem', 'semval', 'waits')


class Sched:
    ENGS = ['pe', 'act', 'pool', 'dve', 'sp']

    def __init__(self, nc, stack, ndma=56):
        self.nc = nc
        self.esem = {e: stack.enter_context(nc.semaphore('es_' + e)) for e in self.ENGS}
        self.dsem = [stack.enter_context(nc.semaphore('ds_%d' % i)) for i in range(ndma)]
        self.duse = [0] * ndma
        self.dlast = [None] * ndma
        self.dk = 0
        self.dk2 = 0
        self.ecnt = {e: 0 for e in self.ENGS}
        self.waited = {e: {} for e in self.ENGS}
        self.pending = []
        self.acc = {}
        self.n = 0
        self.last = {e: None for e in self.ENGS}
        self.dma_since = []

    def add(self, eng, fn, reads=(), writes=(), dma=False, extra_deps=()):
        op = Op()
        op.i = self.n
        self.n += 1
        op.eng = eng
        op.fn = fn
        op.dma = dma
        op.deps = set(extra_deps)
        op.signal = False
        op.sem = None
        op.semval = 0
        op.waits = []
        rr = [_region(a) for a in reads]
        ww = [_region(a) for a in writes]
        for r in rr:
            psum = len(r) > 5
            for (reg, o, isw) in self.acc.get(r[0], ()):
                if (isw or (psum and (o.eng != eng or o.dma != dma))) and _ovl(reg, r):
                    op.deps.add(o)
        for w in ww:
            for (reg, o, isw) in self.acc.get(w[0], ()):
                if _ovl(reg, w):
                    op.deps.add(o)
        for w in ww:
            lst = self.acc.setdefault(w[0], [])
            lst[:] = [x for x in lst if not _cov(w, x[0])]
            lst.append((w, op, True))
        for r in rr:
            lst = self.acc.setdefault(r[0], [])
            lst[:] = [x for x in lst if not ((not x[2]) and x[1].eng == eng and x[1].dma == dma and _cov(r, x[0]))]
            lst.append((r, op, False))
        op.deps.discard(op)
        if dma:
            half = len(self.dsem) // 2
            if eng == 'pool':
                k = half + self.dk2 % (len(self.dsem) - half)
                self.dk2 += 1
            else:
                k = self.dk % half
                self.dk += 1
            if self.dlast[k] is not None:
                op.deps.add(self.dlast[k])
            self.duse[k] += 1
            op.sem = self.dsem[k]
            op.semval = 16 * self.duse[k]
            self.dlast[k] = op
            self.dma_since.append(op)
        else:
            self.last[eng] = op
        self.pending.append(op)
        return op

    def barrier(self):
        deps = [o for o in self.last.values() if o is not None] + list(self.dma_since)
        for e in self.ENGS:
            self.add(e, None, extra_deps=list(deps))
        self.dma_since = []
        self.acc = {}
        self.last = {e: None for e in self.ENGS}

    def flush(self):
        nc = self.nc
        pend = self.pending
        self.pending = []
        if not pend:
            return
        first_i = pend[0].i
        for op in pend:
            for d in op.deps:
                if d.dma or d.i < first_i:
                    continue
                if d.eng == 'pe' and op.eng == 'pe' and not op.dma:
                    continue
                d.signal = True
        for op in pend:
            if not op.dma and op.signal:
                self.ecnt[op.eng] += 1
                op.sem = self.esem[op.eng]
                op.semval = self.ecnt[op.eng]
        per = {e: [] for e in self.ENGS}
        for op in pend:
            wl = {}
            for d in op.deps:
                if d.i < first_i and not d.dma:
                    continue
                if (not d.dma) and d.eng == 'pe' and op.eng == 'pe' and not op.dma:
                    continue
                if d.sem is None:
                    continue
                k = id(d.sem)
                if k not in wl or wl[k][1] < d.semval:
                    wl[k] = (d.sem, d.semval)
            wd = self.waited[op.eng]
            for k, (s, v) in wl.items():
                if wd.get(k, 0) >= v:
                    continue
                wd[k] = v
                op.waits.append((s, v))
            per[op.eng].append(op)

        def emit(e, ops):
            for op in ops:
                for (s, v) in op.waits:
                    e.wait_ge(s, v)
                if op.fn is None:
                    continue
                ins = op.fn(e)
                if op.dma:
                    ins.then_inc(op.sem, 16)
                elif op.signal:
                    ins.then_inc(op.sem, 1)

        with nc.Block() as blk:
            @blk.tensor
            def _(e):
                emit(e, per['pe'])

            @blk.scalar
            def _(e):
                emit(e, per['act'])

            @blk.gpsimd
            def _(e):
                emit(e, per['pool'])

            @blk.vector
            def _(e):
                emit(e, per['dve'])

            @blk.sync
            def _(e):
                emit(e, per['sp'])


class K:
    def __init__(self, S):
        self.S = S
        self.rr = 0

    def dma(self, out, in_, q='sp', slow=False):
        if slow:
            return self.S.add(q, lambda e: e.dma_start(out=out, in_=in_, allow_slow_non_contiguous=True), reads=[in_], writes=[out], dma=True)
        return self.S.add(q, lambda e: e.dma_start(out=out, in_=in_), reads=[in_], writes=[out], dma=True)

    def mm(self, out, lhsT, rhs, start=True, stop=True):
        return self.S.add('pe', lambda e: e.matmul(out, lhsT=lhsT, rhs=rhs, start=start, stop=stop), reads=[lhsT, rhs], writes=[out])

    def tr(self, out, in_, ident):
        return self.S.add('pe', lambda e: e.transpose(out=out, in_=in_, identity=ident), reads=[in_, ident], writes=[out])

    def tt(self, out, in0, in1, op, eng='dve'):
        return self.S.add(eng, lambda e: e.tensor_tensor(out=out, in0=in0, in1=in1, op=op), reads=[in0, in1], writes=[out])

    def ts(self, out, in0, s1, s2, op0, op1=None, eng='dve'):
        rd = [in0] + [s for s in (s1, s2) if not isinstance(s, (int, float, type(None)))]
        if op1 is None:
            return self.S.add(eng, lambda e: e.tensor_scalar(out=out, in0=in0, scalar1=s1, scalar2=None, op0=op0), reads=rd, writes=[out])
        return self.S.add(eng, lambda e: e.tensor_scalar(out=out, in0=in0, scalar1=s1, scalar2=s2, op0=op0, op1=op1), reads=rd, writes=[out])

    def stt(self, out, in0, scalar, in1, op0, op1):
        rd = [in0, in1] + ([scalar] if not isinstance(scalar, (int, float)) else [])
        return self.S.add('dve', lambda e: e.scalar_tensor_tensor(out=out, in0=in0, scalar=scalar, in1=in1, op0=op0, op1=op1), reads=rd, writes=[out])

    def act(self, out, in_, func, bias=None, scale=None, accum=None):
        rd = [in_]
        wr = [out]
        kw = {}
        if bias is not None:
            kw['bias'] = bias
            if not isinstance(bias, (int, float)):
                rd.append(bias)
        if scale is not None:
            kw['scale'] = scale
            if not isinstance(scale, (int, float)):
                rd.append(scale)
        if accum is not None:
            kw['accum_out'] = accum
            wr.append(accum)
        return self.S.add('act', lambda e: e.activation(out=out, in_=in_, func=func, **kw), reads=rd, writes=wr)

    def cp(self, out, in_, eng='dve'):
        if eng == 'act':
            return self.S.add('act', lambda e: e.copy(out=out, in_=in_), reads=[in_], writes=[out])
        return self.S.add(eng, lambda e: e.tensor_copy(out=out, in_=in_), reads=[in_], writes=[out])

    def cpr(self, out, in_):
        self.rr += 1
        return self.cp(out, in_, eng=('dve', 'act')[self.rr % 2])

    def memset(self, ap, v, eng='pool'):
        return self.S.add(eng, lambda e: e.memset(ap, v), writes=[ap])

    def recip(self, out, in_):
        return self.S.add('dve', lambda e: e.reciprocal(out=out, in_=in_), reads=[in_], writes=[out])

    def rmax(self, out, in_):
        return self.S.add('dve', lambda e: e.tensor_reduce(out=out, in_=in_, axis=AX.X, op=ALU.max), reads=[in_], writes=[out])


def bc(ap, shape):
    return ap.to_broadcast(list(shape))


def rr(*gens):
    gens = [g for g in gens if g is not None]
    while gens:
        for g in list(gens):
            try:
                next(g)
            except StopIteration:
                gens.remove(g)


def zip2(bodies, half):
    def adv(g):
        try:
            next(g)
            return True
        except StopIteration:
            return False
    cur, cnt = None, 0
    for mk in bodies:
        nxt, ncnt = mk(), 0
        if cur is not None:
            while cnt < half:
                if not adv(cur):
                    cur = None
                    break
                cnt += 1
            while cur is not None:
                if not adv(cur):
                    cur = None
                    break
                if adv(nxt):
                    ncnt += 1
        cur, cnt = nxt, ncnt
    while cur is not None and adv(cur):
        pass


def zipp(bodies, half, pattern):
    def adv(g):
        try:
            next(g)
            return True
        except StopIteration:
            return False
    cur = None
    for mk in bodies:
        nxt = mk()
        if cur is None:
            for _ in range(half):
                adv(nxt)
            cur = nxt
            continue
        for ch in pattern:
            adv(cur if ch == 'A' else nxt)
        while adv(cur):
            pass
        cur = nxt
    while cur is not None and adv(cur):
        pass


def dram_bcast(ap1d, nparts, width):
    return bass.AP(ap1d.tensor, int(ap1d.offset), [[0, nparts], [1, width]])


def build_program():
    nc = bass.Bass("TRN2", target_bir_lowering=False)

    def din(name, shape, dt=F32):
        return nc.dram_tensor(name, list(shape), dt, kind="ExternalInput").ap()

    def dout(name, shape):
        return nc.dram_tensor(name, list(shape), F32, kind="ExternalOutput").ap()

    I = dict(
        xp=din('xp', [T, D]), xs=din('xs', [NS, D]),
        ccmp=din('ccmp', [NPHYS * 128, 256]), cslc=din('cslc', [NPHYS * 128, 256]),
        cwin=din('cwin', [NS, 512, 256]), sssm=din('sssm', [NS, 512, 128]),
        sconv=din('sconv', [NS, 3, 1024]), slru=din('slru', [NS, LRU]), slconv=din('slconv', [NS, 3, LRU]),
        ptab=din('ptab', [NS, 64], I32),
        norm_mix=din('norm_mix', [2, D]), norm_ffn=din('norm_ffn', [2, D]), norm_final=din('norm_final', [D]),
        wg=din('wg', [2, D, FFN]), wu=din('wu', [2, D, FFN]), wd=din('wd', [2, FFN, D]),
        win=din('win', [D, 2848]), wout=din('wout', [D, D]),
        pek=din('pek', [32, 64]), w1k=din('w1k', [2048, 256]), w2k=din('w2k', [256, 64]),
        pev=din('pev', [32, 64]), w1v=din('w1v', [2048, 256]), w2v=din('w2v', [256, 64]),
        cw=din('cw', [4, 1024]), cb=din('cb', [1024]), dtb=din('dtb', [8]), alog=din('alog', [8]),
        dsk=din('dsk', [8]), snorm=din('snorm', [512]),
        winc=din('winc', [D, 2 * LRU]), lcw=din('lcw', [4, LRU]), lcb=din('lcb', [LRU]),
        lwa=din('lwa', [10, 128, 128]), lba=din('lba', [LRU]), lwx=din('lwx', [10, 128, 128]), lbx=din('lbx', [LRU]),
        lam=din('lam', [LRU]), woutc=din('woutc', [LRU, D]),
        cst=din('cst', [128, 768]), ropetab=din('ropetab', [T + NS, 64]),
        ovl=din('ovl', [128, 32]), cst2=din('cst2', [128, 4]), ovls=din('ovls', [512, 129]),
    )
    O = dict(
        y_p=dout('y_p', [T, D]), y_s=dout('y_s', [NS, D]),
        kvc_p=dout('kvc_p', [T, 256]), kvs_p=dout('kvs_p', [T, 256]), kvw_p=dout('kvw_p', [512, 256]),
        ssm_p=dout('ssm_p', [512, 128]), sconv_p=dout('sconv_p', [3, 1024]),
        lru_p=dout('lru_p', [LRU]), lconv_p=dout('lconv_p', [3, LRU]),
        kvc_s=dout('kvc_s', [NS, 256]), kvs_s=dout('kvs_s', [NS, 256]), kvw_s=dout('kvw_s', [NS, 512, 256]),
        ssm_s=dout('ssm_s', [NS, 512, 128]), sconv_s=dout('sconv_s', [NS, 3, 1024]),
        lru_s=dout('lru_s', [NS, LRU]), lconv_s=dout('lconv_s', [NS, 3, LRU]),
    )
    xres = nc.dram_tensor('xres', [T + NS, D], F32, kind="Internal").ap()

    tiles = [(t, 128, t * 128, t * 128, None) for t in range(NT)] + [(NT + b, 1, 8192, T + b, b) for b in range(NS)]

    with ExitStack() as top:
        S = Sched(nc, top)
        kk = K(S)
        _uid = [0]

        def sb(st, name, shape, dt=F32):
            _uid[0] += 1
            return st.enter_context(nc.sbuf_tensor('s%d_%s' % (_uid[0], name), list(shape), dt))
        cst = sb(top, 'cst', [128, 768])
        ident_bf = sb(top, 'ident_bf', [128, 128], BF16)
        kk.dma(cst[:], I['cst'])
        kk.cp(ident_bf[:], cst[:, 0:128])
        ident_f = cst[:, 0:128]
        Lstrict = cst[:, 128:256]
        Utri = cst[:, 256:384]
        causal = cst[:, 384:512]
        wlo = cst[:, 512:640]
        ones_f = cst[:, 640:768]
        QTs = sb(top, 'QTs', [128, 8, NS], BF16)
        KAs = sb(top, 'KAs', [128, 2, NS], BF16)
        KBs = sb(top, 'KBs', [128, 2, NS], BF16)
        VSs = sb(top, 'VSs', [1, NS, 2, 128], BF16)
        Gs = sb(top, 'Gs', [1, NS, 24])
        YCs = sb(top, 'YCs', [1, NS, 512], BF16)
        cst2 = sb(top, 'cst2', [128, 4])
        kk.dma(cst2[:], I['cst2'])
        l0 = ExitStack()
        QT = sb(l0, 'QT', [128, 8, T], BF16)
        KA = sb(l0, 'KA', [128, 2, T], BF16)
        KB = sb(l0, 'KB', [128, 2, T], BF16)
        VS = sb(l0, 'VS', [128, NT, 2, 128], BF16)
        G = sb(l0, 'G', [128, NT, 24])
        YC = sb(l0, 'YC', [128, NT, 512], BF16)

        def common_work(st):
            W = {}
            W['xt'] = [sb(st, 'xt0', [128, D]), sb(st, 'xt1', [128, D])]
            W['xn'] = sb(st, 'xn', [128, D], BF16)
            W['xnT'] = sb(st, 'xnT', [128, 8, 128], BF16)
            W['gain'] = sb(st, 'gain', [128, D])
            W['ssq'] = sb(st, 'ssq', [128, 1])
            W['rstd'] = sb(st, 'rstd', [128, 1])
            W['junk'] = sb(st, 'junk', [128, D])
            _uid[0] += 1
            W['B'] = [st.enter_context(nc.psum_tensor('B%d_%d' % (_uid[0], i), [128, 512], F32)) for i in range(8)]
            return W

        def bfv(bank, a, b):
            return bank[:].bitcast(BF16).rearrange('p (a b) -> p a b', a=a, b=b)

        def rstd_of(junk, x, n, width, ssq, rstd):
            S.add('dve', lambda e: e.scalar_tensor_tensor(out=junk[:n, 0:width], in0=x[:n], scalar=1.0, in1=x[:n], op0=ALU.mult, op1=ALU.mult, accum_out=ssq[:n]),
                  reads=[x[:n]], writes=[junk[:n, 0:width], ssq[:n]])
            kk.act(rstd[:n], ssq[:n], AF.Ln, scale=1.0 / width, bias=EPS)
            kk.act(rstd[:n], rstd[:n], AF.Exp, scale=-0.5)

        def norm_T(W, xt, n, xnT=None):
            xnT = W['xnT'] if xnT is None else xnT
            rstd_of(W['junk'], xt, n, D, W['ssq'], W['rstd'])
            kk.stt(W['xn'][:n], xt[:n], W['rstd'][:n, 0:1], W['gain'][:n], ALU.mult, ALU.mult)
            psT = bfv(W['B'][0], 8, 128)
            for kc in range(8):
                kk.tr(psT[:, kc, :n], W['xn'][:n, kc * 128:(kc + 1) * 128], ident_bf[:n, :n])
            kk.cpr(xnT[:, :, :n], psT[:, :, :n])

        def load_w(dst, src2d, stg, K_, w):
            kk.dma(stg[:, :K_, :w], src2d.rearrange('(k p) w -> p k w', p=128))
            kk.cpr(dst, stg[:, :K_, :w])

        WGS = [nc.dram_tensor('wgs%d' % l_, [22, 128, 1024], BF16, kind="Internal").ap() for l_ in range(2)]
        WUS = [nc.dram_tensor('wus%d' % l_, [22, 128, 1024], BF16, kind="Internal").ap() for l_ in range(2)]

        def ffn_precast(l_, st, q_='sp'):
            sg_ = sb(st, 'pcg%d' % l_, [128, 8, 128])
            su_ = sb(st, 'pcu%d' % l_, [128, 8, 128])
            bg_ = sb(st, 'pbg%d' % l_, [128, 1024], BF16)
            bu_ = sb(st, 'pbu%d' % l_, [128, 1024], BF16)
            for f in range(22):
                kk.dma(sg_[:], I['wg'][l_][:, f * 128:(f + 1) * 128].rearrange('(k p) w -> p k w', p=128), q=q_)
                kk.dma(su_[:], I['wu'][l_][:, f * 128:(f + 1) * 128].rearrange('(k p) w -> p k w', p=128), q=q_)
                kk.cp(bg_[:], sg_[:].rearrange('p k w -> p (k w)'), eng='act')
                kk.cp(bu_[:], su_[:].rearrange('p k w -> p (k w)'), eng='dve')
                kk.dma(WGS[l_][f], bg_[:], q=q_)
                kk.dma(WUS[l_][f], bu_[:], q=q_)
                yield

        with ExitStack() as ph:
            W = common_work(ph)
            WA = sb(ph, 'WA', [128, 8, 1304], BF16)
            with ExitStack() as ld:
                stg = [sb(ld, 'stgA0', [128, 8, 512]), sb(ld, 'stgA1', [128, 8, 512])]
                segs = [(0, 512, 0), (768, 896, 512), (1024, 1152, 640), (512, 768, 768), (896, 1024, 1024), (1152, 1280, 1152), (1280, 1304, 1280)]
                for i, (s0, s1, d0) in enumerate(segs):
                    load_w(WA[:, :, d0:d0 + (s1 - s0)], I['win'][:, s0:s1], stg[i % 2], 8, s1 - s0)
                kk.dma(W['gain'][:], dram_bcast(I['norm_mix'][0], 128, D))
                S.barrier(); S.flush()
            PA = []
            for par in range(2):
                d = {}
                d['proj'] = sb(ph, 'projA%d' % par, [128, 1304])
                d['rot'] = sb(ph, 'rot%d' % par, [128, 768])
                d['rtab'] = sb(ph, 'rtab%d' % par, [128, 64])
                for nm in ('r1', 'r2', 'r3', 'r4'):
                    d[nm] = sb(ph, nm + '_%d' % par, [128, 12, 32])
                d['tb'] = sb(ph, 'tb%d' % par, [128, 12, 128], BF16)
                if par == 0:
                    d['xn'], d['xnT'], d['junk'] = W['xn'], W['xnT'], W['junk']
                else:
                    d['xn'] = sb(ph, 'xnA%d' % par, [128, D], BF16)
                    d['xnT'] = sb(ph, 'xnTA%d' % par, [128, 8, 128], BF16)
                    d['junk'] = sb(ph, 'junkA%d' % par, [128, D], BF16)
                d['ssq'] = sb(ph, 'ssqA%d' % par, [128, 1])
                d['rstd'] = sb(ph, 'rstdA%d' % par, [128, 1])
                PA.append(d)

            def bodyA(i):
                (ti, n, pos0, row0, sb_) = tiles[i]
                P = PA[i % 2]
                pa, pb, pc, pd = [W['B'][4 * (i % 2) + k_] for k_ in range(4)]
                proj, rot, rtab, r1, r2, r3, r4, tb = P['proj'], P['rot'], P['rtab'], P['r1'], P['r2'], P['r3'], P['r4'], P['tb']
                xt = W['xt'][i % 2]
                src = I['xp'][row0:row0 + n, :] if sb_ is None else I['xs'][sb_:sb_ + 1, :]
                kk.dma(xt[:n], src)
                kk.dma(rtab[:n], I['ropetab'][row0:row0 + n, :])
                rstd_of(P['junk'], xt, n, D, P['ssq'], P['rstd'])
                yield
                kk.stt(P['xn'][:n], xt[:n], P['rstd'][:n, 0:1], W['gain'][:n], ALU.mult, ALU.mult)
                yield
                psT = bfv(pa, 8, 128)
                for kc in range(8):
                    kk.tr(psT[:, kc, :n], P['xn'][:n, kc * 128:(kc + 1) * 128], ident_bf[:n, :n])
                yield
                kk.cp(P['xnT'][:, :, :n], psT[:, :, :n], eng='act')
                yield
                xnT = P['xnT']
                blks = [(0, 512, pb), (512, 1024, pc), (1024, 1304, pd)]
                for (c0, c1, pbk) in blks:
                    for kc in range(8):
                        kk.mm(pbk[:n, 0:c1 - c0], xnT[:, kc, :n], WA[:, kc, c0:c1], start=(kc == 0), stop=(kc == 7))
                yield
                kk.cp(proj[:n, 0:512], pb[:n, 0:512], eng='dve')
                kk.cp(proj[:n, 512:1024], pc[:n, 0:512], eng='act')
                kk.cp(proj[:n, 1024:1304], pd[:n, 0:280], eng='act')
                yield
                pv = proj[:n, 0:768].rearrange('p (h t d) -> p h t d', h=12, t=2, d=32)
                rv = rot[:n, :].rearrange('p (h t d) -> p h t d', h=12, t=2, d=32)
                cosb = bc(rtab[:n, 0:32].rearrange('p (o d) -> p o d', o=1), [n, 12, 32])
                sinb = bc(rtab[:n, 32:64].rearrange('p (o d) -> p o d', o=1), [n, 12, 32])
                kk.tt(r1[:n], pv[:, :, 0, :], cosb, ALU.mult)
                kk.tt(r3[:n], pv[:, :, 1, :], cosb, ALU.mult)
                kk.tt(r2[:n], pv[:, :, 1, :], sinb, ALU.mult, eng='pool')
                kk.tt(r4[:n], pv[:, :, 0, :], sinb, ALU.mult, eng='pool')
                gdst = G[:n, ti, :] if sb_ is None else Gs[0:1, sb_, :]
                kk.act(gdst, proj[:n, 1280:1304], AF.Exp, scale=-1.0)
                kk.ts(gdst, gdst, 1.0, None, ALU.add)
                kk.recip(gdst, gdst)
                if sb_ is None:
                    kk.cp(VS[:n, ti, 0, :], proj[:n, 1024:1152], eng='act')
                    kk.cp(VS[:n, ti, 1, :], proj[:n, 1152:1280], eng='act')
                else:
                    kk.cp(VSs[0:1, sb_, 0, :], proj[:n, 1024:1152], eng='act')
                    kk.cp(VSs[0:1, sb_, 1, :], proj[:n, 1152:1280], eng='act')
                kk.cp(tb[:n, 0:8, 0:64], proj[:n, 0:512].rearrange('p (h d) -> p h d', h=8), eng='act')
                kk.cp(tb[:n, 8:10, 0:64], proj[:n, 768:896].rearrange('p (h d) -> p h d', h=2), eng='act')
                kk.cp(tb[:n, 10:12, 0:64], proj[:n, 896:1024].rearrange('p (h d) -> p h d', h=2), eng='act')
                yield
                kk.tt(rv[:, :, 0, :], r1[:n], r2[:n], ALU.subtract)
                kk.tt(rv[:, :, 1, :], r3[:n], r4[:n], ALU.add)
                kk.cp(tb[:n, 0:8, 64:128], rot[:n, 0:512].rearrange('p (h d) -> p h d', h=8), eng='dve')
                kk.cp(tb[:n, 8:10, 64:128], rot[:n, 512:640].rearrange('p (h d) -> p h d', h=2), eng='dve')
                kk.cp(tb[:n, 10:12, 64:128], rot[:n, 640:768].rearrange('p (h d) -> p h d', h=2), eng='dve')
                yield
                if sb_ is None:
                    kk.dma(O['kvc_p'][row0:row0 + n, :], proj[:n, 768:1024], q='pool')
                    kk.dma(O['kvs_p'][row0:row0 + n, 0:128], rot[:n, 512:640], q='pool')
                    kk.dma(O['kvs_p'][row0:row0 + n, 128:256], proj[:n, 1024:1152], q='pool')
                    if ti >= NT - 4:
                        r = row0 - (T - 512)
                        kk.dma(O['kvw_p'][r:r + n, 0:128], rot[:n, 640:768], q='pool')
                        kk.dma(O['kvw_p'][r:r + n, 128:256], proj[:n, 1152:1280], q='pool')
                else:
                    kk.dma(O['kvc_s'][sb_:sb_ + 1, :], proj[:n, 768:1024], q='pool')
                    kk.dma(O['kvs_s'][sb_:sb_ + 1, 0:128], rot[:n, 512:640], q='pool')
                    kk.dma(O['kvs_s'][sb_:sb_ + 1, 128:256], proj[:n, 1024:1152], q='pool')
                    kk.dma(O['kvw_s'][sb_, 511:512, 0:128], rot[:n, 640:768], q='pool')
                    kk.dma(O['kvw_s'][sb_, 511:512, 128:256], proj[:n, 1152:1280], q='pool')
                    kk.dma(O['kvw_s'][sb_, 0:511, :], I['cwin'][sb_, 1:512, :], q='pool')
                psA = bfv(pa, 8, 128)
                psB = bfv(pd, 8, 128)
                for j in range(8):
                    kk.tr(psA[:, j, :n], tb[:n, j, :], ident_bf[:n, :n])
                for j in range(4):
                    kk.tr(psB[:, j, :n], tb[:n, 8 + j, :], ident_bf[:n, :n])
                yield
                if sb_ is None:
                    kk.cp(QT[:, :, row0:row0 + n], psA[:, :, :n], eng='act')
                    kk.cp(KA[:, :, row0:row0 + n], psB[:, 0:2, :n], eng='dve')
                    kk.cp(KB[:, :, row0:row0 + n], psB[:, 2:4, :n], eng='dve')
                else:
                    kk.cp(QTs[:, :, sb_:sb_ + 1], psA[:, :, :n], eng='act')
                    kk.cp(KAs[:, :, sb_:sb_ + 1], psB[:, 0:2, :n], eng='dve')
                    kk.cp(KBs[:, :, sb_:sb_ + 1], psB[:, 2:4, :n], eng='dve')
                yield

            HALF_A = 5
            cur = bodyA(0)
            for _ in range(HALF_A):
                next(cur)
            for i in range(1, len(tiles)):
                nxt = bodyA(i)
                done = False
                while not done:
                    try:
                        next(cur)
                    except StopIteration:
                        done = True
                    if not done:
                        try:
                            next(nxt)
                        except StopIteration:
                            pass
                cur = nxt
            for _ in cur:
                pass
            S.barrier(); S.flush()

        with ExitStack() as ph:
            W = common_work(ph)
            WB = sb(ph, 'WB', [128, 8, 1544], BF16)
            cw_f = sb(ph, 'cw_f', [128, 8, 4])
            cbias_f = sb(ph, 'cbias_f', [128, 8])
            dtb_b = sb(ph, 'dtb_b', [128, 8])
            Aneg_b = sb(ph, 'Aneg_b', [128, 8])
            D_b = sb(ph, 'D_b', [128, 8])
            snorm_b = sb(ph, 'snorm_b', [128, 512])
            with ExitStack() as ld:
                stg = [sb(ld, 'stgB0', [128, 8, 512]), sb(ld, 'stgB1', [128, 8, 512])]
                segs = [(1304, 1816, 0), (2840, 2848, 512), (1816, 2328, 520), (2328, 2840, 1032)]
                for i, (s0, s1, d0) in enumerate(segs):
                    load_w(WB[:, :, d0:d0 + (s1 - s0)], I['win'][:, s0:s1], stg[i % 2], 8, s1 - s0)
                kk.dma(W['gain'][:], dram_bcast(I['norm_mix'][0], 128, D))
                for k_ in range(4):
                    kk.dma(cw_f[:, :, k_], I['cw'][k_].rearrange('(c p) -> p c', p=128), slow=True)
                kk.dma(cbias_f[:], I['cb'].rearrange('(c p) -> p c', p=128), slow=True)
                kk.dma(dtb_b[:], dram_bcast(I['dtb'], 128, 8))
                kk.dma(Aneg_b[:], dram_bcast(I['alog'], 128, 8))
                kk.dma(D_b[:], dram_bcast(I['dsk'], 128, 8))
                kk.dma(snorm_b[:], dram_bcast(I['snorm'], 128, 512))
                kk.act(Aneg_b[:], Aneg_b[:], AF.Exp)
                kk.ts(Aneg_b[:], Aneg_b[:], -1.0, None, ALU.mult)
                S.barrier(); S.flush()
            hT = sb(ph, 'hT', [128, 512])
            hT_bf = sb(ph, 'hT_bf', [128, 512], BF16)
            hio = sb(ph, 'hio', [128, 4, 128])
            PB = []
            for par in range(2):
                d = {}
                d['proj'] = sb(ph, 'projB%d' % par, [128, 520])
                d['cbuf'] = sb(ph, 'cbuf%d' % par, [128, 8, 131])
                d['c3'] = sb(ph, 'c3%d' % par, [128, 8, 3])
                d['ca'] = sb(ph, 'ca%d' % par, [128, 8, 128])
                d['cb2'] = sb(ph, 'cb2%d' % par, [128, 8, 128])
                d['xbcT'] = sb(ph, 'xbcT%d' % par, [128, 8, 128], BF16)
                d['xsB'] = sb(ph, 'xsB%d' % par, [128, 768], BF16)
                for nm in ('dt_t', 'a_t', 'e_t', 'd_t', 'gdec'):
                    d[nm] = sb(ph, nm + str(par), [128, 8])
                d['xdt'] = sb(ph, 'xdt%d' % par, [128, 512], BF16)
                d['xdtd'] = sb(ph, 'xdtd%d' % par, [128, 512], BF16)
                d['GTm'] = sb(ph, 'GTm%d' % par, [128, 2, 128])
                d['aL'] = sb(ph, 'aL%d' % par, [128, 8, 128])
                d['LT'] = sb(ph, 'LT%d' % par, [128, 8, 128], BF16)
                d['WT'] = sb(ph, 'WT%d' % par, [128, 8, 128], BF16)
                d['yt'] = sb(ph, 'yt%d' % par, [128, 512])
                if par == 0:
                    d['xn'], d['xnT'], d['junk'] = W['xn'], W['xnT'], W['junk']
                else:
                    d['xn'] = sb(ph, 'xnp%d' % par, [128, D], BF16)
                    d['xnT'] = sb(ph, 'xnTp%d' % par, [128, 8, 128], BF16)
                    d['junk'] = sb(ph, 'junkp%d' % par, [128, D], BF16)
                d['ssq'] = sb(ph, 'ssqp%d' % par, [128, 1])
                d['rstd'] = sb(ph, 'rstdp%d' % par, [128, 1])
                d['gain'] = W['gain']
                d['B'] = W['B'][4 * par:4 * par + 4] + W['B'][4 * par:4 * par + 4]
                PB.append(d)
            BALL = W['B']

            def state_out(dst, pd):
                for c in range(4):
                    kk.tr(pd[:, c * 128:(c + 1) * 128], hT[:, c * 128:(c + 1) * 128], ident_f)
                kk.cp(hio[:].rearrange('p c n -> p (c n)'), pd[:, :])
                kk.dma(dst.rearrange('(c p) n -> p c n', p=128), hio[:], q='pool')

            def body(i):
                (ti, n, pos0, row0, sb_) = tiles[i]
                P = PB[i % 2]
                pa, pb, pc, pd = BALL[4 * (i % 2)], BALL[4 * (i % 2) + 1], BALL[4 * (i % 2) + 2], BALL[4 * (i % 2) + 3]
                proj, cbuf, c3, ca, cb2, xbcT, xsB = P['proj'], P['cbuf'], P['c3'], P['ca'], P['cb2'], P['xbcT'], P['xsB']
                dt_t, a_t, e_t, d_t, gdec = P['dt_t'], P['a_t'], P['e_t'], P['d_t'], P['gdec']
                xdt, xdtd, GTm, aL, LT, WT, yt = P['xdt'], P['xdtd'], P['GTm'], P['aL'], P['LT'], P['WT'], P['yt']
                y2 = cb2[:, 0:4, :].rearrange('p a b -> p (a b)')
                zs = ca[:, 0:4, :].rearrange('p a b -> p (a b)')
                xt = W['xt'][i % 2]
                src = I['xp'][row0:row0 + n, :] if sb_ is None else I['xs'][sb_:sb_ + 1, :]
                kk.dma(xt[:n], src)
                if ti == 0:
                    kk.memset(cbuf[:, :, 0:3], 0.0)
                    kk.memset(hT[:], 0.0)
                    kk.memset(hT_bf[:], 0.0)
                if sb_ is not None:
                    for j_ in range(3):
                        kk.dma(cbuf[:, :, j_], I['sconv'][sb_, j_].rearrange('(c p) -> p c', p=128), slow=True)
                rstd_of(P['junk'], xt, n, D, P['ssq'], P['rstd'])
                yield
                kk.stt(P['xn'][:n], xt[:n], P['rstd'][:n, 0:1], P['gain'][:n], ALU.mult, ALU.mult)
                yield
                psT = bfv(pa, 8, 128)
                for kc in range(8):
                    kk.tr(psT[:, kc, :n], P['xn'][:n, kc * 128:(kc + 1) * 128], ident_bf[:n, :n])
                yield
                kk.cp(P['xnT'][:, :, :n], psT[:, :, :n], eng='act')
                yield
                xnT = P['xnT']
                for fc in range(8):
                    pbk = pc if fc < 4 else pd
                    for kc in range(8):
                        kk.mm(pbk[:, (fc % 4) * 128:(fc % 4) * 128 + n], WB[:, kc, 520 + fc * 128:520 + (fc + 1) * 128], xnT[:, kc, :n], start=(kc == 0), stop=(kc == 7))
                for kc in range(8):
                    kk.mm(pb[:n, 0:8], xnT[:, kc, :n], WB[:, kc, 512:520], start=(kc == 0), stop=(kc == 7))
                for kc in range(8):
                    kk.mm(pa[:n, 0:512], xnT[:, kc, :n], WB[:, kc, 0:512], start=(kc == 0), stop=(kc == 7))
                yield
                kk.cp(cbuf[:, 0:4, 3:3 + n], pc[:, :].rearrange('p (c t) -> p c t', c=4)[:, :, :n], eng='dve')
                kk.cp(cbuf[:, 4:8, 3:3 + n], pd[:, :].rearrange('p (c t) -> p c t', c=4)[:, :, :n], eng='act')
                kk.tt(dt_t[:n], pb[:n, 0:8], dtb_b[:n], ALU.add)
                kk.cp(proj[:n, 0:512], pa[:n, 0:512], eng='act')
                yield
                kk.act(dt_t[:n], dt_t[:n], AF.Exp)
                kk.act(dt_t[:n], dt_t[:n], AF.Ln, bias=1.0)
                kk.tt(cb2[:, :, :n], cbuf[:, :, 1:1 + n], bc(cw_f[:, :, 1:2], [128, 8, n]), ALU.mult, eng='pool')
                kk.tt(ca[:, :, :n], cbuf[:, :, 0:n], bc(cw_f[:, :, 0:1], [128, 8, n]), ALU.mult)
                yield
                if sb_ is not None or ti == NT - 1:
                    kk.cp(c3[:], cbuf[:, :, n:n + 3], eng='pool')
                    dst = O['sconv_p'] if sb_ is None else O['sconv_s'][sb_]
                    for j_ in range(3):
                        kk.dma(dst[j_].rearrange('(c p) -> p c', p=128), c3[:, :, j_], q='pool', slow=True)
                if sb_ is None and ti < NT - 1:
                    kk.cp(PB[(i + 1) % 2]['cbuf'][:, :, 0:3], cbuf[:, :, n:n + 3], eng='pool')
                kk.tt(a_t[:n], dt_t[:n], Aneg_b[:n], ALU.mult)
                kk.tt(ca[:, :, :n], ca[:, :, :n], cb2[:, :, :n], ALU.add)
                kk.tt(cb2[:, :, :n], cbuf[:, :, 2:2 + n], bc(cw_f[:, :, 2:3], [128, 8, n]), ALU.mult, eng='pool')
                yield
                kk.mm(pb[:n, 8:16], Utri[:n, :n], a_t[:n, :])
                kk.mm(pb[:, 16:24], ones_f[:n, :], a_t[:n, :])
                kk.tt(aL[:n, :, :n], bc(Lstrict[:n, :n].rearrange('p (o l) -> p o l', o=1), [n, 8, n]), bc(a_t[:n].rearrange('p (h o) -> p h o', o=1), [n, 8, n]), ALU.mult)
                kk.tt(ca[:, :, :n], ca[:, :, :n], cb2[:, :, :n], ALU.add)
                kk.tt(cb2[:, :, :n], cbuf[:, :, 3:3 + n], bc(cw_f[:, :, 3:4], [128, 8, n]), ALU.mult, eng='pool')
                yield
                kk.act(e_t[:n], pb[:n, 8:16], AF.Exp)
                kk.act(gdec[:], pb[:, 16:24], AF.Exp)
                kk.cp(d_t[:n], pb[:n, 8:16])
                kk.tt(d_t[:n], pb[:n, 16:24], d_t[:n], ALU.subtract)
                for h in range(8):
                    pbk = pc if h < 4 else pd
                    kk.mm(pbk[:n, (h % 4) * 128:(h % 4) * 128 + n], aL[:n, h, :n], Utri[:n, :n])
                kk.tt(ca[:, :, :n], ca[:, :, :n], cb2[:, :, :n], ALU.add)
                yield
                kk.act(d_t[:n], d_t[:n], AF.Exp)
                kk.act(LT[:n, 0:4, :n], pc[:n, :].rearrange('p (c t) -> p c t', c=4)[:, :, :n], AF.Exp)
                kk.act(LT[:n, 4:8, :n], pd[:n, :].rearrange('p (c t) -> p c t', c=4)[:, :, :n], AF.Exp)
                for fc in range(8):
                    kk.act(xbcT[:, fc, :n], ca[:, fc, :n], AF.Silu, bias=cbias_f[:, fc:fc + 1])
                kk.act(zs[:n], proj[:n, 0:512], AF.Silu)
                yield
                psT2 = bfv(pa, 8, 128)
                for fc in range(6):
                    kk.tr(psT2[:n, fc, :], xbcT[:, fc, :n], ident_bf[:, :])
                for g in range(2):
                    kk.mm(pb[:n, 128 + g * 128:128 + g * 128 + n], xbcT[:, 4 + g, :n], xbcT[:, 6 + g, :n])
                yield
                kk.cp(xsB[:n, :], psT2[:n, 0:6, :].rearrange('p c f -> p (c f)'), eng='act')
                kk.tt(GTm[:n, :, :n], pb[:n, 128:384].rearrange('p (g l) -> p g l', g=2)[:, :, :n], bc(Utri[:n, :n].rearrange('p (o l) -> p o l', o=1), [n, 2, n]), ALU.mult)
                for g in range(2):
                    kk.tt(WT[:n, g * 4:(g + 1) * 4, :n], LT[:n, g * 4:(g + 1) * 4, :n], bc(GTm[:n, g:g + 1, :n], [n, 4, n]), ALU.mult)
                yield
                xs_v = xsB[:n, 0:512].rearrange('p (h d) -> p h d', h=8)
                kk.tt(xdt[:n].rearrange('p (h d) -> p h d', h=8), xs_v, bc(dt_t[:n].rearrange('p (h o) -> p h o', o=1), [n, 8, 64]), ALU.mult)
                kk.tt(xdtd[:n].rearrange('p (h d) -> p h d', h=8), xdt[:n].rearrange('p (h d) -> p h d', h=8), bc(d_t[:n].rearrange('p (h o) -> p h o', o=1), [n, 8, 64]), ALU.mult)
                kk.tt(y2[:n].rearrange('p (h d) -> p h d', h=8), xs_v, bc(D_b[:n].rearrange('p (h o) -> p h o', o=1), [n, 8, 64]), ALU.mult, eng='pool')
                yield
                if sb_ is not None:
                    kk.dma(hio[:], I['sssm'][sb_].rearrange('(c p) n -> p c n', p=128))
                    for c in range(4):
                        kk.tr(pd[:, c * 128:(c + 1) * 128], hio[:, c, :], ident_f)
                    kk.cp(hT[:], pd[:, :])
                    kk.cp(hT_bf[:], pd[:, :], eng='act')
                    yield
                for h in range(8):
                    kk.mm(pa[:n, h * 64:(h + 1) * 64], WT[:n, h, :n], xdt[:n, h * 64:(h + 1) * 64])
                for g in range(2):
                    kk.mm(pc[:n, g * 256:(g + 1) * 256], xbcT[:, 6 + g, :n], hT_bf[:, g * 256:(g + 1) * 256])
                for g in range(2):
                    kk.mm(pd[:, g * 256:(g + 1) * 256], xsB[:n, 512 + g * 128:512 + (g + 1) * 128], xdtd[:n, g * 256:(g + 1) * 256])
                yield
                kk.tt(yt[:n].rearrange('p (h d) -> p h d', h=8), pc[:n, :].rearrange('p (h d) -> p h d', h=8), bc(e_t[:n].rearrange('p (h o) -> p h o', o=1), [n, 8, 64]), ALU.mult)
                kk.tt(yt[:n], yt[:n], pa[:n, :], ALU.add)
                kk.tt(hT[:].rearrange('p (h d) -> p h d', h=8), hT[:].rearrange('p (h d) -> p h d', h=8), bc(gdec[:].rearrange('p (h o) -> p h o', o=1), [128, 8, 64]), ALU.mult)
                kk.tt(hT[:], hT[:], pd[:, :], ALU.add)
                kk.tt(yt[:n], yt[:n], y2[:n], ALU.add)
                kk.tt(yt[:n], yt[:n], zs[:n], ALU.mult)
                yield
                kk.cp(hT_bf[:], hT[:], eng='act')
                rstd_of(P['junk'], yt, n, 512, P['ssq'], P['rstd'])
                if sb_ is not None:
                    state_out(O['ssm_s'][sb_], pd)
                elif ti == NT - 1:
                    state_out(O['ssm_p'], pd)
                yield
                ycd = YC[:n, ti, :] if sb_ is None else YCs[0:1, sb_, :]
                kk.stt(ycd, yt[:n], P['rstd'][:n, 0:1], snorm_b[:n], ALU.mult, ALU.mult)
                yield

            HALF = 9
            cur = body(0)
            for _ in range(HALF):
                next(cur)
            for i in range(1, len(tiles)):
                nxt = body(i)
                rr(cur, nxt) if False else None
                done = False
                while not done:
                    try:
                        next(cur)
                    except StopIteration:
                        done = True
                    if not done:
                        try:
                            next(nxt)
                        except StopIteration:
                            pass
                cur = nxt
            for _ in cur:
                pass
            S.barrier(); S.flush()

        with ExitStack() as ph:
            W = common_work(ph)
            B = W['B']
            kcT = sb(ph, 'kcT', [64, 2, 128], BF16)
            vcO = sb(ph, 'vcO', [128, 2, 96], BF16)
            Wout = sb(ph, 'Wout', [128, 8, D], BF16)
            with ExitStack() as ld:
                stg = [sb(ld, 'stgC0', [128, 8, 512]), sb(ld, 'stgC1', [128, 8, 512])]
                W1 = [sb(ld, 'W1k', [64, 32, 256], BF16), sb(ld, 'W1v', [64, 32, 256], BF16)]
                W2 = [sb(ld, 'W2k', [128, 2, 64], BF16), sb(ld, 'W2v', [128, 2, 64], BF16)]
                peT = sb(ld, 'peT', [64, 2, 32])
                peTb = sb(ld, 'peTb', [64, 2, 32], BF16)
                biasv = sb(ld, 'biasv', [128, 2, 2])
                Hs = sb(ld, 'Hs', [128, 2, 2, 2, 128], BF16)
                ovl_t = sb(ld, 'ovl_t', [128, 32])
                for kv, (w1n, w2n, pen) in enumerate([('w1k', 'w2k', 'pek'), ('w1v', 'w2v', 'pev')]):
                    for hh in range(2):
                        sv = stg[hh].rearrange('p a (b c) -> p (a b) c', b=2)
                        kk.dma(sv[0:64, :, :], I[w1n][hh * 1024:(hh + 1) * 1024, :].rearrange('(hs d) f -> d hs f', d=64))
                        kk.cpr(W1[kv][:, hh * 16:(hh + 1) * 16, :], sv[0:64, :, :])
                    kk.dma(stg[0][:, 0:2, 0:64], I[w2n].rearrange('(k p) w -> p k w', p=128))
                    kk.cpr(W2[kv][:], stg[0][:, 0:2, 0:64])
                    for hs in range(32):
                        kk.dma(peT[:, kv, hs:hs + 1], I[pen][hs].rearrange('(d o) -> d o', o=1), q='pool', slow=True)
                kk.cp(peTb[:], peT[:])
                kk.dma(ovl_t[:], I['ovl'])
                for c0 in range(0, D, 512):
                    load_w(Wout[:, :, c0:c0 + 512], I['wout'][:, c0:c0 + 512], stg[(c0 // 512) % 2], 8, 512)
                kk.memset(Hs[:], 0.0)
                for kv in range(2):
                    HT = KA if kv == 0 else KB
                    for fc in range(2):
                        for hs in range(32):
                            kk.mm(B[0][:, 0:1], W1[kv][:, hs, fc * 128:(fc + 1) * 128], peTb[:, kv, hs:hs + 1], start=(hs == 0), stop=(hs == 31))
                        kk.cp(biasv[:, kv, fc:fc + 1], B[0][:, 0:1])
                        for g in range(2):
                            pb = B[1 + (g % 2)]
                            for hs in range(32):
                                kk.mm(pb[:, 0:127], W1[kv][:, hs, fc * 128:(fc + 1) * 128], HT[0:64, g, hs:hs + 16 * 126 + 1:16], start=(hs == 0), stop=(hs == 31))
                            kk.act(Hs[:, kv, g, fc, 0:127], pb[:, 0:127], AF.Silu, bias=biasv[:, kv, fc:fc + 1])
                kk.memset(vcO[:], 0.0)
                kk.memset(kcT[:], 0.0)
                for g in range(2):
                    for fc in range(2):
                        kk.mm(B[3][0:64, 0:127], W2[0][:, fc, :], Hs[:, 0, g, fc, 0:127], start=(fc == 0), stop=(fc == 1))
                    kk.cp(kcT[:, g, 0:127], B[3][0:64, 0:127])
                    for fc in range(2):
                        kk.mm(B[4][0:127, 0:64], Hs[:, 1, g, fc, 0:127], W2[1][:, fc, :], start=(fc == 0), stop=(fc == 1))
                    kk.cp(vcO[0:127, g, 0:64], B[4][0:127, 0:64])
                    kk.cp(vcO[0:127, g, 64:96], ovl_t[0:127, :])
                S.barrier(); S.flush()

            Mc = sb(ph, 'Mc', [128, 128])
            Vm = sb(ph, 'Vm', [128, 32])
            Bm = sb(ph, 'Bm', [128, 32])
            nB = sb(ph, 'nB', [128, 32])
            Vm1 = sb(ph, 'Vm1', [128, 32])
            impacc = sb(ph, 'impacc', [128, 2, 32])
            imp4 = sb(ph, 'imp4', [128, 4, 32])
            imp2 = sb(ph, 'imp2', [128, 32])
            imp3 = sb(ph, 'imp3', [128, 32])
            m8 = sb(ph, 'm8', [128, 8])
            sel = sb(ph, 'sel', [128, 32])
            negsel = sb(ph, 'negsel', [128, 2, 32])
            Mfull = sb(ph, 'Mfull', [128, 2, T], BF16)
            Mwin = sb(ph, 'Mwin', [128, 640], BF16)
            mxa = [sb(ph, 'mxa0', [128, 4]), sb(ph, 'mxa1', [128, 4])]
            scs = [sb(ph, 'sc0', [128, T]), sb(ph, 'sc1', [128, T])]
            pbfs = [sb(ph, 'pbf0', [128, T], BF16), sb(ph, 'pbf1', [128, T], BF16)]
            pTs = [sb(ph, 'pT0', [128, 16, 128], BF16), sb(ph, 'pT1', [128, 16, 128], BF16)]
            mxs = [sb(ph, 'mx0', [128, 1]), sb(ph, 'mx1', [128, 1])]
            negms = [sb(ph, 'negm0', [128, 1]), sb(ph, 'negm1', [128, 1])]
            rss = [sb(ph, 'rs0', [128, 1]), sb(ph, 'rs1', [128, 1]), sb(ph, 'rs2', [128, 1])]
            coef = sb(ph, 'coef', [128, 1])
            coef4 = sb(ph, 'coef4', [128, 4])
            accs = [sb(ph, 'acc0', [128, 8, 64]), sb(ph, 'acc1', [128, 8, 64])]
            catb = sb(ph, 'catb', [128, D], BF16)
            catT = sb(ph, 'catT', [128, 8, 128], BF16)
            xo = sb(ph, 'xo', [128, D])
            csc = sb(ph, 'csc', [128, 4, 128])
            cpbf = sb(ph, 'cpbf', [128, 4, 128], BF16)
            cpT = sb(ph, 'cpT', [128, 4, 128], BF16)
            cmx = sb(ph, 'cmx', [128, 4])
            cnegm = sb(ph, 'cnegm', [128, 4])
            crs = sb(ph, 'crs', [128, 4])
            qk_i = [0]
            tr_i = [0]
            causal_bf = sb(ph, 'causal_bf', [128, 128], BF16)
            kk.cp(causal_bf[:], causal, eng='act')
            kk.memset(Mwin[:], 0.0)
            kk.cp(Mwin[:, 0:128], wlo, eng='pool')
            kk.cp(Mwin[:, 512:640], causal, eng='pool')
            kk.memset(negsel[:], 0.0)
            kk.memset(pbfs[0][:], 0.0)
            kk.memset(pbfs[1][:], 0.0)
            n = 128

            def pre_stages(t, g):
                pos0 = t * 128
                tok = slice(pos0, pos0 + n)
                acc = accs[t % 2]
                nk = (t + 1) * 128

                def p0():
                    if g == 0:
                        kk.dma(W['xt'][t % 2][:n], I['xp'][pos0:pos0 + n, :])
                        kk.memset(Mc[:], NEG)
                        kk.memset(Mc[:, 0:127], 0.0)
                        S.add('pool', lambda e: e.affine_select(out=Mc[:, 0:127], in_=Mc[:, 0:127], pattern=[[-16, 127]], compare_op=ALU.is_ge, fill=NEG, base=pos0 - 31, channel_multiplier=1), reads=[Mc[:, 0:127]], writes=[Mc[:, 0:127]])
                        if t >= 8:
                            kk.memset(Vm[:], 1.0)
                            S.add('pool', lambda e: e.affine_select(out=Vm[:], in_=Vm[:], pattern=[[-64, 32]], compare_op=ALU.is_ge, fill=0.0, base=pos0, channel_multiplier=1), reads=[Vm[:]], writes=[Vm[:]])
                            S.add('pool', lambda e: e.affine_select(out=Bm[:], in_=Vm[:], pattern=[[64, 32]], compare_op=ALU.is_ge, fill=0.0, base=127 - pos0, channel_multiplier=-1), reads=[Vm[:]], writes=[Bm[:]])
                            kk.memset(Bm[:, 0:1], 1.0)
                            kk.ts(nB[:], Bm[:], -1.0, 1.0, ALU.mult, ALU.add, eng='pool')
                            kk.ts(Vm1[:], Vm[:], -1.0, 1e9, ALU.add, ALU.mult, eng='pool')
                    for r in range(4):
                        kk.mm(B[0][:n, r * 128:(r + 1) * 128], QT[0:64, g * 4 + r, tok], kcT[:, g, 0:128])

                def p1():
                    kk.stt(csc[:n], B[0][:n, :].rearrange('p (h k) -> p h k', h=4), 0.125, bc(Mc[:n, :].rearrange('p (o k) -> p o k', o=1), [n, 4, 128]), ALU.mult, ALU.add)
                    S.add('dve', lambda e: e.tensor_reduce(out=cmx[:n], in_=csc[:n], axis=AX.X, op=ALU.max), reads=[csc[:n]], writes=[cmx[:n]])
                    kk.ts(cnegm[:n], cmx[:n], -30000.0, -1.0, ALU.max, ALU.mult)

                def p2():
                    for r in range(4):
                        kk.act(cpbf[:n, r, :], csc[:n, r, :], AF.Exp, bias=cnegm[:n, r:r + 1], accum=crs[:n, r:r + 1])

                def p3():
                    pst = bfv(B[0], 8, 128)
                    for r in range(4):
                        kk.tr(pst[:, r, :n], cpbf[:n, r, :], ident_bf[:n, :n])
                    kk.cp(cpT[:, :, :n], pst[:, 0:4, :n], eng='act')

                def p4():
                    for r in range(4):
                        kk.mm(B[0][:n, r * 96:(r + 1) * 96], cpT[0:127, r, :n], vcO[0:127, g, :])
                    kk.ts(crs[:n], crs[:n], 1e-30, None, ALU.max)
                    kk.recip(crs[:n], crs[:n])
                    gv = G[:n, t, :].rearrange('p (h r) -> p h r', r=3)
                    kk.tt(coef4[:n].rearrange('p (h o) -> p h o', o=1), crs[:n].rearrange('p (h o) -> p h o', o=1), gv[:, g * 4:(g + 1) * 4, 0:1], ALU.mult)
                    ov = B[0][:n, 0:384].rearrange('p (h c) -> p h c', h=4)
                    kk.tt(acc[:n, g * 4:(g + 1) * 4, :], ov[:, :, 0:64], bc(coef4[:n].rearrange('p (h o) -> p h o', o=1), [n, 4, 64]), ALU.mult)
                    kk.tt(imp4[:n], ov[:, :, 64:96], bc(crs[:n].rearrange('p (h o) -> p h o', o=1), [n, 4, 32]), ALU.mult)
                    S.add('dve', lambda e: e.tensor_reduce(out=impacc[:n, g, :], in_=imp4[:n].rearrange('p h j -> p j h'), axis=AX.X, op=ALU.add), reads=[imp4[:n]], writes=[impacc[:n, g, :]])

                def p5():
                    if t >= 8:
                        kk.tt(imp2[:n], impacc[:n, g, :], nB[:n], ALU.mult)
                        kk.stt(imp2[:n], Bm[:n], 1e9, imp2[:n], ALU.mult, ALU.add)
                        kk.tt(imp3[:n], imp2[:n], Vm[:n], ALU.mult)
                        kk.tt(imp3[:n], imp3[:n], Vm1[:n], ALU.add)
                        S.add('dve', lambda e: e.max(out=m8[:], in_=imp3[:]), reads=[imp3[:]], writes=[m8[:]])
                        S.add('dve', lambda e: e.match_replace(out=imp2[:], in_to_replace=m8[:], in_values=imp3[:], imm_value=-3e9), reads=[m8[:], imp3[:]], writes=[imp2[:]])
                        S.add('dve', lambda e: e.max(out=m8[:], in_=imp2[:]), reads=[imp2[:]], writes=[m8[:]])
                        kk.ts(sel[:n], imp3[:n], m8[:n, 7:8], None, ALU.is_ge)
                        kk.ts(negsel[:n, g, :], sel[:n], -1.0, 1e30, ALU.add, ALU.mult)

                def p6():
                    if t >= 8:
                        nblk = 2 * (t + 1)
                        S.add('act', lambda e: e.activation(out=Mfull[:n, g, 0:nk].rearrange('p (b k) -> p b k', k=64), in_=bc(negsel[:n, g, 0:nblk].rearrange('p (b o) -> p b o', o=1), [n, nblk, 64]), func=AF.Copy),
                              reads=[negsel[:n, g, 0:nblk]], writes=[Mfull[:n, g, 0:nk]])

                return [p0, p1, p2, p3, p4, p5, p6]

            def make_task(t, g, hd, br, k):
                pos0 = t * 128
                tok = slice(pos0, pos0 + n)
                acc = accs[t % 2]
                pp = k % 2
                rs = rss[k % 3]
                sc, pbf, pT, mx, negm = scs[pp], pbfs[pp], pTs[pp], mxs[pp], negms[pp]
                if br == 1:
                    jl = list(range(t + 1))
                    KT, kbase, Msk, mbase, vsel = KA, 0, Mfull[:, g, :], 0, 0
                else:
                    j0w = max(0, t - 4)
                    jl = list(range(j0w, t + 1))
                    KT, kbase, Msk, vsel = KB, j0w * 128, Mwin, 1
                    mbase = 640 - len(jl) * 128
                nkk = len(jl) * 128

                def A1():
                    ma = mxa[pp]
                    nbk = 0
                    for kb in range(0, nkk, 512):
                        w = min(512, nkk - kb)
                        qk_i[0] += 1
                        pb = B[1 + qk_i[0] % 4]
                        extra = []
                        if br == 1:
                            dg = t * 128 - kb
                            if t >= 8:
                                extra.append((pb[:n, 0:w], Msk[:n, mbase + kb:mbase + kb + w]))
                            if 0 <= dg < w:
                                extra.append((pb[:n, dg:dg + 128], causal_bf[:n, :]))
                        else:
                            extra.append((pb[:n, 0:w], Msk[:n, mbase + kb:mbase + kb + w]))
                        kk.mm(pb[:n, 0:w], QT[64:128, hd, tok], KT[64:128, g, kbase + kb:kbase + kb + w], start=True, stop=(len(extra) == 0))
                        for ei, (eo, er) in enumerate(extra):
                            kk.mm(eo, ident_bf[:n, :n], er, start=False, stop=(ei == len(extra) - 1))
                        init = -1e30 if nbk == 0 else ma[:n, nbk - 1:nbk]
                        rd = [pb[:n, 0:w]] + ([] if nbk == 0 else [ma[:n, nbk - 1:nbk]])
                        S.add('dve', lambda e, kb=kb, w=w, pb=pb, init=init, nbk=nbk: e.tensor_scalar(out=sc[:n, kb:kb + w], in0=pb[:n, 0:w], scalar1=0.125, scalar2=init, op0=ALU.mult, op1=ALU.max, accum_out=ma[:n, nbk:nbk + 1]),
                              reads=rd, writes=[sc[:n, kb:kb + w], ma[:n, nbk:nbk + 1]])
                        nbk += 1
                    kk.ts(negm[:n], ma[:n, nbk - 1:nbk], -1.0, None, ALU.mult)

                def A2():
                    kk.act(pbf[:n, 0:nkk], sc[:n, 0:nkk], AF.Exp, bias=negm[:n, 0:1], accum=rs[:n])

                def B1():
                    nb = len(jl)
                    for i0 in range(0, nb, 8):
                        tr_i[0] += 1
                        pst = bfv(B[5 + tr_i[0] % 2], 8, 128)
                        cnt = min(8, nb - i0)
                        for i in range(cnt):
                            kk.tr(pst[:, i, :n], pbf[:n, (i0 + i) * 128:(i0 + i + 1) * 128], ident_bf[:n, :n])
                        kk.cp(pT[:, i0:i0 + cnt, :n], pst[:, 0:cnt, :n], eng='act')

                def B2():
                    nb = len(jl)
                    for i, j in enumerate(jl):
                        kk.mm(B[7][:n, 0:64], pT[:, i, :n], VS[:, j, vsel, g * 64:(g + 1) * 64], start=(i == 0), stop=(i == nb - 1))
                    kk.ts(rs[:n], rs[:n], 1e-30, None, ALU.max)
                    kk.recip(rs[:n], rs[:n])
                    kk.tt(coef[:n], rs[:n], G[:n, t, hd * 3 + br:hd * 3 + br + 1], ALU.mult)
                    kk.stt(acc[:n, hd, :], B[7][:n, 0:64], coef[:n, 0:1], acc[:n, hd, :], ALU.mult, ALU.add)

                return (A1, A2, B1, B2)

            def post_stages(t):
                pos0 = t * 128
                acc = accs[t % 2]
                xt = W['xt'][t % 2]

                def q0():
                    kk.cp(catb[:n, 0:512], acc[:n].rearrange('p h d -> p (h d)'), eng='act')
                    kk.cp(catb[:n, 512:1024], YC[:n, t, :], eng='pool')

                def q1():
                    tr_i[0] += 1
                    pst = bfv(B[5 + tr_i[0] % 2], 8, 128)
                    for kc in range(8):
                        kk.tr(pst[:, kc, :n], catb[:n, kc * 128:(kc + 1) * 128], ident_bf[:n, :n])
                    kk.cp(catT[:, :, :n], pst[:, :, :n], eng='act')

                def q2():
                    for cb in range(2):
                        qk_i[0] += 1
                        pb = B[1 + qk_i[0] % 4]
                        for kc in range(8):
                            kk.mm(pb[:n, :], catT[:, kc, :n], Wout[:, kc, cb * 512:(cb + 1) * 512], start=(kc == 0), stop=(kc == 7))
                        kk.tt(xo[:n, cb * 512:(cb + 1) * 512], pb[:n, :], xt[:n, cb * 512:(cb + 1) * 512], ALU.add)
                    kk.dma(xres[pos0:pos0 + n, :], xo[:n, :], q='pool')

                return [q0, q1, q2]

            groups = [(t, g) for t in range(NT) for g in range(2)]
            NG = len(groups)
            nsteps = 8 * NG + 8
            steps = [[] for _ in range(nsteps)]
            for f_ in pre_stages(0, 0):
                f_()
            alltasks = []
            for gi, (t, g) in enumerate(groups):
                for r in range(4):
                    for br in (1, 2):
                        alltasks.append(make_task(t, g, g * 4 + r, br, len(alltasks)))
            NTK = len(alltasks)
            for k in range(NTK + 1):
                st = steps[k]
                if k < NTK:
                    st.append(alltasks[k][0])
                if k >= 1:
                    st.append(alltasks[k - 1][2])
                if k < NTK:
                    st.append(alltasks[k][1])
                if k >= 1:
                    st.append(alltasks[k - 1][3])
            for gi, (t, g) in enumerate(groups):
                base = 8 * gi
                if gi + 1 < NG:
                    for j_, f_ in enumerate(pre_stages(*groups[gi + 1])):
                        steps[base + j_].append(f_)
                if g == 1:
                    for j_, f_ in enumerate(post_stages(t)):
                        steps[base + 9 + j_].append(f_)
            for st in steps:
                for f_ in st:
                    f_()
            S.barrier(); S.flush()
        l0.close()

        with ExitStack() as ph:
            _uid[0] += 1
            B = [ph.enter_context(nc.psum_tensor('B%d_%d' % (_uid[0], i), [128, 512], F32)) for i in range(8)]
            Wout = sb(ph, 'WoutS', [128, 8, D], BF16)
            ocmp = sb(ph, 'ocmp', [1, NS, 512])
            osw = sb(ph, 'osw', [1, 2, 512])
            xt1 = sb(ph, 'xt1', [1, D])
            negsel_s = sb(ph, 'negsel_s', [4, NS, 2, 136])
            qSr = sb(ph, 'qSr', [64, 8, NS], BF16)
            kn_s = sb(ph, 'kn_s', [64, 2, NS], BF16)
            kn_w = sb(ph, 'kn_w', [64, 2, NS], BF16)
            sc_s = sb(ph, 'sc_s', [4, 8320])
            p_s = sb(ph, 'p_s', [4, 8320], BF16)
            pT_s = sb(ph, 'pT_s', [128, 65, 4], BF16)
            mx = sb(ph, 'mxs', [4, 1])
            negm = sb(ph, 'negms', [4, 1])
            rs = sb(ph, 'rss', [4, 1])
            ocn = sb(ph, 'ocn', [4, 64])
            with ExitStack() as ld:
                stg = [sb(ld, 'stgS0', [128, 8, 512]), sb(ld, 'stgS1', [128, 8, 512])]
                for c0 in range(0, D, 512):
                    load_w(Wout[:, :, c0:c0 + 512], I['wout'][:, c0:c0 + 512], stg[(c0 // 512) % 2], 8, 512)
                S.barrier(); S.flush()
            kk.dma(qSr[:], QTs[64:128, :, :])
            kk.dma(kn_s[:], KAs[64:128, :, :])
            kk.dma(kn_w[:], KBs[64:128, :, :])
            kk.memset(p_s[:], 0.0)

            def softmax4(nk):
                kk.rmax(mx[:], sc_s[:, 0:nk])
                kk.ts(negm[:], mx[:], -30000.0, -1.0, ALU.max, ALU.mult)
                kk.act(p_s[:, 0:nk], sc_s[:, 0:nk], AF.Exp, bias=negm[:, 0:1], accum=rs[:])
                kk.ts(rs[:], rs[:], 1e-30, None, ALU.max)
                kk.recip(rs[:], rs[:])

            def transposes4(nchunks):
                pst = bfv(B[4], 256, 4)
                for j in range(nchunks):
                    kk.tr(pst[:, j, :], p_s[:, j * 128:(j + 1) * 128], ident_bf[0:4, 0:4])
                kk.cpr(pT_s[:, 0:nchunks, :], pst[:, 0:nchunks, :])

            with ExitStack() as s1:
                W1s = [sb(s1, 'W1sk', [128, 16, 256], BF16), sb(s1, 'W1sv', [128, 16, 256], BF16)]
                W2s = [sb(s1, 'W2sk', [128, 2, 64], BF16), sb(s1, 'W2sv', [128, 2, 64], BF16)]
                pe128 = sb(s1, 'pe128', [128, 2, 16])
                pe128b = sb(s1, 'pe128b', [128, 2, 16], BF16)
                biasv = sb(s1, 'biasvs', [128, 2, 2])
                ovl_f = sb(s1, 'ovl_f', [128, 4, 129])
                vcO_s = sb(s1, 'vcO_s', [128, 4, 2, 200], BF16)
                with ExitStack() as ld:
                    stg = [sb(ld, 'stgT0', [128, 8, 512]), sb(ld, 'stgT1', [128, 8, 512])]
                    for kv, (w1n, w2n, pen) in enumerate([('w1k', 'w2k', 'pek'), ('w1v', 'w2v', 'pev')]):
                        sv = stg[kv].rearrange('p a (b c) -> p (a b) c', b=2)
                        kk.dma(sv, I[w1n].rearrange('(c p) f -> p c f', p=128))
                        kk.cpr(W1s[kv][:], sv)
                        kk.dma(stg[kv][:, 0:2, 0:64], I[w2n].rearrange('(k p) w -> p k w', p=128))
                        kk.cpr(W2s[kv][:], stg[kv][:, 0:2, 0:64])
                        for c_ in range(16):
                            kk.dma(pe128[:, kv, c_:c_ + 1], I[pen][2 * c_:2 * c_ + 2, :].rearrange('r (d o) -> (r d) o', o=1), q='pool', slow=True)
                    kk.cp(pe128b[:], pe128[:])
                    kk.dma(ovl_f[:], I['ovls'].rearrange('(c p) j -> p c j', p=128))
                    kk.memset(vcO_s[:], 0.0)
                    for g in range(2):
                        kk.cp(vcO_s[:, :, g, 64:193], ovl_f[:, :, :])
                    for kv in range(2):
                        for fc in range(2):
                            for c_ in range(16):
                                kk.mm(B[0][:, 0:1], W1s[kv][:, c_, fc * 128:(fc + 1) * 128], pe128b[:, kv, c_:c_ + 1], start=(c_ == 0), stop=(c_ == 15))
                            kk.cp(biasv[:, kv, fc:fc + 1], B[0][:, 0:1])
                    S.barrier(); S.flush()
                idx_i = sb(s1, 'idx_i', [128, NS * 4], I32)
                idx_f = sb(s1, 'idx_f', [128, NS * 4])
                idx_u = sb(s1, 'idx_u', [128, NS * 4], I32)
                Xg = [sb(s1, 'Xg0', [128, 4096]), sb(s1, 'Xg1', [128, 4096])]
                XT = sb(s1, 'XT', [128, 4, 8, 512], BF16)
                Xr = sb(s1, 'Xr', [128, 4, 1024], BF16)
                Hs = sb(s1, 'Hss', [128, 2, 2, 2, 512], BF16)
                kcT_s = sb(s1, 'kcT_s', [64, 2, 512], BF16)
                imph = sb(s1, 'imph', [4, 136])
                impg = sb(s1, 'impg', [4, 136])
                imp2 = sb(s1, 'imp2s', [4, 136])
                m8 = sb(s1, 'm8s', [4, 8])
                sel = sb(s1, 'sels', [4, 136])
                ccv = I['ccmp'].rearrange('(n r) c -> n (r c)', r=16)
                for b in range(NS):
                    for tt in range(4):
                        srcp = bass.AP(I['ptab'].tensor, b * 64 + tt * 16, [[1, 16], [0, 8], [1, 1]])
                        kk.dma(idx_i[:, b * 4 + tt:b * 4 + tt + 1], srcp)
                kk.cp(idx_f[:], idx_i[:])
                kk.ts(idx_f[:], idx_f[:], 8.0, cst2[:, 0:1], ALU.mult, ALU.add)
                kk.cp(idx_u[:], idx_f[:])
                def s1_gath(b):
                    for tt in range(4):
                        xg = Xg[tt % 2]
                        ic = b * 4 + tt
                        S.add('pool', lambda e, xg=xg, ic=ic: e.indirect_dma_start(out=xg[:, :], out_offset=None, in_=ccv, in_offset=bass.IndirectOffsetOnAxis(ap=idx_u[:, ic:ic + 1], axis=0)),
                              reads=[idx_u[:, ic:ic + 1], ccv], writes=[xg[:, :]], dma=True)
                        xv = xg[:, :].rearrange('p (s c d) -> p c s d', s=16, c=4, d=64)
                        for comb in range(4):
                            kk.cpr(Xr[:, comb, :].rearrange('p (s d) -> p s d', d=64), xv[:, comb, :, :])
                        for comb in range(4):
                            pst = bfv(B[1 + comb % 2], 8, 128)
                            for sp in range(8):
                                kk.tr(pst[:, sp, :], Xr[:, comb, sp * 128:(sp + 1) * 128], ident_bf[:, :])
                            kk.cpr(XT[:, comb, :, tt * 128:(tt + 1) * 128], pst[:, :, :])

                def s1_comp(b):
                    for kv in range(2):
                        for g in range(2):
                            comb = kv * 2 + g
                            for fc in range(2):
                                pb = B[3 + (fc % 2)]
                                for c_ in range(16):
                                    rhs = XT[:, comb, c_, 0:511] if c_ < 8 else XT[:, comb, c_ - 8, 1:512]
                                    kk.mm(pb[:, 0:511], W1s[kv][:, c_, fc * 128:(fc + 1) * 128], rhs, start=(c_ == 0), stop=(c_ == 15))
                                kk.act(Hs[:, kv, g, fc, 0:511], pb[:, 0:511], AF.Silu, bias=biasv[:, kv, fc:fc + 1])
                    for g in range(2):
                        for fc in range(2):
                            kk.mm(B[5][0:64, 0:511], W2s[0][:, fc, :], Hs[:, 0, g, fc, 0:511], start=(fc == 0), stop=(fc == 1))
                        kk.cp(kcT_s[:, g, 0:511], B[5][0:64, 0:511])
                        for ch in range(4):
                            m = 128 if ch < 3 else 127
                            for fc in range(2):
                                kk.mm(B[6][0:m, ch * 64:(ch + 1) * 64], Hs[:, 1, g, fc, ch * 128:ch * 128 + m], W2s[1][:, fc, :], start=(fc == 0), stop=(fc == 1))
                            kk.cp(vcO_s[0:m, ch, g, 0:64], B[6][0:m, ch * 64:(ch + 1) * 64])

                def s1_tail(b):
                    for g in range(2):
                        kk.mm(B[0][0:4, 0:511], QTs[0:64, g * 4:(g + 1) * 4, b], kcT_s[:, g, 0:511])
                        kk.ts(sc_s[:, 0:511], B[0][0:4, 0:511], 0.125, None, ALU.mult)
                        kk.memset(p_s[:, 511:512], 0.0)
                        softmax4(511)
                        transposes4(4)
                        for ch in range(4):
                            m = 128 if ch < 3 else 127
                            kk.mm(B[7][0:4, 0:193], pT_s[0:m, ch, :], vcO_s[0:m, ch, g, 0:193], start=(ch == 0), stop=(ch == 3))
                        kk.ts(ocn[:], B[7][0:4, 0:64], rs[:, 0:1], None, ALU.mult)
                        kk.dma(ocmp[0:1, b, g * 256:(g + 1) * 256].rearrange('p (h d) -> p h d', h=4), ocn[:])
                        kk.ts(imph[:, 0:129], B[7][0:4, 64:193], rs[:, 0:1], None, ALU.mult)
                        kk.mm(B[0][0:4, 0:129], ones_f[0:4, 0:4], imph[:, 0:129])
                        kk.cp(impg[:, 0:129], B[0][0:4, 0:129])
                        kk.memset(impg[:, 0:1], 1e9, eng='dve')
                        kk.memset(impg[:, 127:129], 1e9, eng='dve')
                        S.add('dve', lambda e: e.max(out=m8[:], in_=impg[:, 0:129]), reads=[impg[:, 0:129]], writes=[m8[:]])
                        S.add('dve', lambda e: e.match_replace(out=imp2[:, 0:129], in_to_replace=m8[:], in_values=impg[:, 0:129], imm_value=-3e9), reads=[m8[:], impg[:, 0:129]], writes=[imp2[:, 0:129]])
                        S.add('dve', lambda e: e.max(out=m8[:], in_=imp2[:, 0:129]), reads=[imp2[:, 0:129]], writes=[m8[:]])
                        kk.ts(sel[:, 0:129], impg[:, 0:129], m8[:, 7:8], None, ALU.is_ge)
                        kk.ts(negsel_s[:, b, g, 0:129], sel[:, 0:129], -1.0, 1e30, ALU.add, ALU.mult)

                s1_gath(0)
                s1_comp(0)
                for b in range(NS):
                    if b + 1 < NS:
                        s1_gath(b + 1)
                    s1_tail(b)
                    if b + 1 < NS:
                        s1_comp(b + 1)
                S.barrier(); S.flush()

            with ExitStack() as s2:
                idr_i = sb(s2, 'idr_i', [128, NS * 64], I32)
                idr_f = sb(s2, 'idr_f', [128, NS * 64])
                idr_u = sb(s2, 'idr_u', [128, NS * 64], I32)
                Kp = [sb(s2, 'Kp0', [128, 8, 256]), sb(s2, 'Kp1', [128, 8, 256])]
                KsTs = [sb(s2, 'KsT0', [128, 8320], BF16), sb(s2, 'KsT1', [128, 8320], BF16)]
                Vs_ss = [sb(s2, 'Vs_s0', [128, 64, 128], BF16), sb(s2, 'Vs_s1', [128, 64, 128], BF16)]
                Wp = sb(s2, 'Wp', [128, 4, 256])
                KwT = sb(s2, 'KwT', [64, 2, 640], BF16)
                Vw_s = sb(s2, 'Vw_s', [128, 4, 128], BF16)
                acc1 = sb(s2, 'acc1', [1, 512])
                tmp1 = sb(s2, 'tmp1', [1, 512])
                catb = sb(s2, 'catbs', [1, D], BF16)
                catT = sb(s2, 'catTs', [128, 8, 1], BF16)
                xo = xt1
                csv = I['cslc']
                kk.dma(idr_i[:], bass.AP(I['ptab'].tensor, 0, [[0, 128], [1, NS * 64]]))
                kk.cp(idr_f[:], idr_i[:])
                kk.ts(idr_f[:], idr_f[:], 128.0, cst2[:, 1:2], ALU.mult, ALU.add)
                kk.cp(idr_u[:], idr_f[:])
                def gather_gen(b):
                    KsT, Vs_s = KsTs[b % 2], Vs_ss[b % 2]
                    for j0 in range(0, 64, 8):
                        kp = Kp[(j0 // 8) % 2]
                        for jj in range(8):
                            j = b * 64 + j0 + jj
                            S.add('pool', lambda e, kp=kp, jj=jj, j=j: e.indirect_dma_start(out=kp[:, jj, :], out_offset=None, in_=csv, in_offset=bass.IndirectOffsetOnAxis(ap=idr_u[:, j:j + 1], axis=0)),
                                  reads=[idr_u[:, j:j + 1], csv], writes=[kp[:, jj, :]], dma=True)
                        for q_ in range(2):
                            pb = B[1 + q_]
                            for jj in range(4):
                                kk.tr(pb[:, jj * 128:(jj + 1) * 128], kp[:, q_ * 4 + jj, 0:128], ident_f)
                            kk.cpr(KsT[:, (j0 + q_ * 4) * 128:(j0 + q_ * 4 + 4) * 128], pb[:, :])
                        kk.cpr(Vs_s[:, j0:j0 + 8, :], kp[:, :, 128:256])
                        yield
                    kk.cp(KsT[0:64, 8192:8193], kn_s[:, 0, b:b + 1])
                    kk.cp(KsT[64:128, 8192:8193], KAs[64:128, 1, b:b + 1])
                    yield

                def attn_gen(b):
                    KsT, Vs_s = KsTs[b % 2], Vs_ss[b % 2]
                    ti = NT + b
                    row0 = T + b
                    for g in range(2):
                        lq = qSr[:, 0:4, b] if g == 0 else QTs[64:128, 4:8, b]
                        kT = KsT[0:64, :] if g == 0 else KsT[64:128, :]
                        for kb in range(0, 8192, 512):
                            pb = B[3 + (kb // 512) % 2]
                            kk.mm(pb[0:4, :], lq, kT[:, kb:kb + 512])
                            kk.stt(sc_s[:, kb:kb + 512].rearrange('p (a k) -> p a k', k=64), pb[0:4, :].rearrange('p (a k) -> p a k', k=64), 0.125,
                                   bc(negsel_s[:, b, g, kb // 64:kb // 64 + 8].rearrange('p (a o) -> p a o', o=1), [4, 8, 64]), ALU.mult, ALU.add)
                        kk.mm(B[3][0:4, 0:1], lq, kT[:, 8192:8193])
                        kk.ts(sc_s[:, 8192:8193], B[3][0:4, 0:1], 0.125, None, ALU.mult)
                        yield
                        softmax4(8193)
                        yield
                        transposes4(65)
                        yield
                        for j in range(64):
                            kk.mm(B[5][0:4, 0:64], pT_s[:, j, :], Vs_s[:, j, g * 64:(g + 1) * 64], start=(j == 0), stop=False)
                        kk.mm(B[5][0:4, 0:64], pT_s[0:1, 64, :], VSs[0:1, b, 0, g * 64:(g + 1) * 64], start=False, stop=True)
                        kk.ts(ocn[:], B[5][0:4, 0:64], rs[:, 0:1], None, ALU.mult)
                        kk.dma(osw[0:1, 0, g * 256:(g + 1) * 256].rearrange('p (h d) -> p h d', h=4), ocn[:])
                    yield
                    kk.dma(Wp[:], I['cwin'][b].rearrange('(c p) f -> p c f', p=128))
                    for g in range(2):
                        for c_ in range(4):
                            kk.tr(B[1][0:64, c_ * 128:(c_ + 1) * 128], Wp[:, c_, g * 64:(g + 1) * 64], ident_f)
                        kk.cpr(KwT[:, g, 0:512], B[1][0:64, :])
                        kk.cp(KwT[:, g, 512:513], kn_w[:, g, b:b + 1])
                    kk.cpr(Vw_s[:], Wp[:, :, 128:256])
                    kk.memset(p_s[:, 513:640], 0.0)
                    for g in range(2):
                        lq = qSr[:, g * 4:(g + 1) * 4, b]
                        kk.mm(B[3][0:4, :], lq, KwT[:, g, 0:512])
                        kk.ts(sc_s[:, 0:512], B[3][0:4, :], 0.125, None, ALU.mult)
                        kk.mm(B[4][0:4, 0:1], lq, KwT[:, g, 512:513])
                        kk.ts(sc_s[:, 512:513], B[4][0:4, 0:1], 0.125, None, ALU.mult)
                        kk.memset(sc_s[:, 0:1], NEG, eng='dve')
                        yield
                        softmax4(513)
                        yield
                        transposes4(5)
                        for j in range(4):
                            kk.mm(B[5][0:4, 0:64], pT_s[:, j, :], Vw_s[:, j, g * 64:(g + 1) * 64], start=(j == 0), stop=False)
                        kk.mm(B[5][0:4, 0:64], pT_s[0:1, 4, :], VSs[0:1, b, 1, g * 64:(g + 1) * 64], start=False, stop=True)
                        kk.ts(ocn[:], B[5][0:4, 0:64], rs[:, 0:1], None, ALU.mult)
                        kk.dma(osw[0:1, 1, g * 256:(g + 1) * 256].rearrange('p (h d) -> p h d', h=4), ocn[:])
                    yield
                    gv = Gs[0:1, b, :].rearrange('p (h r) -> p h r', r=3)
                    for br in range(3):
                        dst = acc1 if br == 0 else tmp1
                        osrc = ocmp[0:1, b, :] if br == 0 else osw[0:1, br - 1, :]
                        kk.tt(dst[0:1, :].rearrange('p (h d) -> p h d', h=8), osrc.rearrange('p (h d) -> p h d', h=8), bc(gv[:, :, br:br + 1], [1, 8, 64]), ALU.mult)
                        if br > 0:
                            kk.tt(acc1[0:1, :], acc1[0:1, :], tmp1[0:1, :], ALU.add)
                    kk.cp(catb[0:1, 0:512], acc1[0:1, :], eng='act')
                    kk.cp(catb[0:1, 512:1024], YCs[0:1, b, :], eng='dve')
                    pst = bfv(B[6], 8, 128)
                    for kc in range(8):
                        kk.tr(pst[:, kc, 0:1], catb[0:1, kc * 128:(kc + 1) * 128], ident_bf[0:1, 0:1])
                    kk.cpr(catT[:, :, 0:1], pst[:, :, 0:1])
                    xt = xt1
                    kk.dma(xt[0:1], I['xs'][b:b + 1, :])
                    for cb in range(2):
                        pb = B[1 + cb]
                        for kc in range(8):
                            kk.mm(pb[0:1, :], catT[:, kc, 0:1], Wout[:, kc, cb * 512:(cb + 1) * 512], start=(kc == 0), stop=(kc == 7))
                        kk.tt(xo[0:1, cb * 512:(cb + 1) * 512], pb[0:1, :], xt[0:1, cb * 512:(cb + 1) * 512], ALU.add)
                    kk.dma(xres[row0:row0 + 1, :], xo[0:1, :], q='pool')
                    yield

                pre0 = ffn_precast(0, s2)

                def pre_slice(cnt):
                    for _ in range(cnt):
                        if next(pre0, 'end') == 'end':
                            return
                        yield

                for _ in gather_gen(0):
                    pass
                for b in range(NS):
                    rr(attn_gen(b), gather_gen(b + 1) if b + 1 < NS else None, pre_slice(6))
                for _ in pre0:
                    pass
                S.barrier(); S.flush()

        def ffn_phase(l):
            wgs, wus = WGS[l], WUS[l]
            with ExitStack() as ph:
                W = common_work(ph)
                B = W['B']
                WD = sb(ph, 'WD', [128, 22, D], BF16)
                gfin = sb(ph, 'gfin', [128, D])
                stgW = sb(ph, 'stgFW', [128, 4, D])

                def load_wd_chunk(ci):
                    k0 = ci * 4
                    kn = min(4, 22 - k0)
                    kk.dma(stgW[:, 0:kn, :], I['wd'][l][k0 * 128:(k0 + kn) * 128, :].rearrange('(k p) w -> p k w', p=128), q='pool')
                    kk.cp(WD[:, k0:k0 + kn, :], stgW[:, 0:kn, :], eng='act')
                kk.dma(W['gain'][:], dram_bcast(I['norm_ffn'][l], 128, D))
                kk.dma(gfin[:], dram_bcast(I['norm_final'], 128, D))
                xnTbs = [sb(ph, 'xnTb0', [128, 8, 516], BF16), sb(ph, 'xnTb1', [128, 8, 516], BF16)]
                hT = sb(ph, 'hTf', [128, 22, 516], BF16)
                wgf = [sb(ph, 'wgf%d' % q_, [128, 8, 128], BF16) for q_ in range(4)]
                wuf = [sb(ph, 'wuf%d' % q_, [128, 8, 128], BF16) for q_ in range(4)]
                sgs = [sb(ph, 'sg0', [128, 516]), sb(ph, 'sg1', [128, 516])]
                xo = sb(ph, 'xof', [128, D])
                yo = sb(ph, 'yof', [128, D])
                xr = [sb(ph, 'xr0', [128, D]), sb(ph, 'xr1', [128, D])]
                blocks = []
                for blk in range(4):
                    btiles = [tl for tl in tiles if (tl[4] is None and tl[0] // 4 == blk) or (tl[4] is not None and blk == 3)]
                    cols = {}
                    c = 0
                    for tl in btiles:
                        cols[tl[0]] = c
                        c += tl[1]
                    blocks.append((btiles, cols, c))

                def norm_tile(blk, k_):
                    btiles, cols, ntok = blocks[blk]
                    if k_ >= len(btiles):
                        return
                    (ti, n, pos0, row0, sb_) = btiles[k_]
                    xt = W['xt'][ti % 2]
                    kk.dma(xt[:n], xres[row0:row0 + n, :])
                    norm_T(W, xt, n)
                    kk.cpr(xnTbs[blk % 2][:, :, cols[ti]:cols[ti] + n], W['xnT'][:, :, :n])

                def norm_block(blk):
                    for k_ in range(len(blocks[blk][0])):
                        norm_tile(blk, k_)

                def gateup_block(blk):
                    btiles, cols, ntok = blocks[blk]
                    xnTb = xnTbs[blk % 2]
                    segs = [(0, min(512, ntok))] + ([(512, ntok)] if ntok > 512 else [])
                    for fp in range(11):
                        hb = fp % 2
                        for fi in range(2):
                            f_ = fp * 2 + fi
                            kk.dma(wgf[f_ % 4][:].rearrange('p k w -> p (k w)'), wgs[f_])
                            kk.dma(wuf[f_ % 4][:].rearrange('p k w -> p (k w)'), wus[f_])
                        for fi in range(2):
                            f = fp * 2 + fi
                            for si, (s0, s1) in enumerate(segs):
                                w = s1 - s0
                                if si == 0:
                                    pg = B[1 + (f % 2) * 2]
                                    pu = B[2 + (f % 2) * 2]
                                else:
                                    pg = B[5]
                                    pu = B[6]
                                for kc in range(8):
                                    kk.mm(pg[:, 0:w], wgf[f % 4][:, kc, :], xnTb[:, kc, s0:s1], start=(kc == 0), stop=(kc == 7))
                                for kc in range(8):
                                    kk.mm(pu[:, 0:w], wuf[f % 4][:, kc, :], xnTb[:, kc, s0:s1], start=(kc == 0), stop=(kc == 7))
                                sg = sgs[f % 2]
                                kk.act(sg[:, s0:s1], pg[:, 0:w], AF.Silu)
                                kk.tt(hT[:, f, s0:s1], sg[:, s0:s1], pu[:, 0:w], ALU.mult)
                        if blk + 1 < 4:
                            norm_tile(blk + 1, fp)
                        if blk == 0 and fp < 6:
                            load_wd_chunk(fp)
                        if l == 0 and blk in (1, 2):
                            next(pre_next, None)

                def down_block(blk):
                    btiles, cols, ntok = blocks[blk]
                    for (ti, n, pos0, row0, sb_) in btiles:
                        c = cols[ti]
                        xt = xr[ti % 2]
                        kk.dma(xt[:n], xres[row0:row0 + n, :], q='pool')
                        for cb in range(2):
                            pb = B[5 + cb]
                            for f in range(22):
                                kk.mm(pb[:n, :], hT[:, f, c:c + n], WD[:, f, cb * 512:(cb + 1) * 512], start=(f == 0), stop=(f == 21))
                            kk.tt(xo[:n, cb * 512:(cb + 1) * 512], pb[:n, :], xt[:n, cb * 512:(cb + 1) * 512], ALU.add)
                        if l == 0:
                            kk.dma(xres[row0:row0 + n, :], xo[:n, :], q='pool')
                        else:
                            rstd_of(yo, xo, n, D, ssq2, rstd2)
                            kk.stt(yo[:n], xo[:n], rstd2[:n, 0:1], gfin[:n], ALU.mult, ALU.mult)
                            dst = O['y_p'][row0:row0 + n, :] if sb_ is None else O['y_s'][sb_:sb_ + 1, :]
                            kk.dma(dst, yo[:n, :], q='pool')

                ssq2 = sb(ph, 'ssq2', [128, 1])
                rstd2 = sb(ph, 'rstd2', [128, 1])
                pre_next = ffn_precast(1, ph, 'act') if l == 0 else iter(())
                norm_block(0)
                for blk in range(4):
                    gateup_block(blk)
                    down_block(blk)
                for _ in pre_next:
                    pass
                S.barrier(); S.flush()

        ffn_phase(0)

        with ExitStack() as ph:
            W = common_work(ph)
            B = W['B']
            WC = sb(ph, 'WC', [128, 8, 2 * LRU], BF16)
            WOC = sb(ph, 'WOC', [128, 10, D], BF16)
            LWA = sb(ph, 'LWA', [128, 10, 128], BF16)
            LWX = sb(ph, 'LWX', [128, 10, 128], BF16)
            lcw_f = sb(ph, 'lcw_f', [128, 10, 4])
            lcb_f = sb(ph, 'lcb_f', [128, 10])
            lba_f = sb(ph, 'lba_f', [128, 10])
            lbx_f = sb(ph, 'lbx_f', [128, 10])
            c8 = sb(ph, 'c8', [128, 10])
            dg = sb(ph, 'dgw', [128, 40, 128])
            lcb_row = sb(ph, 'lcb_row', [1, LRU])
            c8x2 = sb(ph, 'c8x2', [128, 10])
            with ExitStack() as ld:
                stg = [sb(ld, 'stgL0', [128, 8, 512]), sb(ld, 'stgL1', [128, 8, 512])]
                for i, c0 in enumerate(range(0, 2 * LRU, 512)):
                    load_w(WC[:, :, c0:c0 + 512], I['winc'][:, c0:c0 + 512], stg[i % 2], 8, 512)
                for i, c0 in enumerate(range(0, D, 512)):
                    kk.dma(stg[i % 2][:, 0:8, :], I['woutc'][0:1024, c0:c0 + 512].rearrange('(k p) w -> p k w', p=128))
                    kk.cpr(WOC[:, 0:8, c0:c0 + 512], stg[i % 2][:, 0:8, :])
                    kk.dma(stg[i % 2][:, 0:2, :], I['woutc'][1024:1280, c0:c0 + 512].rearrange('(k p) w -> p k w', p=128))
                    kk.cpr(WOC[:, 8:10, c0:c0 + 512], stg[i % 2][:, 0:2, :])
                sva = stg[0].rearrange('p a (b c) -> p (a b) c', b=4)[:, 0:10, :]
                svx = stg[1].rearrange('p a (b c) -> p (a b) c', b=4)[:, 0:10, :]
                kk.dma(sva, I['lwa'].rearrange('h i j -> i h j'))
                kk.cpr(LWA[:], sva)
                kk.dma(svx, I['lwx'].rearrange('h i j -> i h j'))
                kk.cpr(LWX[:], svx)
                for k_ in range(4):
                    kk.dma(lcw_f[:, :, k_], I['lcw'][k_].rearrange('(c p) -> p c', p=128), slow=True)
                kk.dma(lcb_f[:], I['lcb'].rearrange('(c p) -> p c', p=128), slow=True)
                kk.dma(lba_f[:], I['lba'].rearrange('(c p) -> p c', p=128), slow=True)
                kk.dma(lbx_f[:], I['lbx'].rearrange('(c p) -> p c', p=128), slow=True)
                kk.dma(c8[:], I['lam'].rearrange('(c p) -> p c', p=128), slow=True)
                kk.dma(W['gain'][:], dram_bcast(I['norm_mix'][1], 128, D))
                kk.dma(lcb_row[:], I['lcb'].rearrange('(o c) -> o c', o=1))
                for h_ in range(10):
                    for k_ in range(4):
                        kk.ts(dg[:, h_ * 4 + k_, :], ident_f, lcw_f[:, h_, k_:k_ + 1], None, ALU.mult, eng=('dve', 'pool')[k_ % 2])
                kk.act(c8[:], c8[:], AF.Exp, scale=-1.0)
                kk.act(c8[:], c8[:], AF.Ln, bias=1.0)
                kk.ts(c8[:], c8[:], -8.0, None, ALU.mult)
                kk.ts(c8x2[:], c8[:], 2.0, None, ALU.mult)
                S.barrier(); S.flush()
            cbuf = sb(ph, 'lcbuf', [128, 10, 131])
            c3 = sb(ph, 'lc3', [128, 10, 3])
            gq = sb(ph, 'gq', [128, 10, 128])
            gsb = sb(ph, 'gsb', [128, 10, 128])
            rg = sb(ph, 'rg', [128, 10, 128])
            ig = sb(ph, 'ig', [128, 10, 128])
            av = sb(ph, 'av', [128, 10, 128])
            hh = sb(ph, 'hh', [128, 10, 128])
            hst = sb(ph, 'hst', [128, 10])
            yT = sb(ph, 'yT', [128, 10, 128], BF16)
            xo = sb(ph, 'xol', [128, D])
            cas = [sb(ph, 'lca0', [128, 10, 128]), sb(ph, 'lca1', [128, 10, 128])]
            gus = [sb(ph, 'gu0', [128, 10, 128]), sb(ph, 'gu1', [128, 10, 128])]
            xcbs = [sb(ph, 'xcb0', [128, 10, 128], BF16), sb(ph, 'xcb1', [128, 10, 128], BF16)]
            ssqs = [sb(ph, 'lssq0', [128, 1]), sb(ph, 'lssq1', [128, 1])]
            rstds = [sb(ph, 'lrstd0', [128, 1]), sb(ph, 'lrstd1', [128, 1])]

            def bodyL(i):
                (ti, n, pos0, row0, sb_) = tiles[i]
                par = i % 2
                pa, pb, pc, pd = [B[4 * par + k_] for k_ in range(4)]
                ca, gu, xcb = cas[par], gus[par], xcbs[par]
                ssq, rstd = ssqs[par], rstds[par]
                xt = W['xt'][par]
                xn, xnT = W['xn'], W['xnT']

                def slot(h):
                    bk = (pb, pc, pd)[h // 4]
                    return bk[:, (h % 4) * 128:(h % 4) * 128 + n]

                def slots3():
                    return [(pb[:, :].rearrange('p (c t) -> p c t', c=4)[:, :, :n], 0, 4),
                            (pc[:, :].rearrange('p (c t) -> p c t', c=4)[:, :, :n], 4, 4),
                            (pd[:, 0:256].rearrange('p (c t) -> p c t', c=2)[:, :, :n], 8, 2)]
                kk.dma(xt[:n], xres[row0:row0 + n, :])
                rstd_of(W['junk'], xt, n, D, ssq, rstd)
                yield
                kk.stt(xn[:n], xt[:n], rstd[:n, 0:1], W['gain'][:n], ALU.mult, ALU.mult)
                yield
                psT = bfv(pa, 8, 128)
                for kc in range(8):
                    kk.tr(psT[:, kc, :n], xn[:n, kc * 128:(kc + 1) * 128], ident_bf[:n, :n])
                yield
                kk.cp(xnT[:, :, :n], psT[:, :, :n], eng='act')
                yield
                for h in range(10):
                    fc = 10 + h
                    for kc in range(8):
                        kk.mm(slot(h), WC[:, kc, fc * 128:(fc + 1) * 128], xnT[:, kc, :n], start=(kc == 0), stop=(kc == 7))
                yield
                if ti == 0:
                    kk.memset(cbuf[:, :, 0:3], 0.0)
                if sb_ is not None:
                    for j_ in range(3):
                        kk.dma(cbuf[:, :, j_], I['slconv'][sb_, j_].rearrange('(c p) -> p c', p=128), slow=True)
                for (pv_, h0, k_) in slots3():
                    kk.cpr(cbuf[:, h0:h0 + k_, 3:3 + n], pv_)
                yield
                if sb_ is not None or ti == NT - 1:
                    kk.cp(c3[:], cbuf[:, :, n:n + 3], eng='act')
                    dst = O['lconv_p'] if sb_ is None else O['lconv_s'][sb_]
                    for j_ in range(3):
                        kk.dma(dst[j_].rearrange('(c p) -> p c', p=128), c3[:, :, j_], q='pool', slow=True)
                for h in range(10):
                    for k_ in range(4):
                        kk.mm(slot(h), dg[:, h * 4 + k_, :], cbuf[:, h, k_:k_ + n], start=(k_ == 0), stop=False)
                    kk.mm(slot(h), lcb_row[0:1, h * 128:(h + 1) * 128], ones_f[0:1, 0:n], start=False, stop=True)
                yield
                for (pv_, h0, k_) in slots3():
                    kk.act(ca[:, h0:h0 + k_, :n], pv_, AF.Copy)
                    kk.act(xcb[:, h0:h0 + k_, :n], pv_, AF.Copy)
                if sb_ is None and ti < NT - 1:
                    kk.cp(c3[:], cbuf[:, :, n:n + 3], eng='act')
                    kk.cp(cbuf[:, :, 0:3], c3[:], eng='act')
                yield
                for h in range(10):
                    for kc in range(8):
                        kk.mm(slot(h), WC[:, kc, h * 128:(h + 1) * 128], xnT[:, kc, :n], start=(kc == 0), stop=(kc == 7))
                yield
                for (pv_, h0, k_) in slots3():
                    kk.act(gsb[:, h0:h0 + k_, :n], pv_, AF.Copy)
                yield
                kk.tt(gq[:, :, :n], gsb[:, :, :n], gsb[:, :, :n], ALU.mult)
                kk.ts(gq[:, :, :n], gq[:, :, :n], 0.044715, 1.0, ALU.mult, ALU.add)
                kk.tt(gq[:, :, :n], gq[:, :, :n], gsb[:, :, :n], ALU.mult)
                yield
                kk.act(gq[:, :, :n], gq[:, :, :n], AF.Sigmoid, scale=1.5957691216057308)
                yield
                kk.tt(gu[:, :, :n], gq[:, :, :n], gsb[:, :, :n], ALU.mult)
                for h in range(10):
                    kk.mm(slot(h), LWA[:, h, :], xcb[:, h, :n])
                yield
                for h in range(10):
                    kk.act(rg[:, h, :n], slot(h), AF.Sigmoid, bias=lba_f[:, h:h + 1])
                yield
                for h in range(10):
                    kk.mm(slot(h), LWX[:, h, :], xcb[:, h, :n])
                kk.tt(rg[:, :, :n], rg[:, :, :n], bc(c8[:].rearrange('p (c o) -> p c o', o=1), [128, 10, n]), ALU.mult)
                yield
                for h in range(10):
                    kk.act(ig[:, h, :n], slot(h), AF.Sigmoid, bias=lbx_f[:, h:h + 1])
                kk.act(av[:, :, :n], rg[:, :, :n], AF.Exp)
                kk.act(rg[:, :, :n], rg[:, :, :n], AF.Exp, scale=2.0)
                kk.act(rg[:, :, :n], rg[:, :, :n], AF.Ln, scale=-1.0, bias=1.0)
                kk.act(rg[:, :, :n], rg[:, :, :n], AF.Exp, scale=0.5)
                yield
                if ti == 0:
                    kk.memset(hst[:], 0.0)
                if sb_ is not None:
                    kk.dma(hst[:], I['slru'][sb_].rearrange('(c p) -> p c', p=128), slow=True)
                kk.tt(ig[:, :, :n], ig[:, :, :n], ca[:, :, :n], ALU.mult)
                kk.tt(ig[:, :, :n], ig[:, :, :n], rg[:, :, :n], ALU.mult)
                for h in range(10):
                    S.add('dve', lambda e, h=h: e.tensor_tensor_scan(out=hh[:, h, :n], data0=av[:, h, :n], data1=ig[:, h, :n], initial=hst[:, h:h + 1], op0=ALU.mult, op1=ALU.add),
                          reads=[av[:, h, :n], ig[:, h, :n], hst[:, h:h + 1]], writes=[hh[:, h, :n]])
                kk.cp(hst[:], hh[:, :, n - 1])
                if sb_ is not None or ti == NT - 1:
                    dst = O['lru_p'] if sb_ is None else O['lru_s'][sb_]
                    kk.dma(dst.rearrange('(c p) -> p c', p=128), hst[:], q='pool', slow=True)
                kk.tt(yT[:, :, :n], gu[:, :, :n], hh[:, :, :n], ALU.mult)
                yield
                for cb in range(2):
                    pbk = (pb, pc)[cb]
                    for h in range(10):
                        kk.mm(pbk[:n, :], yT[:, h, :n], WOC[:, h, cb * 512:(cb + 1) * 512], start=(h == 0), stop=(h == 9))
                yield
                for cb in range(2):
                    pbk = (pb, pc)[cb]
                    kk.tt(xo[:n, cb * 512:(cb + 1) * 512], pbk[:n, :], xt[:n, cb * 512:(cb + 1) * 512], ALU.add)
                kk.dma(xres[row0:row0 + n, :], xo[:n, :], q='pool')
                yield

            zipp([(lambda i=i: bodyL(i)) for i in range(len(tiles))], 13, 'BABBABBABBBBABBBABA')
            S.barrier(); S.flush()

        ffn_phase(1)

        S.barrier(); S.flush()
    return nc


def _consts():
    p = np.arange(128)[:, None]
    j = np.arange(128)[None, :]
    c = np.zeros((128, 768), np.float32)
    c[:, 0:128] = (p == j)
    c[:, 128:256] = (p > j)
    c[:, 256:384] = (p <= j)
    c[:, 384:512] = np.where(j <= p, 0.0, NEG)
    c[:, 512:640] = np.where(j <= p, NEG, 0.0)
    c[:, 640:768] = 1.0
    half = 32
    inv_freq = (np.float32(10000.0) ** (-(np.arange(half, dtype=np.float32)) / np.float32(half))).astype(np.float32)
    pos = np.concatenate([np.arange(T), np.full(NS, 8192)]).astype(np.float32)
    ang = (pos[:, None] * inv_freq[None, :]).astype(np.float32)
    rt = np.concatenate([np.cos(ang), np.sin(ang)], axis=1).astype(np.float32)
    n = np.arange(128)[:, None] * 16
    s = np.arange(32)[None, :] * 64
    ovl = ((n < s + 64) & (n + 32 > s)).astype(np.float32)
    c2 = np.zeros((128, 4), np.float32)
    c2[:, 0] = np.arange(128) % 8
    c2[:, 1] = np.arange(128)
    n5 = np.arange(512)[:, None] * 16
    s5 = np.arange(129)[None, :] * 64
    ovls = ((n5 < s5 + 64) & (n5 + 32 > s5)).astype(np.float32)
    ovls[511, :] = 0.0
    return c, rt, ovl, c2, ovls


_NC_CACHE = {}


def kernel(x_prompt, x_sample, cache_kv_cmp, cache_kv_slc, cache_kv_win, state_ssm, state_ssd_conv,
           state_lru, state_lru_conv, page_table, norm_mix, norm_ffn, norm_final, w_ffn_gate, w_ffn_up,
           w_ffn_down, w_in_a, w_out_a, cmp_pe_k, cmp_w1_k, cmp_w2_k, cmp_pe_v, cmp_w1_v, cmp_w2_v,
           ssd_conv_w, ssd_conv_b, ssd_dt_bias, ssd_a_log, ssd_d, ssd_norm, w_in_c, lru_conv_w, lru_conv_b,
           lru_w_a, lru_b_a, lru_w_x, lru_b_x, lru_lambda, w_out_c):
    f = lambda a: np.ascontiguousarray(np.asarray(a, dtype=np.float32))
    if 'nc' not in _NC_CACHE:
        _NC_CACHE['nc'] = build_program()
    nc = _NC_CACHE['nc']
    cst, rt, ovl, c2, ovls = _consts()
    ccmp = f(cache_kv_cmp).reshape(NPHYS * 128, 256)
    cslc = f(cache_kv_slc).reshape(NPHYS * 128, 256)
    shared = dict(
        ccmp=ccmp, cslc=cslc,
        norm_mix=f(norm_mix), norm_ffn=f(norm_ffn), norm_final=f(norm_final),
        wg=f(w_ffn_gate), wu=f(w_ffn_up), wd=f(w_ffn_down), win=f(w_in_a)[0], wout=f(w_out_a)[0],
        pek=f(cmp_pe_k)[0], w1k=f(cmp_w1_k)[0], w2k=f(cmp_w2_k)[0],
        pev=f(cmp_pe_v)[0], w1v=f(cmp_w1_v)[0], w2v=f(cmp_w2_v)[0],
        cw=f(ssd_conv_w)[0], cb=f(ssd_conv_b)[0], dtb=f(ssd_dt_bias)[0], alog=f(ssd_a_log)[0],
        dsk=f(ssd_d)[0], snorm=f(ssd_norm)[0],
        winc=f(w_in_c)[0], lcw=f(lru_conv_w)[0], lcb=f(lru_conv_b)[0], lwa=f(lru_w_a)[0], lba=f(lru_b_a)[0],
        lwx=f(lru_w_x)[0], lbx=f(lru_b_x)[0], lam=f(lru_lambda)[0], woutc=f(w_out_c)[0],
        cst=cst, ropetab=rt, ovl=ovl, cst2=c2, ovls=ovls,
    )
    xp = f(x_prompt)
    xs = f(x_sample)[:, 0, :]
    pt = np.ascontiguousarray(np.asarray(page_table, dtype=np.int32))
    in_maps = []
    for c in range(8):
        sl = slice(NS * c, NS * (c + 1))
        m = dict(shared)
        m.update(
            xp=xp[c], xs=np.ascontiguousarray(xs[sl]),
            cwin=np.ascontiguousarray(f(cache_kv_win)[0, sl].reshape(NS, 512, 256)),
            sssm=np.ascontiguousarray(f(state_ssm)[0, sl].reshape(NS, 512, 128)),
            sconv=np.ascontiguousarray(f(state_ssd_conv)[0, sl]),
            slru=np.ascontiguousarray(f(state_lru)[0, sl]),
            slconv=np.ascontiguousarray(f(state_lru_conv)[0, sl]),
            ptab=np.ascontiguousarray(pt[sl]),
        )
        in_maps.append(m)
    res = run_bass_kernel_spmd(nc, in_maps, core_ids=list(range(8))).results
    cat = lambda k: np.stack([np.asarray(r[k]) for r in res])
    cats = lambda k: np.concatenate([np.asarray(r[k]) for r in res], 0)
    y_prompt = cat('y_p').reshape(8, T, D)
    y_sample = cats('y_s').reshape(32, 1, D)
    kv_cmp_p = cat('kvc_p').reshape(1, 8, T, 2, 2, 64)
    kv_slc_p = cat('kvs_p').reshape(1, 8, T, 2, 2, 64)
    kv_win_p = cat('kvw_p').reshape(1, 8, 512, 2, 2, 64)
    ssm_p = cat('ssm_p').reshape(1, 8, 8, 64, 128)
    sconv_p = cat('sconv_p').reshape(1, 8, 3, 1024)
    lru_p = cat('lru_p').reshape(1, 8, LRU)
    lconv_p = cat('lconv_p').reshape(1, 8, 3, LRU)
    kv_cmp_s = cats('kvc_s').reshape(1, 32, 1, 2, 2, 64)
    kv_slc_s = cats('kvs_s').reshape(1, 32, 1, 2, 2, 64)
    kv_win_s = cats('kvw_s').reshape(1, 32, 512, 2, 2, 64)
    ssm_s = cats('ssm_s').reshape(1, 32, 8, 64, 128)
    sconv_s = cats('sconv_s').reshape(1, 32, 3, 1024)
    lru_s = cats('lru_s').reshape(1, 32, LRU)
    lconv_s = cats('lconv_s').reshape(1, 32, 3, LRU)
    outs = (y_prompt, y_sample, kv_cmp_p, kv_slc_p, kv_win_p, ssm_p, sconv_p, lru_p, lconv_p,
            kv_cmp_s, kv_slc_s, kv_win_s, ssm_s, sconv_s, lru_s, lconv_s)
    return tuple(np.ascontiguousarray(o, dtype=np.float32) for o in outs)
```

```python
import numpy as np
from contextlib import ExitStack
import concourse.bass as bass
import concourse.mybir as mybir
from concourse.bass_utils import run_bass_kernel_spmd

F32 = mybir.dt.float32
BF16 = mybir.dt.bfloat16
I32 = mybir.dt.int32
ALU = mybir.AluOpType
AF = mybir.ActivationFunctionType
AX = mybir.AxisListType

T = 2048
D = 1024
NT = 16
NS = 4
NPHYS = 2560
FFN = 2816
LRU = 1280
EPS = 1e-6
NEG = -1e30


def _region(ap):
    t = ap.tensor
    name = t.name
    dsz = mybir.dt.size(ap.dtype)
    pairs = [(int(s), int(c)) for s, c in ap.ap]
    off = int(ap.offset)
    if 'DRam' in type(t).__name__:
        lo = off + sum(min(0, s * (c - 1)) for s, c in pairs)
        hi = off + sum(max(0, s * (c - 1)) for s, c in pairs) + 1
        return (name, 0, 1, lo * dsz, hi * dsz)
    if 'PSum' in type(t).__name__:
        return (name, 0, 128, 0, 2048, True)
    R = 1
    for d in list(t.shape)[1:]:
        R *= int(d)
    ps, pc = pairs[0]
    p_lo = off // R
    pstep = max(1, ps // R) if ps else 0
    p_hi = p_lo + (pc - 1) * pstep + 1
    f0 = off % R
    lo = f0 + sum(min(0, s * (c - 1)) for s, c in pairs[1:])
    hi = f0 + sum(max(0, s * (c - 1)) for s, c in pairs[1:]) + 1
    return (name, p_lo, p_hi, lo * dsz, hi * dsz)


def _ovl(a, b):
    return a[1] < b[2] and b[1] < a[2] and a[3] < b[4] and b[3] < a[4]


def _cov(a, b):
    return a[1] <= b[1] and a[2] >= b[2] and a[3] <= b[3] and a[4] >= b[4]


class Op:
    __slots__ = ('i', 'eng', 'fn', 'dma', 'deps', 'signal', 'sem', 'semval', 'waits')


class Sched:
    ENGS = ['pe', 'act', 'pool', 'dve', 'sp']

    def __init__(self, nc, stack, ndma=56):
        self.nc = nc
        self.esem = {e: stack.enter_context(nc.semaphore('es_' + e)) for e in self.ENGS}
        self.dsem = [stack.enter_context(nc.semaphore('ds_%d' % i)) for i in range(ndma)]
        self.duse = [0] * ndma
        self.dlast = [None] * ndma
        self.dk = 0
        self.dk2 = 0
        self.ecnt = {e: 0 for e in self.ENGS}
        self.waited = {e: {} for e in self.ENGS}
        self.pending = []
        self.acc = {}
        self.n = 0
        self.last = {e: None for e in self.ENGS}
        self.dma_since = []

    def add(self, eng, fn, reads=(), writes=(), dma=False, extra_deps=()):
        op = Op()
        op.i = self.n
        self.n += 1
        op.eng = eng
        op.fn = fn
        op.dma = dma
        op.deps = set(extra_deps)
        op.signal = False
        op.sem = None
        op.semval = 0
        op.waits = []
        rr = [_region(a) for a in reads]
        ww = [_region(a) for a in writes]
        for r in rr:
            psum = len(r) > 5
            for (reg, o, isw) in self.acc.get(r[0], ()):
                if (isw or (psum and (o.eng != eng or o.dma != dma))) and _ovl(reg, r):
                    op.deps.add(o)
        for w in ww:
            for (reg, o, isw) in self.acc.get(w[0], ()):
                if _ovl(reg, w):
                    op.deps.add(o)
        for w in ww:
            lst = self.acc.setdefault(w[0], [])
            lst[:] = [x for x in lst if not _cov(w, x[0])]
            lst.append((w, op, True))
        for r in rr:
            lst = self.acc.setdefault(r[0], [])
            lst[:] = [x for x in lst if not ((not x[2]) and x[1].eng == eng and x[1].dma == dma and _cov(r, x[0]))]
            lst.append((r, op, False))
        op.deps.discard(op)
        if dma:
            half = len(self.dsem) // 2
            if eng == 'pool':
                k = half + self.dk2 % (len(self.dsem) - half)
                self.dk2 += 1
            else:
                k = self.dk % half
                self.dk += 1
            if self.dlast[k] is not None:
                op.deps.add(self.dlast[k])
            self.duse[k] += 1
            op.sem = self.dsem[k]
            op.semval = 16 * self.duse[k]
            self.dlast[k] = op
            self.dma_since.append(op)
        else:
            self.last[eng] = op
        self.pending.append(op)
        return op

    def barrier(self):
        deps = [o for o in self.last.values() if o is not None] + list(self.dma_since)
        for e in self.ENGS:
            self.add(e, None, extra_deps=list(deps))
        self.dma_since = []
        self.acc = {}
        self.last = {e: None for e in self.ENGS}

    def flush(self):
        nc = self.nc
        pend = self.pending
        self.pending = []
        if not pend:
            return
        first_i = pend[0].i
        for op in pend:
            for d in op.deps:
                if d.dma or d.i < first_i:
                    continue
                if d.eng == 'pe' and op.eng == 'pe' and not op.dma:
                    continue
                d.signal = True
        for op in pend:
            if not op.dma and op.signal:
                self.ecnt[op.eng] += 1
                op.sem = self.esem[op.eng]
                op.semval = self.ecnt[op.eng]
        per = {e: [] for e in self.ENGS}
        for op in pend:
            wl = {}
            for d in op.deps:
                if d.i < first_i and not d.dma:
                    continue
                if (not d.dma) and d.eng == 'pe' and op.eng == 'pe' and not op.dma:
                    continue
                if d.sem is None:
                    continue
                k = id(d.sem)
                if k not in wl or wl[k][1] < d.semval:
                    wl[k] = (d.sem, d.semval)
            wd = self.waited[op.eng]
            for k, (s, v) in wl.items():
                if wd.get(k, 0) >= v:
                    continue
                wd[k] = v
                op.waits.append((s, v))
            per[op.eng].append(op)

        def emit(e, ops):
            for op in ops:
                for (s, v) in op.waits:
                    e.wait_ge(s, v)
                if op.fn is None:
                    continue
                ins = op.fn(e)
                if op.dma:
                    ins.then_inc(op.sem, 16)
                elif op.signal:
                    ins.then_inc(op.sem, 1)

        with nc.Block() as blk:
            @blk.tensor
            def _(e):
                emit(e, per['pe'])

            @blk.scalar
            def _(e):
                emit(e, per['act'])

            @blk.gpsimd
            def _(e):
                emit(e, per['pool'])

            @blk.vector
            def _(e):
                emit(e, per['dve'])

            @blk.sync
            def _(e):
                emit(e, per['sp'])


class K:
    def __init__(self, S):
        self.S = S
        self.rr = 0

    def dma(self, out, in_, q='sp', slow=False):
        if slow:
            return self.S.add(q, lambda e: e.dma_start(out=out, in_=in_, allow_slow_non_contiguous=True), reads=[in_], writes=[out], dma=True)
        return self.S.add(q, lambda e: e.dma_start(out=out, in_=in_), reads=[in_], writes=[out], dma=True)

    def mm(self, out, lhsT, rhs, start=True, stop=True):
        return self.S.add('pe', lambda e: e.matmul(out, lhsT=lhsT, rhs=rhs, start=start, stop=stop), reads=[lhsT, rhs], writes=[out])

    def tr(self, out, in_, ident):
        return self.S.add('pe', lambda e: e.transpose(out=out, in_=in_, identity=ident), reads=[in_, ident], writes=[out])

    def tt(self, out, in0, in1, op, eng='dve'):
        return self.S.add(eng, lambda e: e.tensor_tensor(out=out, in0=in0, in1=in1, op=op), reads=[in0, in1], writes=[out])

    def ts(self, out, in0, s1, s2, op0, op1=None, eng='dve'):
        rd = [in0] + [s for s in (s1, s2) if not isinstance(s, (int, float, type(None)))]
        if op1 is None:
            return self.S.add(eng, lambda e: e.tensor_scalar(out=out, in0=in0, scalar1=s1, scalar2=None, op0=op0), reads=rd, writes=[out])
        return self.S.add(eng, lambda e: e.tensor_scalar(out=out, in0=in0, scalar1=s1, scalar2=s2, op0=op0, op1=op1), reads=rd, writes=[out])

    def stt(self, out, in0, scalar, in1, op0, op1):
        rd = [in0, in1] + ([scalar] if not isinstance(scalar, (int, float)) else [])
        return self.S.add('dve', lambda e: e.scalar_tensor_tensor(out=out, in0=in0, scalar=scalar, in1=in1, op0=op0, op1=op1), reads=rd, writes=[out])

    def act(self, out, in_, func, bias=None, scale=None, accum=None):
        rd = [in_]
        wr = [out]
        kw = {}
        if bias is not None:
            kw['bias'] = bias
            if not isinstance(bias, (int, float)):
                rd.append(bias)
        if scale is not None:
            kw['scale'] = scale
            if not isinstance(scale, (int, float)):
                rd.append(scale)
        if accum is not None:
            kw['accum_out'] = accum
            wr.append(accum)
        return self.S.add('act', lambda e: e.activation(out=out, in_=in_, func=func, **kw), reads=rd, writes=wr)

    def cp(self, out, in_, eng='dve'):
        if eng == 'act':
            return self.S.add('act', lambda e: e.copy(out=out, in_=in_), reads=[in_], writes=[out])
        return self.S.add(eng, lambda e: e.tensor_copy(out=out, in_=in_), reads=[in_], writes=[out])

    def cpr(self, out, in_):
        self.rr += 1
        return self.cp(out, in_, eng=('dve', 'act')[self.rr % 2])

    def memset(self, ap, v, eng='pool'):
        return self.S.add(eng, lambda e: e.memset(ap, v), writes=[ap])

    def recip(self, out, in_):
        return self.S.add('dve', lambda e: e.reciprocal(out=out, in_=in_), reads=[in_], writes=[out])

    def rmax(self, out, in_):
        return self.S.add('dve', lambda e: e.tensor_reduce(out=out, in_=in_, axis=AX.X, op=ALU.max), reads=[in_], writes=[out])


def bc(ap, shape):
    return ap.to_broadcast(list(shape))


def rr(*gens):
    gens = [g for g in gens if g is not None]
    while gens:
        for g in list(gens):
            try:
                next(g)
            except StopIteration:
                gens.remove(g)


def zip2(bodies, half):
    def adv(g):
        try:
            next(g)
            return True
        except StopIteration:
            return False
    cur, cnt = None, 0
    for mk in bodies:
        nxt, ncnt = mk(), 0
        if cur is not None:
            while cnt < half:
                if not adv(cur):
                    cur = None
                    break
                cnt += 1
            while cur is not None:
                if not adv(cur):
                    cur = None
                    break
                if adv(nxt):
                    ncnt += 1
        cur, cnt = nxt, ncnt
    while cur is not None and adv(cur):
        pass


def zipp(bodies, half, pattern):
    def adv(g):
        try:
            next(g)
            return True
        except StopIteration:
            return False
    cur = None
    for mk in bodies:
        nxt = mk()
        if cur is None:
            for _ in range(half):
                adv(nxt)
            cur = nxt
            continue
        for ch in pattern:
            adv(cur if ch == 'A' else nxt)
        while adv(cur):
            pass
        cur = nxt
    while cur is not None and adv(cur):
        pass


def dram_bcast(ap1d, nparts, width):
    return bass.AP(ap1d.tensor, int(ap1d.offset), [[0, nparts], [1, width]])


def build_program():
    nc = bass.Bass("TRN2", target_bir_lowering=False)

    def din(name, shape, dt=F32):
        return nc.dram_tensor(name, list(shape), dt, kind="ExternalInput").ap()

    def dout(name, shape):
        return nc.dram_tensor(name, list(shape), F32, kind="ExternalOutput").ap()

    I = dict(
        xp=din('xp', [T, D]), xs=din('xs', [NS, D]),
        ccmp=din('ccmp', [NPHYS * 128, 256]), cslc=din('cslc', [NPHYS * 128, 256]),
        cwin=din('cwin', [NS, 512, 256]), sssm=din('sssm', [NS, 512, 128]),
        sconv=din('sconv', [NS, 3, 1024]), slru=din('slru', [NS, LRU]), slconv=din('slconv', [NS, 3, LRU]),
        ptab=din('ptab', [NS, 64], I32),
        norm_mix=din('norm_mix', [2, D]), norm_ffn=din('norm_ffn', [2, D]), norm_final=din('norm_final', [D]),
        wg=din('wg', [2, D, FFN]), wu=din('wu', [2, D, FFN]), wd=din('wd', [2, FFN, D]),
        win=din('win', [D, 2848]), wout=din('wout', [D, D]),
        pek=din('pek', [32, 64]), w1k=din('w1k', [2048, 256]), w2k=din('w2k', [256, 64]),
        pev=din('pev', [32, 64]), w1v=din('w1v', [2048, 256]), w2v=din('w2v', [256, 64]),
        cw=din('cw', [4, 1024]), cb=din('cb', [1024]), dtb=din('dtb', [8]), alog=din('alog', [8]),
        dsk=din('dsk', [8]), snorm=din('snorm', [512]),
        winc=din('winc', [D, 2 * LRU]), lcw=din('lcw', [4, LRU]), lcb=din('lcb', [LRU]),
        lwa=din('lwa', [10, 128, 128]), lba=din('lba', [LRU]), lwx=din('lwx', [10, 128, 128]), lbx=din('lbx', [LRU]),
        lam=din('lam', [LRU]), woutc=din('woutc', [LRU, D]),
        cst=din('cst', [128, 768]), ropetab=din('ropetab', [T + NS, 64]),
        ovl=din('ovl', [128, 32]), cst2=din('cst2', [128, 4]), ovls=din('ovls', [512, 129]),
    )
    O = dict(
        y_p=dout('y_p', [T, D]), y_s=dout('y_s', [NS, D]),
        kvc_p=dout('kvc_p', [T, 256]), kvs_p=dout('kvs_p', [T, 256]), kvw_p=dout('kvw_p', [512, 256]),
        ssm_p=dout('ssm_p', [512, 128]), sconv_p=dout('sconv_p', [3, 1024]),
        lru_p=dout('lru_p', [LRU]), lconv_p=dout('lconv_p', [3, LRU]),
        kvc_s=dout('kvc_s', [NS, 256]), kvs_s=dout('kvs_s', [NS, 256]), kvw_s=dout('kvw_s', [NS, 512, 256]),
        ssm_s=dout('ssm_s', [NS, 512, 128]), sconv_s=dout('sconv_s', [NS, 3, 1024]),
        lru_s=dout('lru_s', [NS, LRU]), lconv_s=dout('lconv_s', [NS, 3, LRU]),
    )
    xres = nc.dram_tensor('xres', [T + NS, D], F32, kind="Internal").ap()

    tiles = [(t, 128, t * 128, t * 128, None) for t in range(NT)] + [(NT + b, 1, 8192, T + b, b) for b in range(NS)]

    with ExitStack() as top:
        S = Sched(nc, top)
        kk = K(S)
        _uid = [0]

        def sb(st, name, shape, dt=F32):
            _uid[0] += 1
            return st.enter_context(nc.sbuf_tensor('s%d_%s' % (_uid[0], name), list(shape), dt))
        cst = sb(top, 'cst', [128, 768])
        ident_bf = sb(top, 'ident_bf', [128, 128], BF16)
        kk.dma(cst[:], I['cst'])
        kk.cp(ident_bf[:], cst[:, 0:128])
        ident_f = cst[:, 0:128]
        Lstrict = cst[:, 128:256]
        Utri = cst[:, 256:384]
        causal = cst[:, 384:512]
        wlo = cst[:, 512:640]
        ones_f = cst[:, 640:768]
        QTs = sb(top, 'QTs', [128, 8, NS], BF16)
        KAs = sb(top, 'KAs', [128, 2, NS], BF16)
        KBs = sb(top, 'KBs', [128, 2, NS], BF16)
        VSs = sb(top, 'VSs', [1, NS, 2, 128], BF16)
        Gs = sb(top, 'Gs', [1, NS, 24])
        YCs = sb(top, 'YCs', [1, NS, 512], BF16)
        cst2 = sb(top, 'cst2', [128, 4])
        kk.dma(cst2[:], I['cst2'])
        l0 = ExitStack()
        QT = sb(l0, 'QT', [128, 8, T], BF16)
        KA = sb(l0, 'KA', [128, 2, T], BF16)
        KB = sb(l0, 'KB', [128, 2, T], BF16)
        VS = sb(l0, 'VS', [128, NT, 2, 128], BF16)
        G = sb(l0, 'G', [128, NT, 24])
        YC = sb(l0, 'YC', [128, NT, 512], BF16)

        def common_work(st):
            W = {}
            W['xt'] = [sb(st, 'xt0', [128, D]), sb(st, 'xt1', [128, D])]
            W['xn'] = sb(st, 'xn', [128, D], BF16)
            W['xnT'] = sb(st, 'xnT', [128, 8, 128], BF16)
            W['gain'] = sb(st, 'gain', [128, D])
            W['ssq'] = sb(st, 'ssq', [128, 1])
            W['rstd'] = sb(st, 'rstd', [128, 1])
            W['junk'] = sb(st, 'junk', [128, D])
            _uid[0] += 1
            W['B'] = [st.enter_context(nc.psum_tensor('B%d_%d' % (_uid[0], i), [128, 512], F32)) for i in range(8)]
            return W

        def bfv(bank, a, b):
            return bank[:].bitcast(BF16).rearrange('p (a b) -> p a b', a=a, b=b)

        def rstd_of(junk, x, n, width, ssq, rstd):
            S.add('dve', lambda e: e.scalar_tensor_tensor(out=junk[:n, 0:width], in0=x[:n], scalar=1.0, in1=x[:n], op0=ALU.mult, op1=ALU.mult, accum_out=ssq[:n]),
                  reads=[x[:n]], writes=[junk[:n, 0:width], ssq[:n]])
            kk.act(rstd[:n], ssq[:n], AF.Ln, scale=1.0 / width, bias=EPS)
            kk.act(rstd[:n], rstd[:n], AF.Exp, scale=-0.5)

        def norm_T(W, xt, n, xnT=None):
            xnT = W['xnT'] if xnT is None else xnT
            rstd_of(W['junk'], xt, n, D, W['ssq'], W['rstd'])
            kk.stt(W['xn'][:n], xt[:n], W['rstd'][:n, 0:1], W['gain'][:n], ALU.mult, ALU.mult)
            psT = bfv(W['B'][0], 8, 128)
            for kc in range(8):
                kk.tr(psT[:, kc, :n], W['xn'][:n, kc * 128:(kc + 1) * 128], ident_bf[:n, :n])
            kk.cpr(xnT[:, :, :n], psT[:, :, :n])

        def load_w(dst, src2d, stg, K_, w):
            kk.dma(stg[:, :K_, :w], src2d.rearrange('(k p) w -> p k w', p=128))
            kk.cpr(dst, stg[:, :K_, :w])

        WGS = [nc.dram_tensor('wgs%d' % l_, [22, 128, 1024], BF16, kind="Internal").ap() for l_ in range(2)]
        WUS = [nc.dram_tensor('wus%d' % l_, [22, 128, 1024], BF16, kind="Internal").ap() for l_ in range(2)]

        def ffn_precast(l_, st, q_='sp', ce=('act', 'dve')):
            sg_ = sb(st, 'pcg%d' % l_, [128, 8, 128])
            su_ = sb(st, 'pcu%d' % l_, [128, 8, 128])
            bg_ = sb(st, 'pbg%d' % l_, [128, 1024], BF16)
            bu_ = sb(st, 'pbu%d' % l_, [128, 1024], BF16)
            for f in range(22):
                kk.dma(sg_[:], I['wg'][l_][:, f * 128:(f + 1) * 128].rearrange('(k p) w -> p k w', p=128), q=q_)
                kk.dma(su_[:], I['wu'][l_][:, f * 128:(f + 1) * 128].rearrange('(k p) w -> p k w', p=128), q=q_)
                kk.cp(bg_[:], sg_[:].rearrange('p k w -> p (k w)'), eng=ce[0])
                kk.cp(bu_[:], su_[:].rearrange('p k w -> p (k w)'), eng=ce[1])
                kk.dma(WGS[l_][f], bg_[:], q=q_)
                kk.dma(WUS[l_][f], bu_[:], q=q_)
                yield

        with ExitStack() as ph:
            W = common_work(ph)
            WA = sb(ph, 'WA', [128, 8, 1304], BF16)
            with ExitStack() as ld:
                stg = [sb(ld, 'stgA0', [128, 8, 512]), sb(ld, 'stgA1', [128, 8, 512])]
                segs = [(0, 512, 0), (768, 896, 512), (1024, 1152, 640), (512, 768, 768), (896, 1024, 1024), (1152, 1280, 1152), (1280, 1304, 1280)]
                for i, (s0, s1, d0) in enumerate(segs):
                    load_w(WA[:, :, d0:d0 + (s1 - s0)], I['win'][:, s0:s1], stg[i % 2], 8, s1 - s0)
                kk.dma(W['gain'][:], dram_bcast(I['norm_mix'][0], 128, D))
                S.barrier(); S.flush()
            PA = []
            for par in range(2):
                d = {}
                d['proj'] = sb(ph, 'projA%d' % par, [128, 1304])
                d['rot'] = sb(ph, 'rot%d' % par, [128, 768])
                d['rtab'] = sb(ph, 'rtab%d' % par, [128, 64])
                for nm in ('r1', 'r2', 'r3', 'r4'):
                    d[nm] = sb(ph, nm + '_%d' % par, [128, 12, 32])
                d['tb'] = sb(ph, 'tb%d' % par, [128, 12, 128], BF16)
                if par == 0:
                    d['xn'], d['xnT'], d['junk'] = W['xn'], W['xnT'], W['junk']
                else:
                    d['xn'] = sb(ph, 'xnA%d' % par, [128, D], BF16)
                    d['xnT'] = sb(ph, 'xnTA%d' % par, [128, 8, 128], BF16)
                    d['junk'] = sb(ph, 'junkA%d' % par, [128, D], BF16)
                d['ssq'] = sb(ph, 'ssqA%d' % par, [128, 1])
                d['rstd'] = sb(ph, 'rstdA%d' % par, [128, 1])
                PA.append(d)

            def bodyA(i):
                (ti, n, pos0, row0, sb_) = tiles[i]
                P = PA[i % 2]
                pa, pb, pc, pd = [W['B'][4 * (i % 2) + k_] for k_ in range(4)]
                proj, rot, rtab, r1, r2, r3, r4, tb = P['proj'], P['rot'], P['rtab'], P['r1'], P['r2'], P['r3'], P['r4'], P['tb']
                xt = W['xt'][i % 2]
                src = I['xp'][row0:row0 + n, :] if sb_ is None else I['xs'][sb_:sb_ + 1, :]
                kk.dma(xt[:n], src)
                kk.dma(rtab[:n], I['ropetab'][row0:row0 + n, :])
                rstd_of(P['junk'], xt, n, D, P['ssq'], P['rstd'])
                yield
                kk.stt(P['xn'][:n], xt[:n], P['rstd'][:n, 0:1], W['gain'][:n], ALU.mult, ALU.mult)
                yield
                psT = bfv(pa, 8, 128)
                for kc in range(8):
                    kk.tr(psT[:, kc, :n], P['xn'][:n, kc * 128:(kc + 1) * 128], ident_bf[:n, :n])
                yield
                kk.cp(P['xnT'][:, :, :n], psT[:, :, :n], eng='act')
                yield
                xnT = P['xnT']
                blks = [(0, 512, pb), (512, 1024, pc), (1024, 1304, pd)]
                for (c0, c1, pbk) in blks:
                    for kc in range(8):
                        kk.mm(pbk[:n, 0:c1 - c0], xnT[:, kc, :n], WA[:, kc, c0:c1], start=(kc == 0), stop=(kc == 7))
                yield
                kk.cp(proj[:n, 0:512], pb[:n, 0:512], eng='dve')
                kk.cp(proj[:n, 512:1024], pc[:n, 0:512], eng='act')
                kk.cp(proj[:n, 1024:1304], pd[:n, 0:280], eng='act')
                yield
                pv = proj[:n, 0:768].rearrange('p (h t d) -> p h t d', h=12, t=2, d=32)
                rv = rot[:n, :].rearrange('p (h t d) -> p h t d', h=12, t=2, d=32)
                cosb = bc(rtab[:n, 0:32].rearrange('p (o d) -> p o d', o=1), [n, 12, 32])
                sinb = bc(rtab[:n, 32:64].rearrange('p (o d) -> p o d', o=1), [n, 12, 32])
                kk.tt(r1[:n], pv[:, :, 0, :], cosb, ALU.mult)
                kk.tt(r3[:n], pv[:, :, 1, :], cosb, ALU.mult)
                kk.tt(r2[:n], pv[:, :, 1, :], sinb, ALU.mult, eng='pool')
                kk.tt(r4[:n], pv[:, :, 0, :], sinb, ALU.mult, eng='pool')
                gdst = G[:n, ti, :] if sb_ is None else Gs[0:1, sb_, :]
                kk.act(gdst, proj[:n, 1280:1304], AF.Exp, scale=-1.0)
                kk.ts(gdst, gdst, 1.0, None, ALU.add)
                kk.recip(gdst, gdst)
                if sb_ is None:
                    kk.cp(VS[:n, ti, 0, :], proj[:n, 1024:1152], eng='act')
                    kk.cp(VS[:n, ti, 1, :], proj[:n, 1152:1280], eng='act')
                else:
                    kk.cp(VSs[0:1, sb_, 0, :], proj[:n, 1024:1152], eng='act')
                    kk.cp(VSs[0:1, sb_, 1, :], proj[:n, 1152:1280], eng='act')
                kk.cp(tb[:n, 0:8, 0:64], proj[:n, 0:512].rearrange('p (h d) -> p h d', h=8), eng='act')
                kk.cp(tb[:n, 8:10, 0:64], proj[:n, 768:896].rearrange('p (h d) -> p h d', h=2), eng='act')
                kk.cp(tb[:n, 10:12, 0:64], proj[:n, 896:1024].rearrange('p (h d) -> p h d', h=2), eng='act')
                yield
                kk.tt(rv[:, :, 0, :], r1[:n], r2[:n], ALU.subtract)
                kk.tt(rv[:, :, 1, :], r3[:n], r4[:n], ALU.add)
                kk.cp(tb[:n, 0:8, 64:128], rot[:n, 0:512].rearrange('p (h d) -> p h d', h=8), eng='dve')
                kk.cp(tb[:n, 8:10, 64:128], rot[:n, 512:640].rearrange('p (h d) -> p h d', h=2), eng='dve')
                kk.cp(tb[:n, 10:12, 64:128], rot[:n, 640:768].rearrange('p (h d) -> p h d', h=2), eng='dve')
                yield
                if sb_ is None:
                    kk.dma(O['kvc_p'][row0:row0 + n, :], proj[:n, 768:1024], q='pool')
                    kk.dma(O['kvs_p'][row0:row0 + n, 0:128], rot[:n, 512:640], q='pool')
                    kk.dma(O['kvs_p'][row0:row0 + n, 128:256], proj[:n, 1024:1152], q='pool')
                    if ti >= NT - 4:
                        r = row0 - (T - 512)
                        kk.dma(O['kvw_p'][r:r + n, 0:128], rot[:n, 640:768], q='pool')
                        kk.dma(O['kvw_p'][r:r + n, 128:256], proj[:n, 1152:1280], q='pool')
                else:
                    kk.dma(O['kvc_s'][sb_:sb_ + 1, :], proj[:n, 768:1024], q='pool')
                    kk.dma(O['kvs_s'][sb_:sb_ + 1, 0:128], rot[:n, 512:640], q='pool')
                    kk.dma(O['kvs_s'][sb_:sb_ + 1, 128:256], proj[:n, 1024:1152], q='pool')
                    kk.dma(O['kvw_s'][sb_, 511:512, 0:128], rot[:n, 640:768], q='pool')
                    kk.dma(O['kvw_s'][sb_, 511:512, 128:256], proj[:n, 1152:1280], q='pool')
                    kk.dma(O['kvw_s'][sb_, 0:511, :], I['cwin'][sb_, 1:512, :], q='pool')
                psA = bfv(pa, 8, 128)
                psB = bfv(pd, 8, 128)
                for j in range(8):
                    kk.tr(psA[:, j, :n], tb[:n, j, :], ident_bf[:n, :n])
                for j in range(4):
                    kk.tr(psB[:, j, :n], tb[:n, 8 + j, :], ident_bf[:n, :n])
                yield
                if sb_ is None:
                    kk.cp(QT[:, :, row0:row0 + n], psA[:, :, :n], eng='act')
                    kk.cp(KA[:, :, row0:row0 + n], psB[:, 0:2, :n], eng='dve')
                    kk.cp(KB[:, :, row0:row0 + n], psB[:, 2:4, :n], eng='dve')
                else:
                    kk.cp(QTs[:, :, sb_:sb_ + 1], psA[:, :, :n], eng='act')
                    kk.cp(KAs[:, :, sb_:sb_ + 1], psB[:, 0:2, :n], eng='dve')
                    kk.cp(KBs[:, :, sb_:sb_ + 1], psB[:, 2:4, :n], eng='dve')
                yield

            HALF_A = 5
            cur = bodyA(0)
            for _ in range(HALF_A):
                next(cur)
            for i in range(1, len(tiles)):
                nxt = bodyA(i)
                done = False
                while not done:
                    try:
                        next(cur)
                    except StopIteration:
                        done = True
                    if not done:
                        try:
                            next(nxt)
                        except StopIteration:
                            pass
                cur = nxt
            for _ in cur:
                pass
            S.barrier(); S.flush()

        with ExitStack() as ph:
            W = common_work(ph)
            WB = sb(ph, 'WB', [128, 8, 1544], BF16)
            cw_f = sb(ph, 'cw_f', [128, 8, 4])
            cbias_f = sb(ph, 'cbias_f', [128, 8])
            dtb_b = sb(ph, 'dtb_b', [128, 8])
            Aneg_b = sb(ph, 'Aneg_b', [128, 8])
            D_b = sb(ph, 'D_b', [128, 8])
            snorm_b = sb(ph, 'snorm_b', [128, 512])
            with ExitStack() as ld:
                stg = [sb(ld, 'stgB0', [128, 8, 512]), sb(ld, 'stgB1', [128, 8, 512])]
                segs = [(1304, 1816, 0), (2840, 2848, 512), (1816, 2328, 520), (2328, 2840, 1032)]
                for i, (s0, s1, d0) in enumerate(segs):
                    load_w(WB[:, :, d0:d0 + (s1 - s0)], I['win'][:, s0:s1], stg[i % 2], 8, s1 - s0)
                kk.dma(W['gain'][:], dram_bcast(I['norm_mix'][0], 128, D))
                for k_ in range(4):
                    kk.dma(cw_f[:, :, k_], I['cw'][k_].rearrange('(c p) -> p c', p=128), slow=True)
                kk.dma(cbias_f[:], I['cb'].rearrange('(c p) -> p c', p=128), slow=True)
                kk.dma(dtb_b[:], dram_bcast(I['dtb'], 128, 8))
                kk.dma(Aneg_b[:], dram_bcast(I['alog'], 128, 8))
                kk.dma(D_b[:], dram_bcast(I['dsk'], 128, 8))
                kk.dma(snorm_b[:], dram_bcast(I['snorm'], 128, 512))
                kk.act(Aneg_b[:], Aneg_b[:], AF.Exp)
                kk.ts(Aneg_b[:], Aneg_b[:], -1.0, None, ALU.mult)
                S.barrier(); S.flush()
            hT = sb(ph, 'hT', [128, 512])
            hT_bf = sb(ph, 'hT_bf', [128, 512], BF16)
            hio = sb(ph, 'hio', [128, 4, 128])
            PB = []
            for par in range(2):
                d = {}
                d['proj'] = sb(ph, 'projB%d' % par, [128, 520])
                d['cbuf'] = sb(ph, 'cbuf%d' % par, [128, 8, 131])
                d['c3'] = sb(ph, 'c3%d' % par, [128, 8, 3])
                d['ca'] = sb(ph, 'ca%d' % par, [128, 8, 128])
                d['cb2'] = sb(ph, 'cb2%d' % par, [128, 8, 128])
                d['xbcT'] = sb(ph, 'xbcT%d' % par, [128, 8, 128], BF16)
                d['xsB'] = sb(ph, 'xsB%d' % par, [128, 768], BF16)
                for nm in ('dt_t', 'a_t', 'e_t', 'd_t', 'gdec'):
                    d[nm] = sb(ph, nm + str(par), [128, 8])
                d['xdt'] = sb(ph, 'xdt%d' % par, [128, 512], BF16)
                d['xdtd'] = sb(ph, 'xdtd%d' % par, [128, 512], BF16)
                d['GTm'] = sb(ph, 'GTm%d' % par, [128, 2, 128])
                d['aL'] = sb(ph, 'aL%d' % par, [128, 8, 128])
                d['LT'] = sb(ph, 'LT%d' % par, [128, 8, 128], BF16)
                d['WT'] = sb(ph, 'WT%d' % par, [128, 8, 128], BF16)
                d['yt'] = sb(ph, 'yt%d' % par, [128, 512])
                if par == 0:
                    d['xn'], d['xnT'], d['junk'] = W['xn'], W['xnT'], W['junk']
                else:
                    d['xn'] = sb(ph, 'xnp%d' % par, [128, D], BF16)
                    d['xnT'] = sb(ph, 'xnTp%d' % par, [128, 8, 128], BF16)
                    d['junk'] = sb(ph, 'junkp%d' % par, [128, D], BF16)
                d['ssq'] = sb(ph, 'ssqp%d' % par, [128, 1])
                d['rstd'] = sb(ph, 'rstdp%d' % par, [128, 1])
                d['gain'] = W['gain']
                d['B'] = W['B'][4 * par:4 * par + 4] + W['B'][4 * par:4 * par + 4]
                PB.append(d)
            BALL = W['B']

            def state_out(dst, pd):
                for c in range(4):
                    kk.tr(pd[:, c * 128:(c + 1) * 128], hT[:, c * 128:(c + 1) * 128], ident_f)
                kk.cp(hio[:].rearrange('p c n -> p (c n)'), pd[:, :])
                kk.dma(dst.rearrange('(c p) n -> p c n', p=128), hio[:], q='pool')

            def body(i):
                (ti, n, pos0, row0, sb_) = tiles[i]
                P = PB[i % 2]
                pa, pb, pc, pd = BALL[4 * (i % 2)], BALL[4 * (i % 2) + 1], BALL[4 * (i % 2) + 2], BALL[4 * (i % 2) + 3]
                proj, cbuf, c3, ca, cb2, xbcT, xsB = P['proj'], P['cbuf'], P['c3'], P['ca'], P['cb2'], P['xbcT'], P['xsB']
                dt_t, a_t, e_t, d_t, gdec = P['dt_t'], P['a_t'], P['e_t'], P['d_t'], P['gdec']
                xdt, xdtd, GTm, aL, LT, WT, yt = P['xdt'], P['xdtd'], P['GTm'], P['aL'], P['LT'], P['WT'], P['yt']
                y2 = cb2[:, 0:4, :].rearrange('p a b -> p (a b)')
                zs = ca[:, 0:4, :].rearrange('p a b -> p (a b)')
                xt = W['xt'][i % 2]
                src = I['xp'][row0:row0 + n, :] if sb_ is None else I['xs'][sb_:sb_ + 1, :]
                kk.dma(xt[:n], src)
                if ti == 0:
                    kk.memset(cbuf[:, :, 0:3], 0.0)
                    kk.memset(hT[:], 0.0)
                    kk.memset(hT_bf[:], 0.0)
                if sb_ is not None:
                    for j_ in range(3):
                        kk.dma(cbuf[:, :, j_], I['sconv'][sb_, j_].rearrange('(c p) -> p c', p=128), slow=True)
                rstd_of(P['junk'], xt, n, D, P['ssq'], P['rstd'])
                yield
                kk.stt(P['xn'][:n], xt[:n], P['rstd'][:n, 0:1], P['gain'][:n], ALU.mult, ALU.mult)
                yield
                psT = bfv(pa, 8, 128)
                for kc in range(8):
                    kk.tr(psT[:, kc, :n], P['xn'][:n, kc * 128:(kc + 1) * 128], ident_bf[:n, :n])
                yield
                kk.cp(P['xnT'][:, :, :n], psT[:, :, :n], eng='act')
                yield
                xnT = P['xnT']
                for fc in range(8):
                    pbk = pc if fc < 4 else pd
                    for kc in range(8):
                        kk.mm(pbk[:, (fc % 4) * 128:(fc % 4) * 128 + n], WB[:, kc, 520 + fc * 128:520 + (fc + 1) * 128], xnT[:, kc, :n], start=(kc == 0), stop=(kc == 7))
                for kc in range(8):
                    kk.mm(pb[:n, 0:8], xnT[:, kc, :n], WB[:, kc, 512:520], start=(kc == 0), stop=(kc == 7))
                for kc in range(8):
                    kk.mm(pa[:n, 0:512], xnT[:, kc, :n], WB[:, kc, 0:512], start=(kc == 0), stop=(kc == 7))
                yield
                kk.cp(cbuf[:, 0:4, 3:3 + n], pc[:, :].rearrange('p (c t) -> p c t', c=4)[:, :, :n], eng='dve')
                kk.cp(cbuf[:, 4:8, 3:3 + n], pd[:, :].rearrange('p (c t) -> p c t', c=4)[:, :, :n], eng='act')
                kk.tt(dt_t[:n], pb[:n, 0:8], dtb_b[:n], ALU.add)
                kk.cp(proj[:n, 0:512], pa[:n, 0:512], eng='act')
                yield
                kk.act(dt_t[:n], dt_t[:n], AF.Exp)
                kk.act(dt_t[:n], dt_t[:n], AF.Ln, bias=1.0)
                kk.tt(cb2[:, :, :n], cbuf[:, :, 1:1 + n], bc(cw_f[:, :, 1:2], [128, 8, n]), ALU.mult, eng='pool')
                kk.tt(ca[:, :, :n], cbuf[:, :, 0:n], bc(cw_f[:, :, 0:1], [128, 8, n]), ALU.mult)
                yield
                if sb_ is not None or ti == NT - 1:
                    kk.cp(c3[:], cbuf[:, :, n:n + 3], eng='pool')
                    dst = O['sconv_p'] if sb_ is None else O['sconv_s'][sb_]
                    for j_ in range(3):
                        kk.dma(dst[j_].rearrange('(c p) -> p c', p=128), c3[:, :, j_], q='pool', slow=True)
                if sb_ is None and ti < NT - 1:
                    kk.cp(PB[(i + 1) % 2]['cbuf'][:, :, 0:3], cbuf[:, :, n:n + 3], eng='pool')
                kk.tt(a_t[:n], dt_t[:n], Aneg_b[:n], ALU.mult)
                kk.tt(ca[:, :, :n], ca[:, :, :n], cb2[:, :, :n], ALU.add)
                kk.tt(cb2[:, :, :n], cbuf[:, :, 2:2 + n], bc(cw_f[:, :, 2:3], [128, 8, n]), ALU.mult, eng='pool')
                yield
                kk.mm(pb[:n, 8:16], Utri[:n, :n], a_t[:n, :])
                kk.mm(pb[:, 16:24], ones_f[:n, :], a_t[:n, :])
                kk.tt(aL[:n, :, :n], bc(Lstrict[:n, :n].rearrange('p (o l) -> p o l', o=1), [n, 8, n]), bc(a_t[:n].rearrange('p (h o) -> p h o', o=1), [n, 8, n]), ALU.mult)
                kk.tt(ca[:, :, :n], ca[:, :, :n], cb2[:, :, :n], ALU.add)
                kk.tt(cb2[:, :, :n], cbuf[:, :, 3:3 + n], bc(cw_f[:, :, 3:4], [128, 8, n]), ALU.mult, eng='pool')
                yield
                kk.act(e_t[:n], pb[:n, 8:16], AF.Exp)
                kk.act(gdec[:], pb[:, 16:24], AF.Exp)
                kk.cp(d_t[:n], pb[:n, 8:16])
                kk.tt(d_t[:n], pb[:n, 16:24], d_t[:n], ALU.subtract)
                for h in range(8):
                    pbk = pc if h < 4 else pd
                    kk.mm(pbk[:n, (h % 4) * 128:(h % 4) * 128 + n], aL[:n, h, :n], Utri[:n, :n])
                kk.tt(ca[:, :, :n], ca[:, :, :n], cb2[:, :, :n], ALU.add)
                yield
                kk.act(d_t[:n], d_t[:n], AF.Exp)
                kk.act(LT[:n, 0:4, :n], pc[:n, :].rearrange('p (c t) -> p c t', c=4)[:, :, :n], AF.Exp)
                kk.act(LT[:n, 4:8, :n], pd[:n, :].rearrange('p (c t) -> p c t', c=4)[:, :, :n], AF.Exp)
                for fc in range(8):
                    kk.act(xbcT[:, fc, :n], ca[:, fc, :n], AF.Silu, bias=cbias_f[:, fc:fc + 1])
                kk.act(zs[:n], proj[:n, 0:512], AF.Silu)
                yield
                psT2 = bfv(pa, 8, 128)
                for fc in range(6):
                    kk.tr(psT2[:n, fc, :], xbcT[:, fc, :n], ident_bf[:, :])
                for g in range(2):
                    kk.mm(pb[:n, 128 + g * 128:128 + g * 128 + n], xbcT[:, 4 + g, :n], xbcT[:, 6 + g, :n])
                yield
                kk.cp(xsB[:n, :], psT2[:n, 0:6, :].rearrange('p c f -> p (c f)'), eng='act')
                kk.tt(GTm[:n, :, :n], pb[:n, 128:384].rearrange('p (g l) -> p g l', g=2)[:, :, :n], bc(Utri[:n, :n].rearrange('p (o l) -> p o l', o=1), [n, 2, n]), ALU.mult)
                for g in range(2):
                    kk.tt(WT[:n, g * 4:(g + 1) * 4, :n], LT[:n, g * 4:(g + 1) * 4, :n], bc(GTm[:n, g:g + 1, :n], [n, 4, n]), ALU.mult)
                yield
                xs_v = xsB[:n, 0:512].rearrange('p (h d) -> p h d', h=8)
                kk.tt(xdt[:n].rearrange('p (h d) -> p h d', h=8), xs_v, bc(dt_t[:n].rearrange('p (h o) -> p h o', o=1), [n, 8, 64]), ALU.mult)
                kk.tt(xdtd[:n].rearrange('p (h d) -> p h d', h=8), xdt[:n].rearrange('p (h d) -> p h d', h=8), bc(d_t[:n].rearrange('p (h o) -> p h o', o=1), [n, 8, 64]), ALU.mult)
                kk.tt(y2[:n].rearrange('p (h d) -> p h d', h=8), xs_v, bc(D_b[:n].rearrange('p (h o) -> p h o', o=1), [n, 8, 64]), ALU.mult, eng='pool')
                yield
                if sb_ is not None:
                    kk.dma(hio[:], I['sssm'][sb_].rearrange('(c p) n -> p c n', p=128))
                    for c in range(4):
                        kk.tr(pd[:, c * 128:(c + 1) * 128], hio[:, c, :], ident_f)
                    kk.cp(hT[:], pd[:, :])
                    kk.cp(hT_bf[:], pd[:, :], eng='act')
                    yield
                for h in range(8):
                    kk.mm(pa[:n, h * 64:(h + 1) * 64], WT[:n, h, :n], xdt[:n, h * 64:(h + 1) * 64])
                for g in range(2):
                    kk.mm(pc[:n, g * 256:(g + 1) * 256], xbcT[:, 6 + g, :n], hT_bf[:, g * 256:(g + 1) * 256])
                for g in range(2):
                    kk.mm(pd[:, g * 256:(g + 1) * 256], xsB[:n, 512 + g * 128:512 + (g + 1) * 128], xdtd[:n, g * 256:(g + 1) * 256])
                yield
                kk.tt(yt[:n].rearrange('p (h d) -> p h d', h=8), pc[:n, :].rearrange('p (h d) -> p h d', h=8), bc(e_t[:n].rearrange('p (h o) -> p h o', o=1), [n, 8, 64]), ALU.mult)
                kk.tt(yt[:n], yt[:n], pa[:n, :], ALU.add)
                kk.tt(hT[:].rearrange('p (h d) -> p h d', h=8), hT[:].rearrange('p (h d) -> p h d', h=8), bc(gdec[:].rearrange('p (h o) -> p h o', o=1), [128, 8, 64]), ALU.mult)
                kk.tt(hT[:], hT[:], pd[:, :], ALU.add)
                kk.tt(yt[:n], yt[:n], y2[:n], ALU.add)
                kk.tt(yt[:n], yt[:n], zs[:n], ALU.mult)
                yield
                kk.cp(hT_bf[:], hT[:], eng='act')
                rstd_of(P['junk'], yt, n, 512, P['ssq'], P['rstd'])
                if sb_ is not None:
                    state_out(O['ssm_s'][sb_], pd)
                elif ti == NT - 1:
                    state_out(O['ssm_p'], pd)
                yield
                ycd = YC[:n, ti, :] if sb_ is None else YCs[0:1, sb_, :]
                kk.stt(ycd, yt[:n], P['rstd'][:n, 0:1], snorm_b[:n], ALU.mult, ALU.mult)
                yield

            HALF = 9
            cur = body(0)
            for _ in range(HALF):
                next(cur)
            for i in range(1, len(tiles)):
                nxt = body(i)
                rr(cur, nxt) if False else None
                done = False
                while not done:
                    try:
                        next(cur)
                    except StopIteration:
                        done = True
                    if not done:
                        try:
                            next(nxt)
                        except StopIteration:
                            pass
                cur = nxt
            for _ in cur:
                pass
            S.barrier(); S.flush()

        with ExitStack() as ph:
            W = common_work(ph)
            B = W['B']
            kcT = sb(ph, 'kcT', [64, 2, 128], BF16)
            vcO = sb(ph, 'vcO', [128, 2, 96], BF16)
            Wout = sb(ph, 'Wout', [128, 8, D], BF16)
            with ExitStack() as ld:
                stg = [sb(ld, 'stgC0', [128, 8, 512]), sb(ld, 'stgC1', [128, 8, 512])]
                W1 = [sb(ld, 'W1k', [64, 32, 256], BF16), sb(ld, 'W1v', [64, 32, 256], BF16)]
                W2 = [sb(ld, 'W2k', [128, 2, 64], BF16), sb(ld, 'W2v', [128, 2, 64], BF16)]
                peT = sb(ld, 'peT', [64, 2, 32])
                peTb = sb(ld, 'peTb', [64, 2, 32], BF16)
                biasv = sb(ld, 'biasv', [128, 2, 2])
                Hs = sb(ld, 'Hs', [128, 2, 2, 2, 128], BF16)
                ovl_t = sb(ld, 'ovl_t', [128, 32])
                for kv, (w1n, w2n, pen) in enumerate([('w1k', 'w2k', 'pek'), ('w1v', 'w2v', 'pev')]):
                    for hh in range(2):
                        sv = stg[hh].rearrange('p a (b c) -> p (a b) c', b=2)
                        kk.dma(sv[0:64, :, :], I[w1n][hh * 1024:(hh + 1) * 1024, :].rearrange('(hs d) f -> d hs f', d=64))
                        kk.cpr(W1[kv][:, hh * 16:(hh + 1) * 16, :], sv[0:64, :, :])
                    kk.dma(stg[0][:, 0:2, 0:64], I[w2n].rearrange('(k p) w -> p k w', p=128))
                    kk.cpr(W2[kv][:], stg[0][:, 0:2, 0:64])
                    for hs in range(32):
                        kk.dma(peT[:, kv, hs:hs + 1], I[pen][hs].rearrange('(d o) -> d o', o=1), q='pool', slow=True)
                kk.cp(peTb[:], peT[:])
                kk.dma(ovl_t[:], I['ovl'])
                for c0 in range(0, D, 512):
                    load_w(Wout[:, :, c0:c0 + 512], I['wout'][:, c0:c0 + 512], stg[(c0 // 512) % 2], 8, 512)
                kk.memset(Hs[:], 0.0)
                for kv in range(2):
                    HT = KA if kv == 0 else KB
                    for fc in range(2):
                        for hs in range(32):
                            kk.mm(B[0][:, 0:1], W1[kv][:, hs, fc * 128:(fc + 1) * 128], peTb[:, kv, hs:hs + 1], start=(hs == 0), stop=(hs == 31))
                        kk.cp(biasv[:, kv, fc:fc + 1], B[0][:, 0:1])
                        for g in range(2):
                            pb = B[1 + (g % 2)]
                            for hs in range(32):
                                kk.mm(pb[:, 0:127], W1[kv][:, hs, fc * 128:(fc + 1) * 128], HT[0:64, g, hs:hs + 16 * 126 + 1:16], start=(hs == 0), stop=(hs == 31))
                            kk.act(Hs[:, kv, g, fc, 0:127], pb[:, 0:127], AF.Silu, bias=biasv[:, kv, fc:fc + 1])
                kk.memset(vcO[:], 0.0)
                kk.memset(kcT[:], 0.0)
                for g in range(2):
                    for fc in range(2):
                        kk.mm(B[3][0:64, 0:127], W2[0][:, fc, :], Hs[:, 0, g, fc, 0:127], start=(fc == 0), stop=(fc == 1))
                    kk.cp(kcT[:, g, 0:127], B[3][0:64, 0:127])
                    for fc in range(2):
                        kk.mm(B[4][0:127, 0:64], Hs[:, 1, g, fc, 0:127], W2[1][:, fc, :], start=(fc == 0), stop=(fc == 1))
                    kk.cp(vcO[0:127, g, 0:64], B[4][0:127, 0:64])
                    kk.cp(vcO[0:127, g, 64:96], ovl_t[0:127, :])
                S.barrier(); S.flush()

            Mc = sb(ph, 'Mc', [128, 128])
            Vm = sb(ph, 'Vm', [128, 32])
            Bm = sb(ph, 'Bm', [128, 32])
            nB = sb(ph, 'nB', [128, 32])
            Vm1 = sb(ph, 'Vm1', [128, 32])
            impacc = sb(ph, 'impacc', [128, 2, 32])
            imp4 = sb(ph, 'imp4', [128, 4, 32])
            imp2 = sb(ph, 'imp2', [128, 32])
            imp3 = sb(ph, 'imp3', [128, 32])
            m8 = sb(ph, 'm8', [128, 8])
            sel = sb(ph, 'sel', [128, 32])
            negsel = sb(ph, 'negsel', [128, 2, 32])
            Mfull = sb(ph, 'Mfull', [128, 2, T], BF16)
            Mwin = sb(ph, 'Mwin', [128, 640], BF16)
            mxa = [sb(ph, 'mxa0', [128, 4]), sb(ph, 'mxa1', [128, 4])]
            scs = [sb(ph, 'sc0', [128, T]), sb(ph, 'sc1', [128, T])]
            pbfs = [sb(ph, 'pbf0', [128, T], BF16), sb(ph, 'pbf1', [128, T], BF16)]
            pTs = [sb(ph, 'pT0', [128, 16, 128], BF16), sb(ph, 'pT1', [128, 16, 128], BF16)]
            mxs = [sb(ph, 'mx0', [128, 1]), sb(ph, 'mx1', [128, 1])]
            negms = [sb(ph, 'negm0', [128, 1]), sb(ph, 'negm1', [128, 1])]
            rss = [sb(ph, 'rs0', [128, 1]), sb(ph, 'rs1', [128, 1]), sb(ph, 'rs2', [128, 1])]
            coef = sb(ph, 'coef', [128, 1])
            coef4 = sb(ph, 'coef4', [128, 4])
            accs = [sb(ph, 'acc0', [128, 8, 64]), sb(ph, 'acc1', [128, 8, 64])]
            catb = sb(ph, 'catb', [128, D], BF16)
            catT = sb(ph, 'catT', [128, 8, 128], BF16)
            xo = sb(ph, 'xo', [128, D])
            csc = sb(ph, 'csc', [128, 4, 128])
            cpbf = sb(ph, 'cpbf', [128, 4, 128], BF16)
            cpT = sb(ph, 'cpT', [128, 4, 128], BF16)
            cmx = sb(ph, 'cmx', [128, 4])
            cnegm = sb(ph, 'cnegm', [128, 4])
            crs = sb(ph, 'crs', [128, 4])
            qk_i = [0]
            tr_i = [0]
            causal_bf = sb(ph, 'causal_bf', [128, 128], BF16)
            kk.cp(causal_bf[:], causal, eng='act')
            kk.memset(Mwin[:], 0.0)
            kk.cp(Mwin[:, 0:128], wlo, eng='pool')
            kk.cp(Mwin[:, 512:640], causal, eng='pool')
            kk.memset(negsel[:], 0.0)
            kk.memset(pbfs[0][:], 0.0)
            kk.memset(pbfs[1][:], 0.0)
            n = 128

            def pre_stages(t, g):
                pos0 = t * 128
                tok = slice(pos0, pos0 + n)
                acc = accs[t % 2]
                nk = (t + 1) * 128

                def p0():
                    if g == 0:
                        kk.dma(W['xt'][t % 2][:n], I['xp'][pos0:pos0 + n, :])
                        kk.memset(Mc[:], NEG)
                        kk.memset(Mc[:, 0:127], 0.0)
                        S.add('pool', lambda e: e.affine_select(out=Mc[:, 0:127], in_=Mc[:, 0:127], pattern=[[-16, 127]], compare_op=ALU.is_ge, fill=NEG, base=pos0 - 31, channel_multiplier=1), reads=[Mc[:, 0:127]], writes=[Mc[:, 0:127]])
                        if t >= 8:
                            kk.memset(Vm[:], 1.0)
                            S.add('pool', lambda e: e.affine_select(out=Vm[:], in_=Vm[:], pattern=[[-64, 32]], compare_op=ALU.is_ge, fill=0.0, base=pos0, channel_multiplier=1), reads=[Vm[:]], writes=[Vm[:]])
                            S.add('pool', lambda e: e.affine_select(out=Bm[:], in_=Vm[:], pattern=[[64, 32]], compare_op=ALU.is_ge, fill=0.0, base=127 - pos0, channel_multiplier=-1), reads=[Vm[:]], writes=[Bm[:]])
                            kk.memset(Bm[:, 0:1], 1.0)
                            kk.ts(nB[:], Bm[:], -1.0, 1.0, ALU.mult, ALU.add, eng='pool')
                            kk.ts(Vm1[:], Vm[:], -1.0, 1e9, ALU.add, ALU.mult, eng='pool')
                    for r in range(4):
                        kk.mm(B[0][:n, r * 128:(r + 1) * 128], QT[0:64, g * 4 + r, tok], kcT[:, g, 0:128])

                def p1():
                    kk.stt(csc[:n], B[0][:n, :].rearrange('p (h k) -> p h k', h=4), 0.125, bc(Mc[:n, :].rearrange('p (o k) -> p o k', o=1), [n, 4, 128]), ALU.mult, ALU.add)
                    S.add('dve', lambda e: e.tensor_reduce(out=cmx[:n], in_=csc[:n], axis=AX.X, op=ALU.max), reads=[csc[:n]], writes=[cmx[:n]])
                    kk.ts(cnegm[:n], cmx[:n], -30000.0, -1.0, ALU.max, ALU.mult)

                def p2():
                    for r in range(4):
                        kk.act(cpbf[:n, r, :], csc[:n, r, :], AF.Exp, bias=cnegm[:n, r:r + 1], accum=crs[:n, r:r + 1])

                def p3():
                    pst = bfv(B[0], 8, 128)
                    for r in range(4):
                        kk.tr(pst[:, r, :n], cpbf[:n, r, :], ident_bf[:n, :n])
                    kk.cp(cpT[:, :, :n], pst[:, 0:4, :n], eng='act')

                def p4():
                    for r in range(4):
                        kk.mm(B[0][:n, r * 96:(r + 1) * 96], cpT[0:127, r, :n], vcO[0:127, g, :])
                    kk.ts(crs[:n], crs[:n], 1e-30, None, ALU.max)
                    kk.recip(crs[:n], crs[:n])
                    gv = G[:n, t, :].rearrange('p (h r) -> p h r', r=3)
                    kk.tt(coef4[:n].rearrange('p (h o) -> p h o', o=1), crs[:n].rearrange('p (h o) -> p h o', o=1), gv[:, g * 4:(g + 1) * 4, 0:1], ALU.mult)
                    ov = B[0][:n, 0:384].rearrange('p (h c) -> p h c', h=4)
                    kk.tt(acc[:n, g * 4:(g + 1) * 4, :], ov[:, :, 0:64], bc(coef4[:n].rearrange('p (h o) -> p h o', o=1), [n, 4, 64]), ALU.mult)
                    kk.tt(imp4[:n], ov[:, :, 64:96], bc(crs[:n].rearrange('p (h o) -> p h o', o=1), [n, 4, 32]), ALU.mult)
                    S.add('dve', lambda e: e.tensor_reduce(out=impacc[:n, g, :], in_=imp4[:n].rearrange('p h j -> p j h'), axis=AX.X, op=ALU.add), reads=[imp4[:n]], writes=[impacc[:n, g, :]])

                def p5():
                    if t >= 8:
                        kk.tt(imp2[:n], impacc[:n, g, :], nB[:n], ALU.mult)
                        kk.stt(imp2[:n], Bm[:n], 1e9, imp2[:n], ALU.mult, ALU.add)
                        kk.tt(imp3[:n], imp2[:n], Vm[:n], ALU.mult)
                        kk.tt(imp3[:n], imp3[:n], Vm1[:n], ALU.add)
                        S.add('dve', lambda e: e.max(out=m8[:], in_=imp3[:]), reads=[imp3[:]], writes=[m8[:]])
                        S.add('dve', lambda e: e.match_replace(out=imp2[:], in_to_replace=m8[:], in_values=imp3[:], imm_value=-3e9), reads=[m8[:], imp3[:]], writes=[imp2[:]])
                        S.add('dve', lambda e: e.max(out=m8[:], in_=imp2[:]), reads=[imp2[:]], writes=[m8[:]])
                        kk.ts(sel[:n], imp3[:n], m8[:n, 7:8], None, ALU.is_ge)
                        kk.ts(negsel[:n, g, :], sel[:n], -1.0, 1e30, ALU.add, ALU.mult)

                def p6():
                    if t >= 8:
                        nblk = 2 * (t + 1)
                        S.add('act', lambda e: e.activation(out=Mfull[:n, g, 0:nk].rearrange('p (b k) -> p b k', k=64), in_=bc(negsel[:n, g, 0:nblk].rearrange('p (b o) -> p b o', o=1), [n, nblk, 64]), func=AF.Copy),
                              reads=[negsel[:n, g, 0:nblk]], writes=[Mfull[:n, g, 0:nk]])

                return [p0, p1, p2, p3, p4, p5, p6]

            def make_task(t, g, hd, br, k):
                pos0 = t * 128
                tok = slice(pos0, pos0 + n)
                acc = accs[t % 2]
                pp = k % 2
                rs = rss[k % 3]
                sc, pbf, pT, mx, negm = scs[pp], pbfs[pp], pTs[pp], mxs[pp], negms[pp]
                if br == 1:
                    jl = list(range(t + 1))
                    KT, kbase, Msk, mbase, vsel = KA, 0, Mfull[:, g, :], 0, 0
                else:
                    j0w = max(0, t - 4)
                    jl = list(range(j0w, t + 1))
                    KT, kbase, Msk, vsel = KB, j0w * 128, Mwin, 1
                    mbase = 640 - len(jl) * 128
                nkk = len(jl) * 128

                def A1():
                    ma = mxa[pp]
                    nbk = 0
                    for kb in range(0, nkk, 512):
                        w = min(512, nkk - kb)
                        qk_i[0] += 1
                        pb = B[1 + qk_i[0] % 4]
                        extra = []
                        if br == 1:
                            dg = t * 128 - kb
                            if t >= 8:
                                extra.append((pb[:n, 0:w], Msk[:n, mbase + kb:mbase + kb + w]))
                            if 0 <= dg < w:
                                extra.append((pb[:n, dg:dg + 128], causal_bf[:n, :]))
                        else:
                            extra.append((pb[:n, 0:w], Msk[:n, mbase + kb:mbase + kb + w]))
                        kk.mm(pb[:n, 0:w], QT[64:128, hd, tok], KT[64:128, g, kbase + kb:kbase + kb + w], start=True, stop=(len(extra) == 0))
                        for ei, (eo, er) in enumerate(extra):
                            kk.mm(eo, ident_bf[:n, :n], er, start=False, stop=(ei == len(extra) - 1))
                        init = -1e30 if nbk == 0 else ma[:n, nbk - 1:nbk]
                        rd = [pb[:n, 0:w]] + ([] if nbk == 0 else [ma[:n, nbk - 1:nbk]])
                        S.add('dve', lambda e, kb=kb, w=w, pb=pb, init=init, nbk=nbk: e.tensor_scalar(out=sc[:n, kb:kb + w], in0=pb[:n, 0:w], scalar1=0.125, scalar2=init, op0=ALU.mult, op1=ALU.max, accum_out=ma[:n, nbk:nbk + 1]),
                              reads=rd, writes=[sc[:n, kb:kb + w], ma[:n, nbk:nbk + 1]])
                        nbk += 1
                    kk.ts(negm[:n], ma[:n, nbk - 1:nbk], -1.0, None, ALU.mult)

                def A2():
                    kk.act(pbf[:n, 0:nkk], sc[:n, 0:nkk], AF.Exp, bias=negm[:n, 0:1], accum=rs[:n])

                def B1():
                    nb = len(jl)
                    for i0 in range(0, nb, 8):
                        tr_i[0] += 1
                        pst = bfv(B[5 + tr_i[0] % 2], 8, 128)
                        cnt = min(8, nb - i0)
                        for i in range(cnt):
                            kk.tr(pst[:, i, :n], pbf[:n, (i0 + i) * 128:(i0 + i + 1) * 128], ident_bf[:n, :n])
                        kk.cp(pT[:, i0:i0 + cnt, :n], pst[:, 0:cnt, :n], eng='act')

                def B2():
                    nb = len(jl)
                    for i, j in enumerate(jl):
                        kk.mm(B[7][:n, 0:64], pT[:, i, :n], VS[:, j, vsel, g * 64:(g + 1) * 64], start=(i == 0), stop=(i == nb - 1))
                    kk.ts(rs[:n], rs[:n], 1e-30, None, ALU.max)
                    kk.recip(rs[:n], rs[:n])
                    kk.tt(coef[:n], rs[:n], G[:n, t, hd * 3 + br:hd * 3 + br + 1], ALU.mult)
                    kk.stt(acc[:n, hd, :], B[7][:n, 0:64], coef[:n, 0:1], acc[:n, hd, :], ALU.mult, ALU.add)

                return (A1, A2, B1, B2)

            def post_stages(t):
                pos0 = t * 128
                acc = accs[t % 2]
                xt = W['xt'][t % 2]

                def q0():
                    kk.cp(catb[:n, 0:512], acc[:n].rearrange('p h d -> p (h d)'), eng='act')
                    kk.cp(catb[:n, 512:1024], YC[:n, t, :], eng='pool')

                def q1():
                    tr_i[0] += 1
                    pst = bfv(B[5 + tr_i[0] % 2], 8, 128)
                    for kc in range(8):
                        kk.tr(pst[:, kc, :n], catb[:n, kc * 128:(kc + 1) * 128], ident_bf[:n, :n])
                    kk.cp(catT[:, :, :n], pst[:, :, :n], eng='act')

                def q2():
                    for cb in range(2):
                        qk_i[0] += 1
                        pb = B[1 + qk_i[0] % 4]
                        for kc in range(8):
                            kk.mm(pb[:n, :], catT[:, kc, :n], Wout[:, kc, cb * 512:(cb + 1) * 512], start=(kc == 0), stop=(kc == 7))
                        kk.tt(xo[:n, cb * 512:(cb + 1) * 512], pb[:n, :], xt[:n, cb * 512:(cb + 1) * 512], ALU.add)
                    kk.dma(xres[pos0:pos0 + n, :], xo[:n, :], q='pool')

                return [q0, q1, q2]

            groups = [(t, g) for t in range(NT) for g in range(2)]
            NG = len(groups)
            nsteps = 8 * NG + 8
            steps = [[] for _ in range(nsteps)]
            for f_ in pre_stages(0, 0):
                f_()
            alltasks = []
            for gi, (t, g) in enumerate(groups):
                for r in range(4):
                    for br in (1, 2):
                        alltasks.append(make_task(t, g, g * 4 + r, br, len(alltasks)))
            NTK = len(alltasks)
            for k in range(NTK + 1):
                st = steps[k]
                if k < NTK:
                    st.append(alltasks[k][0])
                if k >= 1:
                    st.append(alltasks[k - 1][2])
                if k < NTK:
                    st.append(alltasks[k][1])
                if k >= 1:
                    st.append(alltasks[k - 1][3])
            for gi, (t, g) in enumerate(groups):
                base = 8 * gi
                if gi + 1 < NG:
                    for j_, f_ in enumerate(pre_stages(*groups[gi + 1])):
                        steps[base + j_].append(f_)
                if g == 1:
                    for j_, f_ in enumerate(post_stages(t)):
                        steps[base + 9 + j_].append(f_)
            preC = ffn_precast(1, ph, 'sp', ('act', 'act'))
            for si_, st in enumerate(steps):
                for f_ in st:
                    f_()
                if si_ % 11 == 5:
                    next(preC, None)
            for _ in preC:
                pass
            S.barrier(); S.flush()
        l0.close()

        with ExitStack() as ph:
            _uid[0] += 1
            B = [ph.enter_context(nc.psum_tensor('B%d_%d' % (_uid[0], i), [128, 512], F32)) for i in range(8)]
            Wout = sb(ph, 'WoutS', [128, 8, D], BF16)
            ocmp = sb(ph, 'ocmp', [1, NS, 512])
            osw = sb(ph, 'osw', [1, 2, 512])
            xt1 = sb(ph, 'xt1', [1, D])
            negsel_s = sb(ph, 'negsel_s', [4, NS, 2, 136])
            qSr = sb(ph, 'qSr', [64, 8, NS], BF16)
            kn_s = sb(ph, 'kn_s', [64, 2, NS], BF16)
            kn_w = sb(ph, 'kn_w', [64, 2, NS], BF16)
            sc_s = sb(ph, 'sc_s', [4, 8320])
            p_s = sb(ph, 'p_s', [4, 8320], BF16)
            pT_s = sb(ph, 'pT_s', [128, 65, 4], BF16)
            mx = sb(ph, 'mxs', [4, 1])
            negm = sb(ph, 'negms', [4, 1])
            rs = sb(ph, 'rss', [4, 1])
            ocn = sb(ph, 'ocn', [4, 64])
            with ExitStack() as ld:
                stg = [sb(ld, 'stgS0', [128, 8, 512]), sb(ld, 'stgS1', [128, 8, 512])]
                for c0 in range(0, D, 512):
                    load_w(Wout[:, :, c0:c0 + 512], I['wout'][:, c0:c0 + 512], stg[(c0 // 512) % 2], 8, 512)
                S.barrier(); S.flush()
            kk.dma(qSr[:], QTs[64:128, :, :])
            kk.dma(kn_s[:], KAs[64:128, :, :])
            kk.dma(kn_w[:], KBs[64:128, :, :])
            kk.memset(p_s[:], 0.0)

            def softmax4(nk):
                kk.rmax(mx[:], sc_s[:, 0:nk])
                kk.ts(negm[:], mx[:], -30000.0, -1.0, ALU.max, ALU.mult)
                kk.act(p_s[:, 0:nk], sc_s[:, 0:nk], AF.Exp, bias=negm[:, 0:1], accum=rs[:])
                kk.ts(rs[:], rs[:], 1e-30, None, ALU.max)
                kk.recip(rs[:], rs[:])

            def transposes4(nchunks):
                pst = bfv(B[4], 256, 4)
                for j in range(nchunks):
                    kk.tr(pst[:, j, :], p_s[:, j * 128:(j + 1) * 128], ident_bf[0:4, 0:4])
                kk.cpr(pT_s[:, 0:nchunks, :], pst[:, 0:nchunks, :])

            with ExitStack() as s1:
                W1s = [sb(s1, 'W1sk', [128, 16, 256], BF16), sb(s1, 'W1sv', [128, 16, 256], BF16)]
                W2s = [sb(s1, 'W2sk', [128, 2, 64], BF16), sb(s1, 'W2sv', [128, 2, 64], BF16)]
                pe128 = sb(s1, 'pe128', [128, 2, 16])
                pe128b = sb(s1, 'pe128b', [128, 2, 16], BF16)
                biasv = sb(s1, 'biasvs', [128, 2, 2])
                ovl_f = sb(s1, 'ovl_f', [128, 4, 129])
                vcO_s = sb(s1, 'vcO_s', [128, 4, 2, 200], BF16)
                with ExitStack() as ld:
                    stg = [sb(ld, 'stgT0', [128, 8, 512]), sb(ld, 'stgT1', [128, 8, 512])]
                    for kv, (w1n, w2n, pen) in enumerate([('w1k', 'w2k', 'pek'), ('w1v', 'w2v', 'pev')]):
                        sv = stg[kv].rearrange('p a (b c) -> p (a b) c', b=2)
                        kk.dma(sv, I[w1n].rearrange('(c p) f -> p c f', p=128))
                        kk.cpr(W1s[kv][:], sv)
                        kk.dma(stg[kv][:, 0:2, 0:64], I[w2n].rearrange('(k p) w -> p k w', p=128))
                        kk.cpr(W2s[kv][:], stg[kv][:, 0:2, 0:64])
                        for c_ in range(16):
                            kk.dma(pe128[:, kv, c_:c_ + 1], I[pen][2 * c_:2 * c_ + 2, :].rearrange('r (d o) -> (r d) o', o=1), q='pool', slow=True)
                    kk.cp(pe128b[:], pe128[:])
                    kk.dma(ovl_f[:], I['ovls'].rearrange('(c p) j -> p c j', p=128))
                    kk.memset(vcO_s[:], 0.0)
                    for g in range(2):
                        kk.cp(vcO_s[:, :, g, 64:193], ovl_f[:, :, :])
                    for kv in range(2):
                        for fc in range(2):
                            for c_ in range(16):
                                kk.mm(B[0][:, 0:1], W1s[kv][:, c_, fc * 128:(fc + 1) * 128], pe128b[:, kv, c_:c_ + 1], start=(c_ == 0), stop=(c_ == 15))
                            kk.cp(biasv[:, kv, fc:fc + 1], B[0][:, 0:1])
                    S.barrier(); S.flush()
                idx_i = sb(s1, 'idx_i', [128, NS * 4], I32)
                idx_f = sb(s1, 'idx_f', [128, NS * 4])
                idx_u = sb(s1, 'idx_u', [128, NS * 4], I32)
                Xg = [sb(s1, 'Xg0', [128, 4096]), sb(s1, 'Xg1', [128, 4096])]
                XT = sb(s1, 'XT', [128, 4, 8, 512], BF16)
                Xr = sb(s1, 'Xr', [128, 4, 1024], BF16)
                Hs = sb(s1, 'Hss', [128, 2, 2, 2, 512], BF16)
                kcT_s = sb(s1, 'kcT_s', [64, 2, 512], BF16)
                imph = sb(s1, 'imph', [4, 136])
                impg = sb(s1, 'impg', [4, 136])
                imp2 = sb(s1, 'imp2s', [4, 136])
                m8 = sb(s1, 'm8s', [4, 8])
                sel = sb(s1, 'sels', [4, 136])
                ccv = I['ccmp'].rearrange('(n r) c -> n (r c)', r=16)
                for b in range(NS):
                    for tt in range(4):
                        srcp = bass.AP(I['ptab'].tensor, b * 64 + tt * 16, [[1, 16], [0, 8], [1, 1]])
                        kk.dma(idx_i[:, b * 4 + tt:b * 4 + tt + 1], srcp)
                kk.cp(idx_f[:], idx_i[:])
                kk.ts(idx_f[:], idx_f[:], 8.0, cst2[:, 0:1], ALU.mult, ALU.add)
                kk.cp(idx_u[:], idx_f[:])
                def s1_gath(b):
                    for tt in range(4):
                        xg = Xg[tt % 2]
                        ic = b * 4 + tt
                        S.add('pool', lambda e, xg=xg, ic=ic: e.indirect_dma_start(out=xg[:, :], out_offset=None, in_=ccv, in_offset=bass.IndirectOffsetOnAxis(ap=idx_u[:, ic:ic + 1], axis=0)),
                              reads=[idx_u[:, ic:ic + 1], ccv], writes=[xg[:, :]], dma=True)
                        xv = xg[:, :].rearrange('p (s c d) -> p c s d', s=16, c=4, d=64)
                        for comb in range(4):
                            kk.cpr(Xr[:, comb, :].rearrange('p (s d) -> p s d', d=64), xv[:, comb, :, :])
                        for comb in range(4):
                            pst = bfv(B[1 + comb % 2], 8, 128)
                            for sp in range(8):
                                kk.tr(pst[:, sp, :], Xr[:, comb, sp * 128:(sp + 1) * 128], ident_bf[:, :])
                            kk.cpr(XT[:, comb, :, tt * 128:(tt + 1) * 128], pst[:, :, :])

                def s1_comp(b):
                    for kv in range(2):
                        for g in range(2):
                            comb = kv * 2 + g
                            for fc in range(2):
                                pb = B[3 + (fc % 2)]
                                for c_ in range(16):
                                    rhs = XT[:, comb, c_, 0:511] if c_ < 8 else XT[:, comb, c_ - 8, 1:512]
                                    kk.mm(pb[:, 0:511], W1s[kv][:, c_, fc * 128:(fc + 1) * 128], rhs, start=(c_ == 0), stop=(c_ == 15))
                                kk.act(Hs[:, kv, g, fc, 0:511], pb[:, 0:511], AF.Silu, bias=biasv[:, kv, fc:fc + 1])
                    for g in range(2):
                        for fc in range(2):
                            kk.mm(B[5][0:64, 0:511], W2s[0][:, fc, :], Hs[:, 0, g, fc, 0:511], start=(fc == 0), stop=(fc == 1))
                        kk.cp(kcT_s[:, g, 0:511], B[5][0:64, 0:511])
                        for ch in range(4):
                            m = 128 if ch < 3 else 127
                            for fc in range(2):
                                kk.mm(B[6][0:m, ch * 64:(ch + 1) * 64], Hs[:, 1, g, fc, ch * 128:ch * 128 + m], W2s[1][:, fc, :], start=(fc == 0), stop=(fc == 1))
                            kk.cp(vcO_s[0:m, ch, g, 0:64], B[6][0:m, ch * 64:(ch + 1) * 64])

                def s1_tail(b):
                    for g in range(2):
                        kk.mm(B[0][0:4, 0:511], QTs[0:64, g * 4:(g + 1) * 4, b], kcT_s[:, g, 0:511])
                        kk.ts(sc_s[:, 0:511], B[0][0:4, 0:511], 0.125, None, ALU.mult)
                        kk.memset(p_s[:, 511:512], 0.0)
                        softmax4(511)
                        transposes4(4)
                        for ch in range(4):
                            m = 128 if ch < 3 else 127
                            kk.mm(B[7][0:4, 0:193], pT_s[0:m, ch, :], vcO_s[0:m, ch, g, 0:193], start=(ch == 0), stop=(ch == 3))
                        kk.ts(ocn[:], B[7][0:4, 0:64], rs[:, 0:1], None, ALU.mult)
                        kk.dma(ocmp[0:1, b, g * 256:(g + 1) * 256].rearrange('p (h d) -> p h d', h=4), ocn[:])
                        kk.ts(imph[:, 0:129], B[7][0:4, 64:193], rs[:, 0:1], None, ALU.mult)
                        kk.mm(B[0][0:4, 0:129], ones_f[0:4, 0:4], imph[:, 0:129])
                        kk.cp(impg[:, 0:129], B[0][0:4, 0:129])
                        kk.memset(impg[:, 0:1], 1e9, eng='dve')
                        kk.memset(impg[:, 127:129], 1e9, eng='dve')
                        S.add('dve', lambda e: e.max(out=m8[:], in_=impg[:, 0:129]), reads=[impg[:, 0:129]], writes=[m8[:]])
                        S.add('dve', lambda e: e.match_replace(out=imp2[:, 0:129], in_to_replace=m8[:], in_values=impg[:, 0:129], imm_value=-3e9), reads=[m8[:], impg[:, 0:129]], writes=[imp2[:, 0:129]])
                        S.add('dve', lambda e: e.max(out=m8[:], in_=imp2[:, 0:129]), reads=[imp2[:, 0:129]], writes=[m8[:]])
                        kk.ts(sel[:, 0:129], impg[:, 0:129], m8[:, 7:8], None, ALU.is_ge)
                        kk.ts(negsel_s[:, b, g, 0:129], sel[:, 0:129], -1.0, 1e30, ALU.add, ALU.mult)

                s1_gath(0)
                s1_comp(0)
                for b in range(NS):
                    if b + 1 < NS:
                        s1_gath(b + 1)
                    s1_tail(b)
                    if b + 1 < NS:
                        s1_comp(b + 1)
                S.barrier(); S.flush()

            with ExitStack() as s2:
                idr_i = sb(s2, 'idr_i', [128, NS * 64], I32)
                idr_f = sb(s2, 'idr_f', [128, NS * 64])
                idr_u = sb(s2, 'idr_u', [128, NS * 64], I32)
                Kp = [sb(s2, 'Kp0', [128, 8, 256]), sb(s2, 'Kp1', [128, 8, 256])]
                KsTs = [sb(s2, 'KsT0', [128, 8320], BF16), sb(s2, 'KsT1', [128, 8320], BF16)]
                Vs_ss = [sb(s2, 'Vs_s0', [128, 64, 128], BF16), sb(s2, 'Vs_s1', [128, 64, 128], BF16)]
                Wp = sb(s2, 'Wp', [128, 4, 256])
                KwT = sb(s2, 'KwT', [64, 2, 640], BF16)
                Vw_s = sb(s2, 'Vw_s', [128, 4, 128], BF16)
                acc1 = sb(s2, 'acc1', [1, 512])
                tmp1 = sb(s2, 'tmp1', [1, 512])
                catb = sb(s2, 'catbs', [1, D], BF16)
                catT = sb(s2, 'catTs', [128, 8, 1], BF16)
                xo = xt1
                csv = I['cslc']
                kk.dma(idr_i[:], bass.AP(I['ptab'].tensor, 0, [[0, 128], [1, NS * 64]]))
                kk.cp(idr_f[:], idr_i[:])
                kk.ts(idr_f[:], idr_f[:], 128.0, cst2[:, 1:2], ALU.mult, ALU.add)
                kk.cp(idr_u[:], idr_f[:])
                def gather_gen(b):
                    KsT, Vs_s = KsTs[b % 2], Vs_ss[b % 2]
                    for j0 in range(0, 64, 8):
                        kp = Kp[(j0 // 8) % 2]
                        for jj in range(8):
                            j = b * 64 + j0 + jj
                            S.add('pool', lambda e, kp=kp, jj=jj, j=j: e.indirect_dma_start(out=kp[:, jj, :], out_offset=None, in_=csv, in_offset=bass.IndirectOffsetOnAxis(ap=idr_u[:, j:j + 1], axis=0)),
                                  reads=[idr_u[:, j:j + 1], csv], writes=[kp[:, jj, :]], dma=True)
                        for q_ in range(2):
                            pb = B[1 + q_]
                            for jj in range(4):
                                kk.tr(pb[:, jj * 128:(jj + 1) * 128], kp[:, q_ * 4 + jj, 0:128], ident_f)
                            kk.cpr(KsT[:, (j0 + q_ * 4) * 128:(j0 + q_ * 4 + 4) * 128], pb[:, :])
                        kk.cpr(Vs_s[:, j0:j0 + 8, :], kp[:, :, 128:256])
                        yield
                    kk.cp(KsT[0:64, 8192:8193], kn_s[:, 0, b:b + 1])
                    kk.cp(KsT[64:128, 8192:8193], KAs[64:128, 1, b:b + 1])
                    yield

                def attn_gen(b):
                    KsT, Vs_s = KsTs[b % 2], Vs_ss[b % 2]
                    ti = NT + b
                    row0 = T + b
                    for g in range(2):
                        lq = qSr[:, 0:4, b] if g == 0 else QTs[64:128, 4:8, b]
                        kT = KsT[0:64, :] if g == 0 else KsT[64:128, :]
                        for kb in range(0, 8192, 512):
                            pb = B[3 + (kb // 512) % 2]
                            kk.mm(pb[0:4, :], lq, kT[:, kb:kb + 512])
                            kk.stt(sc_s[:, kb:kb + 512].rearrange('p (a k) -> p a k', k=64), pb[0:4, :].rearrange('p (a k) -> p a k', k=64), 0.125,
                                   bc(negsel_s[:, b, g, kb // 64:kb // 64 + 8].rearrange('p (a o) -> p a o', o=1), [4, 8, 64]), ALU.mult, ALU.add)
                        kk.mm(B[3][0:4, 0:1], lq, kT[:, 8192:8193])
                        kk.ts(sc_s[:, 8192:8193], B[3][0:4, 0:1], 0.125, None, ALU.mult)
                        yield
                        softmax4(8193)
                        yield
                        transposes4(65)
                        yield
                        for j in range(64):
                            kk.mm(B[5][0:4, 0:64], pT_s[:, j, :], Vs_s[:, j, g * 64:(g + 1) * 64], start=(j == 0), stop=False)
                        kk.mm(B[5][0:4, 0:64], pT_s[0:1, 64, :], VSs[0:1, b, 0, g * 64:(g + 1) * 64], start=False, stop=True)
                        kk.ts(ocn[:], B[5][0:4, 0:64], rs[:, 0:1], None, ALU.mult)
                        kk.dma(osw[0:1, 0, g * 256:(g + 1) * 256].rearrange('p (h d) -> p h d', h=4), ocn[:])
                    yield
                    kk.dma(Wp[:], I['cwin'][b].rearrange('(c p) f -> p c f', p=128))
                    for g in range(2):
                        for c_ in range(4):
                            kk.tr(B[1][0:64, c_ * 128:(c_ + 1) * 128], Wp[:, c_, g * 64:(g + 1) * 64], ident_f)
                        kk.cpr(KwT[:, g, 0:512], B[1][0:64, :])
                        kk.cp(KwT[:, g, 512:513], kn_w[:, g, b:b + 1])
                    kk.cpr(Vw_s[:], Wp[:, :, 128:256])
                    kk.memset(p_s[:, 513:640], 0.0)
                    for g in range(2):
                        lq = qSr[:, g * 4:(g + 1) * 4, b]
                        kk.mm(B[3][0:4, :], lq, KwT[:, g, 0:512])
                        kk.ts(sc_s[:, 0:512], B[3][0:4, :], 0.125, None, ALU.mult)
                        kk.mm(B[4][0:4, 0:1], lq, KwT[:, g, 512:513])
                        kk.ts(sc_s[:, 512:513], B[4][0:4, 0:1], 0.125, None, ALU.mult)
                        kk.memset(sc_s[:, 0:1], NEG, eng='dve')
                        yield
                        softmax4(513)
                        yield
                        transposes4(5)
                        for j in range(4):
                            kk.mm(B[5][0:4, 0:64], pT_s[:, j, :], Vw_s[:, j, g * 64:(g + 1) * 64], start=(j == 0), stop=False)
                        kk.mm(B[5][0:4, 0:64], pT_s[0:1, 4, :], VSs[0:1, b, 1, g * 64:(g + 1) * 64], start=False, stop=True)
                        kk.ts(ocn[:], B[5][0:4, 0:64], rs[:, 0:1], None, ALU.mult)
                        kk.dma(osw[0:1, 1, g * 256:(g + 1) * 256].rearrange('p (h d) -> p h d', h=4), ocn[:])
                    yield
                    gv = Gs[0:1, b, :].rearrange('p (h r) -> p h r', r=3)
                    for br in range(3):
                        dst = acc1 if br == 0 else tmp1
                        osrc = ocmp[0:1, b, :] if br == 0 else osw[0:1, br - 1, :]
                        kk.tt(dst[0:1, :].rearrange('p (h d) -> p h d', h=8), osrc.rearrange('p (h d) -> p h d', h=8), bc(gv[:, :, br:br + 1], [1, 8, 64]), ALU.mult)
                        if br > 0:
                            kk.tt(acc1[0:1, :], acc1[0:1, :], tmp1[0:1, :], ALU.add)
                    kk.cp(catb[0:1, 0:512], acc1[0:1, :], eng='act')
                    kk.cp(catb[0:1, 512:1024], YCs[0:1, b, :], eng='dve')
                    pst = bfv(B[6], 8, 128)
                    for kc in range(8):
                        kk.tr(pst[:, kc, 0:1], catb[0:1, kc * 128:(kc + 1) * 128], ident_bf[0:1, 0:1])
                    kk.cpr(catT[:, :, 0:1], pst[:, :, 0:1])
                    xt = xt1
                    kk.dma(xt[0:1], I['xs'][b:b + 1, :])
                    for cb in range(2):
                        pb = B[1 + cb]
                        for kc in range(8):
                            kk.mm(pb[0:1, :], catT[:, kc, 0:1], Wout[:, kc, cb * 512:(cb + 1) * 512], start=(kc == 0), stop=(kc == 7))
                        kk.tt(xo[0:1, cb * 512:(cb + 1) * 512], pb[0:1, :], xt[0:1, cb * 512:(cb + 1) * 512], ALU.add)
                    kk.dma(xres[row0:row0 + 1, :], xo[0:1, :], q='pool')
                    yield

                pre0 = ffn_precast(0, s2)

                def pre_slice(cnt):
                    for _ in range(cnt):
                        if next(pre0, 'end') == 'end':
                            return
                        yield

                for _ in gather_gen(0):
                    pass
                for b in range(NS):
                    rr(attn_gen(b), gather_gen(b + 1) if b + 1 < NS else None, pre_slice(6))
                for _ in pre0:
                    pass
                S.barrier(); S.flush()

        def ffn_phase(l):
            wgs, wus = WGS[l], WUS[l]
            with ExitStack() as ph:
                W = common_work(ph)
                B = W['B']
                WD = sb(ph, 'WD', [128, 22, D], BF16)
                gfin = sb(ph, 'gfin', [128, D])
                stgW = sb(ph, 'stgFW', [128, 4, D])

                def load_wd_chunk(ci):
                    k0 = ci * 4
                    kn = min(4, 22 - k0)
                    kk.dma(stgW[:, 0:kn, :], I['wd'][l][k0 * 128:(k0 + kn) * 128, :].rearrange('(k p) w -> p k w', p=128), q='pool')
                    kk.cp(WD[:, k0:k0 + kn, :], stgW[:, 0:kn, :], eng='act')
                kk.dma(W['gain'][:], dram_bcast(I['norm_ffn'][l], 128, D))
                kk.dma(gfin[:], dram_bcast(I['norm_final'], 128, D))
                xnTbs = [sb(ph, 'xnTb0', [128, 8, 516], BF16), sb(ph, 'xnTb1', [128, 8, 516], BF16)]
                hT = sb(ph, 'hTf', [128, 22, 516], BF16)
                wgf = [sb(ph, 'wgf%d' % q_, [128, 8, 128], BF16) for q_ in range(4)]
                wuf = [sb(ph, 'wuf%d' % q_, [128, 8, 128], BF16) for q_ in range(4)]
                sgs = [sb(ph, 'sg0', [128, 516]), sb(ph, 'sg1', [128, 516])]
                xo = sb(ph, 'xof', [128, D])
                yo = sb(ph, 'yof', [128, D])
                xr = [sb(ph, 'xr0', [128, D]), sb(ph, 'xr1', [128, D])]
                blocks = []
                for blk in range(4):
                    btiles = [tl for tl in tiles if (tl[4] is None and tl[0] // 4 == blk) or (tl[4] is not None and blk == 3)]
                    cols = {}
                    c = 0
                    for tl in btiles:
                        cols[tl[0]] = c
                        c += tl[1]
                    blocks.append((btiles, cols, c))

                def norm_tile(blk, k_):
                    btiles, cols, ntok = blocks[blk]
                    if k_ >= len(btiles):
                        return
                    (ti, n, pos0, row0, sb_) = btiles[k_]
                    xt = W['xt'][ti % 2]
                    kk.dma(xt[:n], xres[row0:row0 + n, :])
                    norm_T(W, xt, n)
                    kk.cpr(xnTbs[blk % 2][:, :, cols[ti]:cols[ti] + n], W['xnT'][:, :, :n])

                def norm_block(blk):
                    for k_ in range(len(blocks[blk][0])):
                        norm_tile(blk, k_)

                def gateup_block(blk):
                    btiles, cols, ntok = blocks[blk]
                    xnTb = xnTbs[blk % 2]
                    segs = [(0, min(512, ntok))] + ([(512, ntok)] if ntok > 512 else [])
                    for fp in range(11):
                        hb = fp % 2
                        for fi in range(2):
                            f_ = fp * 2 + fi
                            kk.dma(wgf[f_ % 4][:].rearrange('p k w -> p (k w)'), wgs[f_])
                            kk.dma(wuf[f_ % 4][:].rearrange('p k w -> p (k w)'), wus[f_])
                        for fi in range(2):
                            f = fp * 2 + fi
                            for si, (s0, s1) in enumerate(segs):
                                w = s1 - s0
                                if si == 0:
                                    pg = B[1 + (f % 2) * 2]
                                    pu = B[2 + (f % 2) * 2]
                                else:
                                    pg = B[5]
                                    pu = B[6]
                                for kc in range(8):
                                    kk.mm(pg[:, 0:w], wgf[f % 4][:, kc, :], xnTb[:, kc, s0:s1], start=(kc == 0), stop=(kc == 7))
                                for kc in range(8):
                                    kk.mm(pu[:, 0:w], wuf[f % 4][:, kc, :], xnTb[:, kc, s0:s1], start=(kc == 0), stop=(kc == 7))
                                sg = sgs[f % 2]
                                kk.act(sg[:, s0:s1], pg[:, 0:w], AF.Silu)
                                kk.tt(hT[:, f, s0:s1], sg[:, s0:s1], pu[:, 0:w], ALU.mult)
                        if blk + 1 < 4:
                            norm_tile(blk + 1, fp)
                        if blk == 0 and fp < 6:
                            load_wd_chunk(fp)
                        if l == 0 and blk in (1, 2):
                            next(pre_next, None)

                def down_block(blk):
                    btiles, cols, ntok = blocks[blk]
                    for (ti, n, pos0, row0, sb_) in btiles:
                        c = cols[ti]
                        xt = xr[ti % 2]
                        kk.dma(xt[:n], xres[row0:row0 + n, :], q='pool')
                        for cb in range(2):
                            pb = B[5 + cb]
                            for f in range(22):
                                kk.mm(pb[:n, :], hT[:, f, c:c + n], WD[:, f, cb * 512:(cb + 1) * 512], start=(f == 0), stop=(f == 21))
                            kk.tt(xo[:n, cb * 512:(cb + 1) * 512], pb[:n, :], xt[:n, cb * 512:(cb + 1) * 512], ALU.add)
                        if l == 0:
                            kk.dma(xres[row0:row0 + n, :], xo[:n, :], q='pool')
                        else:
                            rstd_of(yo, xo, n, D, ssq2, rstd2)
                            kk.stt(yo[:n], xo[:n], rstd2[:n, 0:1], gfin[:n], ALU.mult, ALU.mult)
                            dst = O['y_p'][row0:row0 + n, :] if sb_ is None else O['y_s'][sb_:sb_ + 1, :]
                            kk.dma(dst, yo[:n, :], q='pool')

                ssq2 = sb(ph, 'ssq2', [128, 1])
                rstd2 = sb(ph, 'rstd2', [128, 1])
                pre_next = iter(())
                norm_block(0)
                for blk in range(4):
                    gateup_block(blk)
                    down_block(blk)
                for _ in pre_next:
                    pass
                S.barrier(); S.flush()

        ffn_phase(0)

        with ExitStack() as ph:
            W = common_work(ph)
            B = W['B']
            WC = sb(ph, 'WC', [128, 8, 2 * LRU], BF16)
            WOC = sb(ph, 'WOC', [128, 10, D], BF16)
            LWA = sb(ph, 'LWA', [128, 10, 128], BF16)
            LWX = sb(ph, 'LWX', [128, 10, 128], BF16)
            lcw_f = sb(ph, 'lcw_f', [128, 10, 4])
            lcb_f = sb(ph, 'lcb_f', [128, 10])
            lba_f = sb(ph, 'lba_f', [128, 10])
            lbx_f = sb(ph, 'lbx_f', [128, 10])
            c8 = sb(ph, 'c8', [128, 10])
            dg = sb(ph, 'dgw', [128, 40, 128])
            lcb_row = sb(ph, 'lcb_row', [1, LRU])
            c8x2 = sb(ph, 'c8x2', [128, 10])
            with ExitStack() as ld:
                stg = [sb(ld, 'stgL0', [128, 8, 512]), sb(ld, 'stgL1', [128, 8, 512])]
                for i, c0 in enumerate(range(0, 2 * LRU, 512)):
                    load_w(WC[:, :, c0:c0 + 512], I['winc'][:, c0:c0 + 512], stg[i % 2], 8, 512)
                for i, c0 in enumerate(range(0, D, 512)):
                    kk.dma(stg[i % 2][:, 0:8, :], I['woutc'][0:1024, c0:c0 + 512].rearrange('(k p) w -> p k w', p=128))
                    kk.cpr(WOC[:, 0:8, c0:c0 + 512], stg[i % 2][:, 0:8, :])
                    kk.dma(stg[i % 2][:, 0:2, :], I['woutc'][1024:1280, c0:c0 + 512].rearrange('(k p) w -> p k w', p=128))
                    kk.cpr(WOC[:, 8:10, c0:c0 + 512], stg[i % 2][:, 0:2, :])
                sva = stg[0].rearrange('p a (b c) -> p (a b) c', b=4)[:, 0:10, :]
                svx = stg[1].rearrange('p a (b c) -> p (a b) c', b=4)[:, 0:10, :]
                kk.dma(sva, I['lwa'].rearrange('h i j -> i h j'))
                kk.cpr(LWA[:], sva)
                kk.dma(svx, I['lwx'].rearrange('h i j -> i h j'))
                kk.cpr(LWX[:], svx)
                for k_ in range(4):
                    kk.dma(lcw_f[:, :, k_], I['lcw'][k_].rearrange('(c p) -> p c', p=128), slow=True)
                kk.dma(lcb_f[:], I['lcb'].rearrange('(c p) -> p c', p=128), slow=True)
                kk.dma(lba_f[:], I['lba'].rearrange('(c p) -> p c', p=128), slow=True)
                kk.dma(lbx_f[:], I['lbx'].rearrange('(c p) -> p c', p=128), slow=True)
                kk.dma(c8[:], I['lam'].rearrange('(c p) -> p c', p=128), slow=True)
                kk.dma(W['gain'][:], dram_bcast(I['norm_mix'][1], 128, D))
                kk.dma(lcb_row[:], I['lcb'].rearrange('(o c) -> o c', o=1))
                for h_ in range(10):
                    for k_ in range(4):
                        kk.ts(dg[:, h_ * 4 + k_, :], ident_f, lcw_f[:, h_, k_:k_ + 1], None, ALU.mult, eng=('dve', 'pool')[k_ % 2])
                kk.act(c8[:], c8[:], AF.Exp, scale=-1.0)
                kk.act(c8[:], c8[:], AF.Ln, bias=1.0)
                kk.ts(c8[:], c8[:], -8.0, None, ALU.mult)
                kk.ts(c8x2[:], c8[:], 2.0, None, ALU.mult)
                S.barrier(); S.flush()
            cbuf = sb(ph, 'lcbuf', [128, 10, 131])
            c3 = sb(ph, 'lc3', [128, 10, 3])
            gq = sb(ph, 'gq', [128, 10, 128])
            gsb = sb(ph, 'gsb', [128, 10, 128])
            rg = sb(ph, 'rg', [128, 10, 128])
            ig = sb(ph, 'ig', [128, 10, 128])
            av = sb(ph, 'av', [128, 10, 128])
            hh = sb(ph, 'hh', [128, 10, 128])
            hst = sb(ph, 'hst', [128, 10])
            yT = sb(ph, 'yT', [128, 10, 128], BF16)
            xo = sb(ph, 'xol', [128, D])
            cas = [sb(ph, 'lca0', [128, 10, 128]), sb(ph, 'lca1', [128, 10, 128])]
            gus = [sb(ph, 'gu0', [128, 10, 128]), sb(ph, 'gu1', [128, 10, 128])]
            xcbs = [sb(ph, 'xcb0', [128, 10, 128], BF16), sb(ph, 'xcb1', [128, 10, 128], BF16)]
            ssqs = [sb(ph, 'lssq0', [128, 1]), sb(ph, 'lssq1', [128, 1])]
            rstds = [sb(ph, 'lrstd0', [128, 1]), sb(ph, 'lrstd1', [128, 1])]

            def bodyL(i):
                (ti, n, pos0, row0, sb_) = tiles[i]
                par = i % 2
                pa, pb, pc, pd = [B[4 * par + k_] for k_ in range(4)]
                ca, gu, xcb = cas[par], gus[par], xcbs[par]
                ssq, rstd = ssqs[par], rstds[par]
                xt = W['xt'][par]
                xn, xnT = W['xn'], W['xnT']

                def slot(h):
                    bk = (pb, pc, pd)[h // 4]
                    return bk[:, (h % 4) * 128:(h % 4) * 128 + n]

                def slots3():
                    return [(pb[:, :].rearrange('p (c t) -> p c t', c=4)[:, :, :n], 0, 4),
                            (pc[:, :].rearrange('p (c t) -> p c t', c=4)[:, :, :n], 4, 4),
                            (pd[:, 0:256].rearrange('p (c t) -> p c t', c=2)[:, :, :n], 8, 2)]
                kk.dma(xt[:n], xres[row0:row0 + n, :])
                rstd_of(W['junk'], xt, n, D, ssq, rstd)
                yield
                kk.stt(xn[:n], xt[:n], rstd[:n, 0:1], W['gain'][:n], ALU.mult, ALU.mult)
                yield
                psT = bfv(pa, 8, 128)
                for kc in range(8):
                    kk.tr(psT[:, kc, :n], xn[:n, kc * 128:(kc + 1) * 128], ident_bf[:n, :n])
                yield
                kk.cp(xnT[:, :, :n], psT[:, :, :n], eng='act')
                yield
                for h in range(10):
                    fc = 10 + h
                    for kc in range(8):
                        kk.mm(slot(h), WC[:, kc, fc * 128:(fc + 1) * 128], xnT[:, kc, :n], start=(kc == 0), stop=(kc == 7))
                yield
                if ti == 0:
                    kk.memset(cbuf[:, :, 0:3], 0.0)
                if sb_ is not None:
                    for j_ in range(3):
                        kk.dma(cbuf[:, :, j_], I['slconv'][sb_, j_].rearrange('(c p) -> p c', p=128), slow=True)
                for (pv_, h0, k_) in slots3():
                    kk.cpr(cbuf[:, h0:h0 + k_, 3:3 + n], pv_)
                yield
                if sb_ is not None or ti == NT - 1:
                    kk.cp(c3[:], cbuf[:, :, n:n + 3], eng='act')
                    dst = O['lconv_p'] if sb_ is None else O['lconv_s'][sb_]
                    for j_ in range(3):
                        kk.dma(dst[j_].rearrange('(c p) -> p c', p=128), c3[:, :, j_], q='pool', slow=True)
                for h in range(10):
                    for k_ in range(4):
                        kk.mm(slot(h), dg[:, h * 4 + k_, :], cbuf[:, h, k_:k_ + n], start=(k_ == 0), stop=False)
                    kk.mm(slot(h), lcb_row[0:1, h * 128:(h + 1) * 128], ones_f[0:1, 0:n], start=False, stop=True)
                yield
                for (pv_, h0, k_) in slots3():
                    kk.act(ca[:, h0:h0 + k_, :n], pv_, AF.Copy)
                    kk.act(xcb[:, h0:h0 + k_, :n], pv_, AF.Copy)
                if sb_ is None and ti < NT - 1:
                    kk.cp(c3[:], cbuf[:, :, n:n + 3], eng='act')
                    kk.cp(cbuf[:, :, 0:3], c3[:], eng='act')
                yield
                for h in range(10):
                    for kc in range(8):
                        kk.mm(slot(h), WC[:, kc, h * 128:(h + 1) * 128], xnT[:, kc, :n], start=(kc == 0), stop=(kc == 7))
                yield
                for (pv_, h0, k_) in slots3():
                    kk.act(gsb[:, h0:h0 + k_, :n], pv_, AF.Copy)
                yield
                kk.tt(gq[:, :, :n], gsb[:, :, :n], gsb[:, :, :n], ALU.mult)
                kk.ts(gq[:, :, :n], gq[:, :, :n], 0.044715, 1.0, ALU.mult, ALU.add)
                kk.tt(gq[:, :, :n], gq[:, :, :n], gsb[:, :, :n], ALU.mult)
                yield
                kk.act(gq[:, :, :n], gq[:, :, :n], AF.Sigmoid, scale=1.5957691216057308)
                yield
                kk.tt(gu[:, :, :n], gq[:, :, :n], gsb[:, :, :n], ALU.mult)
                for h in range(10):
                    kk.mm(slot(h), LWA[:, h, :], xcb[:, h, :n])
                yield
                for h in range(10):
                    kk.act(rg[:, h, :n], slot(h), AF.Sigmoid, bias=lba_f[:, h:h + 1])
                yield
                for h in range(10):
                    kk.mm(slot(h), LWX[:, h, :], xcb[:, h, :n])
                kk.tt(rg[:, :, :n], rg[:, :, :n], bc(c8[:].rearrange('p (c o) -> p c o', o=1), [128, 10, n]), ALU.mult)
                yield
                for h in range(10):
                    kk.act(ig[:, h, :n], slot(h), AF.Sigmoid, bias=lbx_f[:, h:h + 1])
                kk.act(av[:, :, :n], rg[:, :, :n], AF.Exp)
                kk.act(rg[:, :, :n], rg[:, :, :n], AF.Exp, scale=2.0)
                kk.act(rg[:, :, :n], rg[:, :, :n], AF.Ln, scale=-1.0, bias=1.0)
                kk.act(rg[:, :, :n], rg[:, :, :n], AF.Exp, scale=0.5)
                yield
                if ti == 0:
                    kk.memset(hst[:], 0.0)
                if sb_ is not None:
                    kk.dma(hst[:], I['slru'][sb_].rearrange('(c p) -> p c', p=128), slow=True)
                kk.tt(ig[:, :, :n], ig[:, :, :n], ca[:, :, :n], ALU.mult)
                kk.tt(ig[:, :, :n], ig[:, :, :n], rg[:, :, :n], ALU.mult)
                for h in range(10):
                    S.add('dve', lambda e, h=h: e.tensor_tensor_scan(out=hh[:, h, :n], data0=av[:, h, :n], data1=ig[:, h, :n], initial=hst[:, h:h + 1], op0=ALU.mult, op1=ALU.add),
                          reads=[av[:, h, :n], ig[:, h, :n], hst[:, h:h + 1]], writes=[hh[:, h, :n]])
                kk.cp(hst[:], hh[:, :, n - 1])
                if sb_ is not None or ti == NT - 1:
                    dst = O['lru_p'] if sb_ is None else O['lru_s'][sb_]
                    kk.dma(dst.rearrange('(c p) -> p c', p=128), hst[:], q='pool', slow=True)
                kk.tt(yT[:, :, :n], gu[:, :, :n], hh[:, :, :n], ALU.mult)
                yield
                for cb in range(2):
                    pbk = (pb, pc)[cb]
                    for h in range(10):
                        kk.mm(pbk[:n, :], yT[:, h, :n], WOC[:, h, cb * 512:(cb + 1) * 512], start=(h == 0), stop=(h == 9))
                yield
                for cb in range(2):
                    pbk = (pb, pc)[cb]
                    kk.tt(xo[:n, cb * 512:(cb + 1) * 512], pbk[:n, :], xt[:n, cb * 512:(cb + 1) * 512], ALU.add)
                kk.dma(xres[row0:row0 + n, :], xo[:n, :], q='pool')
                yield

            zipp([(lambda i=i: bodyL(i)) for i in range(len(tiles))], 13, 'BABBABBABBBBABBBABA')
            S.barrier(); S.flush()

        ffn_phase(1)

        S.barrier(); S.flush()
    return nc


def _consts():
    p = np.arange(128)[:, None]
    j = np.arange(128)[None, :]
    c = np.zeros((128, 768), np.float32)
    c[:, 0:128] = (p == j)
    c[:, 128:256] = (p > j)
    c[:, 256:384] = (p <= j)
    c[:, 384:512] = np.where(j <= p, 0.0, NEG)
    c[:, 512:640] = np.where(j <= p, NEG, 0.0)
    c[:, 640:768] = 1.0
    half = 32
    inv_freq = (np.float32(10000.0) ** (-(np.arange(half, dtype=np.float32)) / np.float32(half))).astype(np.float32)
    pos = np.concatenate([np.arange(T), np.full(NS, 8192)]).astype(np.float32)
    ang = (pos[:, None] * inv_freq[None, :]).astype(np.float32)
    rt = np.concatenate([np.cos(ang), np.sin(ang)], axis=1).astype(np.float32)
    n = np.arange(128)[:, None] * 16
    s = np.arange(32)[None, :] * 64
    ovl = ((n < s + 64) & (n + 32 > s)).astype(np.float32)
    c2 = np.zeros((128, 4), np.float32)
    c2[:, 0] = np.arange(128) % 8
    c2[:, 1] = np.arange(128)
    n5 = np.arange(512)[:, None] * 16
    s5 = np.arange(129)[None, :] * 64
    ovls = ((n5 < s5 + 64) & (n5 + 32 > s5)).astype(np.float32)
    ovls[511, :] = 0.0
    return c, rt, ovl, c2, ovls


_NC_CACHE = {}


def kernel(x_prompt, x_sample, cache_kv_cmp, cache_kv_slc, cache_kv_win, state_ssm, state_ssd_conv,
           state_lru, state_lru_conv, page_table, norm_mix, norm_ffn, norm_final, w_ffn_gate, w_ffn_up,
           w_ffn_down, w_in_a, w_out_a, cmp_pe_k, cmp_w1_k, cmp_w2_k, cmp_pe_v, cmp_w1_v, cmp_w2_v,
           ssd_conv_w, ssd_conv_b, ssd_dt_bias, ssd_a_log, ssd_d, ssd_norm, w_in_c, lru_conv_w, lru_conv_b,
           lru_w_a, lru_b_a, lru_w_x, lru_b_x, lru_lambda, w_out_c):
    f = lambda a: np.ascontiguousarray(np.asarray(a, dtype=np.float32))
    if 'nc' not in _NC_CACHE:
        _NC_CACHE['nc'] = build_program()
    nc = _NC_CACHE['nc']
    cst, rt, ovl, c2, ovls = _consts()
    ccmp = f(cache_kv_cmp).reshape(NPHYS * 128, 256)
    cslc = f(cache_kv_slc).reshape(NPHYS * 128, 256)
    shared = dict(
        ccmp=ccmp, cslc=cslc,
        norm_mix=f(norm_mix), norm_ffn=f(norm_ffn), norm_final=f(norm_final),
        wg=f(w_ffn_gate), wu=f(w_ffn_up), wd=f(w_ffn_down), win=f(w_in_a)[0], wout=f(w_out_a)[0],
        pek=f(cmp_pe_k)[0], w1k=f(cmp_w1_k)[0], w2k=f(cmp_w2_k)[0],
        pev=f(cmp_pe_v)[0], w1v=f(cmp_w1_v)[0], w2v=f(cmp_w2_v)[0],
        cw=f(ssd_conv_w)[0], cb=f(ssd_conv_b)[0], dtb=f(ssd_dt_bias)[0], alog=f(ssd_a_log)[0],
        dsk=f(ssd_d)[0], snorm=f(ssd_norm)[0],
        winc=f(w_in_c)[0], lcw=f(lru_conv_w)[0], lcb=f(lru_conv_b)[0], lwa=f(lru_w_a)[0], lba=f(lru_b_a)[0],
        lwx=f(lru_w_x)[0], lbx=f(lru_b_x)[0], lam=f(lru_lambda)[0], woutc=f(w_out_c)[0],
        cst=cst, ropetab=rt, ovl=ovl, cst2=c2, ovls=ovls,
    )
    xp = f(x_prompt)
    xs = f(x_sample)[:, 0, :]
    pt = np.ascontiguousarray(np.asarray(page_table, dtype=np.int32))
    in_maps = []
    for c in range(8):
        sl = slice(NS * c, NS * (c + 1))
        m = dict(shared)
        m.update(
            xp=xp[c], xs=np.ascontiguousarray(xs[sl]),
            cwin=np.ascontiguousarray(f(cache_kv_win)[0, sl].reshape(NS, 512, 256)),
            sssm=np.ascontiguousarray(f(state_ssm)[0, sl].reshape(NS, 512, 128)),
            sconv=np.ascontiguousarray(f(state_ssd_conv)[0, sl]),
            slru=np.ascontiguousarray(f(state_lru)[0, sl]),
            slconv=np.ascontiguousarray(f(state_lru_conv)[0, sl]),
            ptab=np.ascontiguousarray(pt[sl]),
        )
        in_maps.append(m)
    res = run_bass_kernel_spmd(nc, in_maps, core_ids=list(range(8))).results
    cat = lambda k: np.stack([np.asarray(r[k]) for r in res])
    cats = lambda k: np.concatenate([np.asarray(r[k]) for r in res], 0)
    y_prompt = cat('y_p').reshape(8, T, D)
    y_sample = cats('y_s').reshape(32, 1, D)
    kv_cmp_p = cat('kvc_p').reshape(1, 8, T, 2, 2, 64)
    kv_slc_p = cat('kvs_p').reshape(1, 8, T, 2, 2, 64)
    kv_win_p = cat('kvw_p').reshape(1, 8, 512, 2, 2, 64)
    ssm_p = cat('ssm_p').reshape(1, 8, 8, 64, 128)
    sconv_p = cat('sconv_p').reshape(1, 8, 3, 1024)
    lru_p = cat('lru_p').reshape(1, 8, LRU)
    lconv_p = cat('lconv_p').reshape(1, 8, 3, LRU)
    kv_cmp_s = cats('kvc_s').reshape(1, 32, 1, 2, 2, 64)
    kv_slc_s = cats('kvs_s').reshape(1, 32, 1, 2, 2, 64)
    kv_win_s = cats('kvw_s').reshape(1, 32, 512, 2, 2, 64)
    ssm_s = cats('ssm_s').reshape(1, 32, 8, 64, 128)
    sconv_s = cats('sconv_s').reshape(1, 32, 3, 1024)
    lru_s = cats('lru_s').reshape(1, 32, LRU)
    lconv_s = cats('lconv_s').reshape(1, 32, 3, LRU)
    outs = (y_prompt, y_sample, kv_cmp_p, kv_slc_p, kv_win_p, ssm_p, sconv_p, lru_p, lconv_p,
            kv_cmp_s, kv_slc_s, kv_win_s, ssm_s, sconv_s, lru_s, lconv_s)
    return tuple(np.ascontiguousarray(o, dtype=np.float32) for o in outs)
```

```python
import numpy as np
from contextlib import ExitStack
import concourse.bass as bass
import concourse.mybir as mybir
from concourse.bass_utils import run_bass_kernel_spmd

F32 = mybir.dt.float32
BF16 = mybir.dt.bfloat16
I32 = mybir.dt.int32
ALU = mybir.AluOpType
AF = mybir.ActivationFunctionType
AX = mybir.AxisListType

T = 2048
D = 1024
NT = 16
NS = 4
NPHYS = 2560
FFN = 2816
LRU = 1280
EPS = 1e-6
NEG = -1e30


def _region(ap):
    t = ap.tensor
    name = t.name
    dsz = mybir.dt.size(ap.dtype)
    pairs = [(int(s), int(c)) for s, c in ap.ap]
    off = int(ap.offset)
    if 'DRam' in type(t).__name__:
        lo = off + sum(min(0, s * (c - 1)) for s, c in pairs)
        hi = off + sum(max(0, s * (c - 1)) for s, c in pairs) + 1
        return (name, 0, 1, lo * dsz, hi * dsz)
    if 'PSum' in type(t).__name__:
        return (name, 0, 128, 0, 2048, True)
    R = 1
    for d in list(t.shape)[1:]:
        R *= int(d)
    ps, pc = pairs[0]
    p_lo = off // R
    pstep = max(1, ps // R) if ps else 0
    p_hi = p_lo + (pc - 1) * pstep + 1
    f0 = off % R
    lo = f0 + sum(min(0, s * (c - 1)) for s, c in pairs[1:])
    hi = f0 + sum(max(0, s * (c - 1)) for s, c in pairs[1:]) + 1
    return (name, p_lo, p_hi, lo * dsz, hi * dsz)


def _ovl(a, b):
    return a[1] < b[2] and b[1] < a[2] and a[3] < b[4] and b[3] < a[4]


def _cov(a, b):
    return a[1] <= b[1] and a[2] >= b[2] and a[3] <= b[3] and a[4] >= b[4]


class Op:
    __slots__ = ('i', 'eng', 'fn', 'dma', 'deps', 'signal', 'sem', 'semval', 'waits')


class Sched:
    ENGS = ['pe', 'act', 'pool', 'dve', 'sp']

    def __init__(self, nc, stack, ndma=56):
        self.nc = nc
        self.esem = {e: stack.enter_context(nc.semaphore('es_' + e)) for e in self.ENGS}
        self.dsem = [stack.enter_context(nc.semaphore('ds_%d' % i)) for i in range(ndma)]
        self.duse = [0] * ndma
        self.dlast = [None] * ndma
        self.dk = 0
        self.dk2 = 0
        self.ecnt = {e: 0 for e in self.ENGS}
        self.waited = {e: {} for e in self.ENGS}
        self.pending = []
        self.acc = {}
        self.n = 0
        self.last = {e: None for e in self.ENGS}
        self.dma_since = []

    def add(self, eng, fn, reads=(), writes=(), dma=False, extra_deps=()):
        op = Op()
        op.i = self.n
        self.n += 1
        op.eng = eng
        op.fn = fn
        op.dma = dma
        op.deps = set(extra_deps)
        op.signal = False
        op.sem = None
        op.semval = 0
        op.waits = []
        rr = [_region(a) for a in reads]
        ww = [_region(a) for a in writes]
        for r in rr:
            psum = len(r) > 5
            for (reg, o, isw) in self.acc.get(r[0], ()):
                if (isw or (psum and (o.eng != eng or o.dma != dma))) and _ovl(reg, r):
                    op.deps.add(o)
        for w in ww:
            for (reg, o, isw) in self.acc.get(w[0], ()):
                if _ovl(reg, w):
                    op.deps.add(o)
        for w in ww:
            lst = self.acc.setdefault(w[0], [])
            lst[:] = [x for x in lst if not _cov(w, x[0])]
            lst.append((w, op, True))
        for r in rr:
            lst = self.acc.setdefault(r[0], [])
            lst[:] = [x for x in lst if not ((not x[2]) and x[1].eng == eng and x[1].dma == dma and _cov(r, x[0]))]
            lst.append((r, op, False))
        op.deps.discard(op)
        if dma:
            half = len(self.dsem) // 2
            if eng == 'pool':
                k = half + self.dk2 % (len(self.dsem) - half)
                self.dk2 += 1
            else:
                k = self.dk % half
                self.dk += 1
            if self.dlast[k] is not None:
                op.deps.add(self.dlast[k])
            self.duse[k] += 1
            op.sem = self.dsem[k]
            op.semval = 16 * self.duse[k]
            self.dlast[k] = op
            self.dma_since.append(op)
        else:
            self.last[eng] = op
        self.pending.append(op)
        return op

    def barrier(self):
        deps = [o for o in self.last.values() if o is not None] + list(self.dma_since)
        for e in self.ENGS:
            self.add(e, None, extra_deps=list(deps))
        self.dma_since = []
        self.acc = {}
        self.last = {e: None for e in self.ENGS}

    def flush(self):
        nc = self.nc
        pend = self.pending
        self.pending = []
        if not pend:
            return
        first_i = pend[0].i
        for op in pend:
            for d in op.deps:
                if d.dma or d.i < first_i:
                    continue
                if d.eng == 'pe' and op.eng == 'pe' and not op.dma:
                    continue
                d.signal = True
        for op in pend:
            if not op.dma and op.signal:
                self.ecnt[op.eng] += 1
                op.sem = self.esem[op.eng]
                op.semval = self.ecnt[op.eng]
        per = {e: [] for e in self.ENGS}
        for op in pend:
            wl = {}
            for d in op.deps:
                if d.i < first_i and not d.dma:
                    continue
                if (not d.dma) and d.eng == 'pe' and op.eng == 'pe' and not op.dma:
                    continue
                if d.sem is None:
                    continue
                k = id(d.sem)
                if k not in wl or wl[k][1] < d.semval:
                    wl[k] = (d.sem, d.semval)
            wd = self.waited[op.eng]
            for k, (s, v) in wl.items():
                if wd.get(k, 0) >= v:
                    continue
                wd[k] = v
                op.waits.append((s, v))
            per[op.eng].append(op)

        def emit(e, ops):
            for op in ops:
                for (s, v) in op.waits:
                    e.wait_ge(s, v)
                if op.fn is None:
                    continue
                ins = op.fn(e)
                if op.dma:
                    ins.then_inc(op.sem, 16)
                elif op.signal:
                    ins.then_inc(op.sem, 1)

        with nc.Block() as blk:
            @blk.tensor
            def _(e):
                emit(e, per['pe'])

            @blk.scalar
            def _(e):
                emit(e, per['act'])

            @blk.gpsimd
            def _(e):
                emit(e, per['pool'])

            @blk.vector
            def _(e):
                emit(e, per['dve'])

            @blk.sync
            def _(e):
                emit(e, per['sp'])


class K:
    def __init__(self, S):
        self.S = S
        self.rr = 0

    def dma(self, out, in_, q='sp', slow=False):
        if slow:
            return self.S.add(q, lambda e: e.dma_start(out=out, in_=in_, allow_slow_non_contiguous=True), reads=[in_], writes=[out], dma=True)
        return self.S.add(q, lambda e: e.dma_start(out=out, in_=in_), reads=[in_], writes=[out], dma=True)

    def mm(self, out, lhsT, rhs, start=True, stop=True):
        return self.S.add('pe', lambda e: e.matmul(out, lhsT=lhsT, rhs=rhs, start=start, stop=stop), reads=[lhsT, rhs], writes=[out])

    def tr(self, out, in_, ident):
        return self.S.add('pe', lambda e: e.transpose(out=out, in_=in_, identity=ident), reads=[in_, ident], writes=[out])

    def tt(self, out, in0, in1, op, eng='dve'):
        return self.S.add(eng, lambda e: e.tensor_tensor(out=out, in0=in0, in1=in1, op=op), reads=[in0, in1], writes=[out])

    def ts(self, out, in0, s1, s2, op0, op1=None, eng='dve'):
        rd = [in0] + [s for s in (s1, s2) if not isinstance(s, (int, float, type(None)))]
        if op1 is None:
            return self.S.add(eng, lambda e: e.tensor_scalar(out=out, in0=in0, scalar1=s1, scalar2=None, op0=op0), reads=rd, writes=[out])
        return self.S.add(eng, lambda e: e.tensor_scalar(out=out, in0=in0, scalar1=s1, scalar2=s2, op0=op0, op1=op1), reads=rd, writes=[out])

    def stt(self, out, in0, scalar, in1, op0, op1):
        rd = [in0, in1] + ([scalar] if not isinstance(scalar, (int, float)) else [])
        return self.S.add('dve', lambda e: e.scalar_tensor_tensor(out=out, in0=in0, scalar=scalar, in1=in1, op0=op0, op1=op1), reads=rd, writes=[out])

    def act(self, out, in_, func, bias=None, scale=None, accum=None):
        rd = [in_]
        wr = [out]
        kw = {}
        if bias is not None:
            kw['bias'] = bias
            if not isinstance(bias, (int, float)):
                rd.append(bias)
        if scale is not None:
            kw['scale'] = scale
            if not isinstance(scale, (int, float)):
                rd.append(scale)
        if accum is not None:
            kw['accum_out'] = accum
            wr.append(accum)
        return self.S.add('act', lambda e: e.activation(out=out, in_=in_, func=func, **kw), reads=rd, writes=wr)

    def cp(self, out, in_, eng='dve'):
        if eng == 'act':
            return self.S.add('act', lambda e: e.copy(out=out, in_=in_), reads=[in_], writes=[out])
        return self.S.add(eng, lambda e: e.tensor_copy(out=out, in_=in_), reads=[in_], writes=[out])

    def cpr(self, out, in_):
        self.rr += 1
        return self.cp(out, in_, eng=('dve', 'act')[self.rr % 2])

    def memset(self, ap, v, eng='pool'):
        return self.S.add(eng, lambda e: e.memset(ap, v), writes=[ap])

    def recip(self, out, in_):
        return self.S.add('dve', lambda e: e.reciprocal(out=out, in_=in_), reads=[in_], writes=[out])

    def rmax(self, out, in_):
        return self.S.add('dve', lambda e: e.tensor_reduce(out=out, in_=in_, axis=AX.X, op=ALU.max), reads=[in_], writes=[out])


def bc(ap, shape):
    return ap.to_broadcast(list(shape))


def rr(*gens):
    gens = [g for g in gens if g is not None]
    while gens:
        for g in list(gens):
            try:
                next(g)
            except StopIteration:
                gens.remove(g)


def zip2(bodies, half):
    def adv(g):
        try:
            next(g)
            return True
        except StopIteration:
            return False
    cur, cnt = None, 0
    for mk in bodies:
        nxt, ncnt = mk(), 0
        if cur is not None:
            while cnt < half:
                if not adv(cur):
                    cur = None
                    break
                cnt += 1
            while cur is not None:
                if not adv(cur):
                    cur = None
                    break
                if adv(nxt):
                    ncnt += 1
        cur, cnt = nxt, ncnt
    while cur is not None and adv(cur):
        pass


def zipp(bodies, half, pattern):
    def adv(g):
        try:
            next(g)
            return True
        except StopIteration:
            return False
    cur = None
    for mk in bodies:
        nxt = mk()
        if cur is None:
            for _ in range(half):
                adv(nxt)
            cur = nxt
            continue
        for ch in pattern:
            adv(cur if ch == 'A' else nxt)
        while adv(cur):
            pass
        cur = nxt
    while cur is not None and adv(cur):
        pass


def dram_bcast(ap1d, nparts, width):
    return bass.AP(ap1d.tensor, int(ap1d.offset), [[0, nparts], [1, width]])


def build_program():
    nc = bass.Bass("TRN2", target_bir_lowering=False)

    def din(name, shape, dt=F32):
        return nc.dram_tensor(name, list(shape), dt, kind="ExternalInput").ap()

    def dout(name, shape):
        return nc.dram_tensor(name, list(shape), F32, kind="ExternalOutput").ap()

    I = dict(
        xp=din('xp', [T, D]), xs=din('xs', [NS, D]),
        ccmp=din('ccmp', [NPHYS * 128, 256]), cslc=din('cslc', [NPHYS * 128, 256]),
        cwin=din('cwin', [NS, 512, 256]), sssm=din('sssm', [NS, 512, 128]),
        sconv=din('sconv', [NS, 3, 1024]), slru=din('slru', [NS, LRU]), slconv=din('slconv', [NS, 3, LRU]),
        ptab=din('ptab', [NS, 64], I32),
        norm_mix=din('norm_mix', [2, D]), norm_ffn=din('norm_ffn', [2, D]), norm_final=din('norm_final', [D]),
        wg=din('wg', [2, D, FFN]), wu=din('wu', [2, D, FFN]), wd=din('wd', [2, FFN, D]),
        win=din('win', [D, 2848]), wout=din('wout', [D, D]),
        pek=din('pek', [32, 64]), w1k=din('w1k', [2048, 256]), w2k=din('w2k', [256, 64]),
        pev=din('pev', [32, 64]), w1v=din('w1v', [2048, 256]), w2v=din('w2v', [256, 64]),
        cw=din('cw', [4, 1024]), cb=din('cb', [1024]), dtb=din('dtb', [8]), alog=din('alog', [8]),
        dsk=din('dsk', [8]), snorm=din('snorm', [512]),
        winc=din('winc', [D, 2 * LRU]), lcw=din('lcw', [4, LRU]), lcb=din('lcb', [LRU]),
        lwa=din('lwa', [10, 128, 128]), lba=din('lba', [LRU]), lwx=din('lwx', [10, 128, 128]), lbx=din('lbx', [LRU]),
        lam=din('lam', [LRU]), woutc=din('woutc', [LRU, D]),
        cst=din('cst', [128, 768]), ropetab=din('ropetab', [T + NS, 64]),
        ovl=din('ovl', [128, 32]), cst2=din('cst2', [128, 4]), ovls=din('ovls', [512, 129]),
    )
    O = dict(
        y_p=dout('y_p', [T, D]), y_s=dout('y_s', [NS, D]),
        kvc_p=dout('kvc_p', [T, 256]), kvs_p=dout('kvs_p', [T, 256]), kvw_p=dout('kvw_p', [512, 256]),
        ssm_p=dout('ssm_p', [512, 128]), sconv_p=dout('sconv_p', [3, 1024]),
        lru_p=dout('lru_p', [LRU]), lconv_p=dout('lconv_p', [3, LRU]),
        kvc_s=dout('kvc_s', [NS, 256]), kvs_s=dout('kvs_s', [NS, 256]), kvw_s=dout('kvw_s', [NS, 512, 256]),
        ssm_s=dout('ssm_s', [NS, 512, 128]), sconv_s=dout('sconv_s', [NS, 3, 1024]),
        lru_s=dout('lru_s', [NS, LRU]), lconv_s=dout('lconv_s', [NS, 3, LRU]),
    )
    xres = nc.dram_tensor('xres', [T + NS, D], F32, kind="Internal").ap()

    tiles = [(t, 128, t * 128, t * 128, None) for t in range(NT)] + [(NT + b, 1, 8192, T + b, b) for b in range(NS)]

    with ExitStack() as top:
        S = Sched(nc, top)
        kk = K(S)
        _uid = [0]

        def sb(st, name, shape, dt=F32):
            _uid[0] += 1
            return st.enter_context(nc.sbuf_tensor('s%d_%s' % (_uid[0], name), list(shape), dt))
        cst = sb(top, 'cst', [128, 768])
        ident_bf = sb(top, 'ident_bf', [128, 128], BF16)
        kk.dma(cst[:], I['cst'])
        kk.cp(ident_bf[:], cst[:, 0:128])
        ident_f = cst[:, 0:128]
        Lstrict = cst[:, 128:256]
        Utri = cst[:, 256:384]
        causal = cst[:, 384:512]
        wlo = cst[:, 512:640]
        ones_f = cst[:, 640:768]
        QTs = sb(top, 'QTs', [128, 8, NS], BF16)
        KAs = sb(top, 'KAs', [128, 2, NS], BF16)
        KBs = sb(top, 'KBs', [128, 2, NS], BF16)
        VSs = sb(top, 'VSs', [1, NS, 2, 128], BF16)
        Gs = sb(top, 'Gs', [1, NS, 24])
        YCs = sb(top, 'YCs', [1, NS, 512], BF16)
        cst2 = sb(top, 'cst2', [128, 4])
        kk.dma(cst2[:], I['cst2'])
        l0 = ExitStack()
        QT = sb(l0, 'QT', [128, 8, T], BF16)
        KA = sb(l0, 'KA', [128, 2, T], BF16)
        KB = sb(l0, 'KB', [128, 2, T], BF16)
        VS = sb(l0, 'VS', [128, NT, 2, 128], BF16)
        G = sb(l0, 'G', [128, NT, 24])
        YC = sb(l0, 'YC', [128, NT, 512], BF16)

        def common_work(st):
            W = {}
            W['xt'] = [sb(st, 'xt0', [128, D]), sb(st, 'xt1', [128, D])]
            W['xn'] = sb(st, 'xn', [128, D], BF16)
            W['xnT'] = sb(st, 'xnT', [128, 8, 128], BF16)
            W['gain'] = sb(st, 'gain', [128, D])
            W['ssq'] = sb(st, 'ssq', [128, 1])
            W['rstd'] = sb(st, 'rstd', [128, 1])
            W['junk'] = sb(st, 'junk', [128, D])
            _uid[0] += 1
            W['B'] = [st.enter_context(nc.psum_tensor('B%d_%d' % (_uid[0], i), [128, 512], F32)) for i in range(8)]
            return W

        def bfv(bank, a, b):
            return bank[:].bitcast(BF16).rearrange('p (a b) -> p a b', a=a, b=b)

        def rstd_of(junk, x, n, width, ssq, rstd):
            S.add('dve', lambda e: e.scalar_tensor_tensor(out=junk[:n, 0:width], in0=x[:n], scalar=1.0, in1=x[:n], op0=ALU.mult, op1=ALU.mult, accum_out=ssq[:n]),
                  reads=[x[:n]], writes=[junk[:n, 0:width], ssq[:n]])
            kk.act(rstd[:n], ssq[:n], AF.Ln, scale=1.0 / width, bias=EPS)
            kk.act(rstd[:n], rstd[:n], AF.Exp, scale=-0.5)

        def norm_T(W, xt, n, xnT=None):
            xnT = W['xnT'] if xnT is None else xnT
            rstd_of(W['junk'], xt, n, D, W['ssq'], W['rstd'])
            kk.stt(W['xn'][:n], xt[:n], W['rstd'][:n, 0:1], W['gain'][:n], ALU.mult, ALU.mult)
            psT = bfv(W['B'][0], 8, 128)
            for kc in range(8):
                kk.tr(psT[:, kc, :n], W['xn'][:n, kc * 128:(kc + 1) * 128], ident_bf[:n, :n])
            kk.cpr(xnT[:, :, :n], psT[:, :, :n])

        def load_w(dst, src2d, stg, K_, w):
            kk.dma(stg[:, :K_, :w], src2d.rearrange('(k p) w -> p k w', p=128))
            kk.cpr(dst, stg[:, :K_, :w])

        WGS = [nc.dram_tensor('wgs%d' % l_, [22, 128, 1024], BF16, kind="Internal").ap() for l_ in range(2)]
        WUS = [nc.dram_tensor('wus%d' % l_, [22, 128, 1024], BF16, kind="Internal").ap() for l_ in range(2)]

        def ffn_precast(l_, st, q_='sp', ce=('act', 'dve')):
            sg_ = sb(st, 'pcg%d' % l_, [128, 8, 128])
            su_ = sb(st, 'pcu%d' % l_, [128, 8, 128])
            bg_ = sb(st, 'pbg%d' % l_, [128, 1024], BF16)
            bu_ = sb(st, 'pbu%d' % l_, [128, 1024], BF16)
            for f in range(22):
                kk.dma(sg_[:], I['wg'][l_][:, f * 128:(f + 1) * 128].rearrange('(k p) w -> p k w', p=128), q=q_)
                kk.dma(su_[:], I['wu'][l_][:, f * 128:(f + 1) * 128].rearrange('(k p) w -> p k w', p=128), q=q_)
                kk.cp(bg_[:], sg_[:].rearrange('p k w -> p (k w)'), eng=ce[0])
                kk.cp(bu_[:], su_[:].rearrange('p k w -> p (k w)'), eng=ce[1])
                kk.dma(WGS[l_][f], bg_[:], q=q_)
                kk.dma(WUS[l_][f], bu_[:], q=q_)
                yield

        with ExitStack() as ph:
            W = common_work(ph)
            WA = sb(ph, 'WA', [128, 8, 1304], BF16)
            with ExitStack() as ld:
                stg = [sb(ld, 'stgA0', [128, 8, 512]), sb(ld, 'stgA1', [128, 8, 512])]
                segs = [(0, 512, 0), (768, 896, 512), (1024, 1152, 640), (512, 768, 768), (896, 1024, 1024), (1152, 1280, 1152), (1280, 1304, 1280)]
                for i, (s0, s1, d0) in enumerate(segs):
                    load_w(WA[:, :, d0:d0 + (s1 - s0)], I['win'][:, s0:s1], stg[i % 2], 8, s1 - s0)
                kk.dma(W['gain'][:], dram_bcast(I['norm_mix'][0], 128, D))
                S.barrier(); S.flush()
            PA = []
            for par in range(2):
                d = {}
                d['proj'] = sb(ph, 'projA%d' % par, [128, 1304])
                d['rot'] = sb(ph, 'rot%d' % par, [128, 768])
                d['rtab'] = sb(ph, 'rtab%d' % par, [128, 64])
                for nm in ('r1', 'r2', 'r3', 'r4'):
                    d[nm] = sb(ph, nm + '_%d' % par, [128, 12, 32])
                d['tb'] = sb(ph, 'tb%d' % par, [128, 12, 128], BF16)
                if par == 0:
                    d['xn'], d['xnT'], d['junk'] = W['xn'], W['xnT'], W['junk']
                else:
                    d['xn'] = sb(ph, 'xnA%d' % par, [128, D], BF16)
                    d['xnT'] = sb(ph, 'xnTA%d' % par, [128, 8, 128], BF16)
                    d['junk'] = sb(ph, 'junkA%d' % par, [128, D], BF16)
                d['ssq'] = sb(ph, 'ssqA%d' % par, [128, 1])
                d['rstd'] = sb(ph, 'rstdA%d' % par, [128, 1])
                PA.append(d)

            def bodyA(i):
                (ti, n, pos0, row0, sb_) = tiles[i]
                P = PA[i % 2]
                pa, pb, pc, pd = [W['B'][4 * (i % 2) + k_] for k_ in range(4)]
                proj, rot, rtab, r1, r2, r3, r4, tb = P['proj'], P['rot'], P['rtab'], P['r1'], P['r2'], P['r3'], P['r4'], P['tb']
                xt = W['xt'][i % 2]
                src = I['xp'][row0:row0 + n, :] if sb_ is None else I['xs'][sb_:sb_ + 1, :]
                kk.dma(xt[:n], src)
                kk.dma(rtab[:n], I['ropetab'][row0:row0 + n, :])
                rstd_of(P['junk'], xt, n, D, P['ssq'], P['rstd'])
                yield
                kk.stt(P['xn'][:n], xt[:n], P['rstd'][:n, 0:1], W['gain'][:n], ALU.mult, ALU.mult)
                yield
                psT = bfv(pa, 8, 128)
                for kc in range(8):
                    kk.tr(psT[:, kc, :n], P['xn'][:n, kc * 128:(kc + 1) * 128], ident_bf[:n, :n])
                yield
                kk.cp(P['xnT'][:, :, :n], psT[:, :, :n], eng='act')
                yield
                xnT = P['xnT']
                blks = [(0, 512, pb), (512, 1024, pc), (1024, 1304, pd)]
                for (c0, c1, pbk) in blks:
                    for kc in range(8):
                        kk.mm(pbk[:n, 0:c1 - c0], xnT[:, kc, :n], WA[:, kc, c0:c1], start=(kc == 0), stop=(kc == 7))
                yield
                kk.cp(proj[:n, 0:512], pb[:n, 0:512], eng='dve')
                kk.cp(proj[:n, 512:1024], pc[:n, 0:512], eng='act')
                kk.cp(proj[:n, 1024:1304], pd[:n, 0:280], eng='act')
                yield
                pv = proj[:n, 0:768].rearrange('p (h t d) -> p h t d', h=12, t=2, d=32)
                rv = rot[:n, :].rearrange('p (h t d) -> p h t d', h=12, t=2, d=32)
                cosb = bc(rtab[:n, 0:32].rearrange('p (o d) -> p o d', o=1), [n, 12, 32])
                sinb = bc(rtab[:n, 32:64].rearrange('p (o d) -> p o d', o=1), [n, 12, 32])
                kk.tt(r1[:n], pv[:, :, 0, :], cosb, ALU.mult)
                kk.tt(r3[:n], pv[:, :, 1, :], cosb, ALU.mult)
                kk.tt(r2[:n], pv[:, :, 1, :], sinb, ALU.mult, eng='pool')
                kk.tt(r4[:n], pv[:, :, 0, :], sinb, ALU.mult, eng='pool')
                gdst = G[:n, ti, :] if sb_ is None else Gs[0:1, sb_, :]
                kk.act(gdst, proj[:n, 1280:1304], AF.Exp, scale=-1.0)
                kk.ts(gdst, gdst, 1.0, None, ALU.add)
                kk.recip(gdst, gdst)
                if sb_ is None:
                    kk.cp(VS[:n, ti, 0, :], proj[:n, 1024:1152], eng='act')
                    kk.cp(VS[:n, ti, 1, :], proj[:n, 1152:1280], eng='act')
                else:
                    kk.cp(VSs[0:1, sb_, 0, :], proj[:n, 1024:1152], eng='act')
                    kk.cp(VSs[0:1, sb_, 1, :], proj[:n, 1152:1280], eng='act')
                kk.cp(tb[:n, 0:8, 0:64], proj[:n, 0:512].rearrange('p (h d) -> p h d', h=8), eng='act')
                kk.cp(tb[:n, 8:10, 0:64], proj[:n, 768:896].rearrange('p (h d) -> p h d', h=2), eng='act')
                kk.cp(tb[:n, 10:12, 0:64], proj[:n, 896:1024].rearrange('p (h d) -> p h d', h=2), eng='act')
                yield
                kk.tt(rv[:, :, 0, :], r1[:n], r2[:n], ALU.subtract)
                kk.tt(rv[:, :, 1, :], r3[:n], r4[:n], ALU.add)
                kk.cp(tb[:n, 0:8, 64:128], rot[:n, 0:512].rearrange('p (h d) -> p h d', h=8), eng='dve')
                kk.cp(tb[:n, 8:10, 64:128], rot[:n, 512:640].rearrange('p (h d) -> p h d', h=2), eng='dve')
                kk.cp(tb[:n, 10:12, 64:128], rot[:n, 640:768].rearrange('p (h d) -> p h d', h=2), eng='dve')
                yield
                if sb_ is None:
                    kk.dma(O['kvc_p'][row0:row0 + n, :], proj[:n, 768:1024], q='pool')
                    kk.dma(O['kvs_p'][row0:row0 + n, 0:128], rot[:n, 512:640], q='pool')
                    kk.dma(O['kvs_p'][row0:row0 + n, 128:256], proj[:n, 1024:1152], q='pool')
                    if ti >= NT - 4:
                        r = row0 - (T - 512)
                        kk.dma(O['kvw_p'][r:r + n, 0:128], rot[:n, 640:768], q='pool')
                        kk.dma(O['kvw_p'][r:r + n, 128:256], proj[:n, 1152:1280], q='pool')
                else:
                    kk.dma(O['kvc_s'][sb_:sb_ + 1, :], proj[:n, 768:1024], q='pool')
                    kk.dma(O['kvs_s'][sb_:sb_ + 1, 0:128], rot[:n, 512:640], q='pool')
                    kk.dma(O['kvs_s'][sb_:sb_ + 1, 128:256], proj[:n, 1024:1152], q='pool')
                    kk.dma(O['kvw_s'][sb_, 511:512, 0:128], rot[:n, 640:768], q='pool')
                    kk.dma(O['kvw_s'][sb_, 511:512, 128:256], proj[:n, 1152:1280], q='pool')
                    kk.dma(O['kvw_s'][sb_, 0:511, :], I['cwin'][sb_, 1:512, :], q='pool')
                psA = bfv(pa, 8, 128)
                psB = bfv(pd, 8, 128)
                for j in range(8):
                    kk.tr(psA[:, j, :n], tb[:n, j, :], ident_bf[:n, :n])
                for j in range(4):
                    kk.tr(psB[:, j, :n], tb[:n, 8 + j, :], ident_bf[:n, :n])
                yield
                if sb_ is None:
                    kk.cp(QT[:, :, row0:row0 + n], psA[:, :, :n], eng='act')
                    kk.cp(KA[:, :, row0:row0 + n], psB[:, 0:2, :n], eng='dve')
                    kk.cp(KB[:, :, row0:row0 + n], psB[:, 2:4, :n], eng='dve')
                else:
                    kk.cp(QTs[:, :, sb_:sb_ + 1], psA[:, :, :n], eng='act')
                    kk.cp(KAs[:, :, sb_:sb_ + 1], psB[:, 0:2, :n], eng='dve')
                    kk.cp(KBs[:, :, sb_:sb_ + 1], psB[:, 2:4, :n], eng='dve')
                yield

            HALF_A = 5
            cur = bodyA(0)
            for _ in range(HALF_A):
                next(cur)
            for i in range(1, len(tiles)):
                nxt = bodyA(i)
                done = False
                while not done:
                    try:
                        next(cur)
                    except StopIteration:
                        done = True
                    if not done:
                        try:
                            next(nxt)
                        except StopIteration:
                            pass
                cur = nxt
            for _ in cur:
                pass
            S.barrier(); S.flush()

        with ExitStack() as ph:
            W = common_work(ph)
            WB = sb(ph, 'WB', [128, 8, 1544], BF16)
            cw_f = sb(ph, 'cw_f', [128, 8, 4])
            cbias_f = sb(ph, 'cbias_f', [128, 8])
            dtb_b = sb(ph, 'dtb_b', [128, 8])
            Aneg_b = sb(ph, 'Aneg_b', [128, 8])
            D_b = sb(ph, 'D_b', [128, 8])
            snorm_b = sb(ph, 'snorm_b', [128, 512])
            with ExitStack() as ld:
                stg = [sb(ld, 'stgB0', [128, 8, 512]), sb(ld, 'stgB1', [128, 8, 512])]
                segs = [(1304, 1816, 0), (2840, 2848, 512), (1816, 2328, 520), (2328, 2840, 1032)]
                for i, (s0, s1, d0) in enumerate(segs):
                    load_w(WB[:, :, d0:d0 + (s1 - s0)], I['win'][:, s0:s1], stg[i % 2], 8, s1 - s0)
                kk.dma(W['gain'][:], dram_bcast(I['norm_mix'][0], 128, D))
                for k_ in range(4):
                    kk.dma(cw_f[:, :, k_], I['cw'][k_].rearrange('(c p) -> p c', p=128), slow=True)
                kk.dma(cbias_f[:], I['cb'].rearrange('(c p) -> p c', p=128), slow=True)
                kk.dma(dtb_b[:], dram_bcast(I['dtb'], 128, 8))
                kk.dma(Aneg_b[:], dram_bcast(I['alog'], 128, 8))
                kk.dma(D_b[:], dram_bcast(I['dsk'], 128, 8))
                kk.dma(snorm_b[:], dram_bcast(I['snorm'], 128, 512))
                kk.act(Aneg_b[:], Aneg_b[:], AF.Exp)
                kk.ts(Aneg_b[:], Aneg_b[:], -1.0, None, ALU.mult)
                S.barrier(); S.flush()
            hT = sb(ph, 'hT', [128, 512])
            hT_bf = sb(ph, 'hT_bf', [128, 512], BF16)
            hio = sb(ph, 'hio', [128, 4, 128])
            PB = []
            for par in range(2):
                d = {}
                d['proj'] = sb(ph, 'projB%d' % par, [128, 520])
                d['cbuf'] = sb(ph, 'cbuf%d' % par, [128, 8, 131])
                d['c3'] = sb(ph, 'c3%d' % par, [128, 8, 3])
                d['ca'] = sb(ph, 'ca%d' % par, [128, 8, 128])
                d['cb2'] = sb(ph, 'cb2%d' % par, [128, 8, 128])
                d['xbcT'] = sb(ph, 'xbcT%d' % par, [128, 8, 128], BF16)
                d['xsB'] = sb(ph, 'xsB%d' % par, [128, 768], BF16)
                for nm in ('dt_t', 'a_t', 'e_t', 'd_t', 'gdec'):
                    d[nm] = sb(ph, nm + str(par), [128, 8])
                d['xdt'] = sb(ph, 'xdt%d' % par, [128, 512], BF16)
                d['xdtd'] = sb(ph, 'xdtd%d' % par, [128, 512], BF16)
                d['GTm'] = sb(ph, 'GTm%d' % par, [128, 2, 128])
                d['aL'] = sb(ph, 'aL%d' % par, [128, 8, 128])
                d['LT'] = sb(ph, 'LT%d' % par, [128, 8, 128], BF16)
                d['WT'] = sb(ph, 'WT%d' % par, [128, 8, 128], BF16)
                d['yt'] = sb(ph, 'yt%d' % par, [128, 512])
                if par == 0:
                    d['xn'], d['xnT'], d['junk'] = W['xn'], W['xnT'], W['junk']
                else:
                    d['xn'] = sb(ph, 'xnp%d' % par, [128, D], BF16)
                    d['xnT'] = sb(ph, 'xnTp%d' % par, [128, 8, 128], BF16)
                    d['junk'] = sb(ph, 'junkp%d' % par, [128, D], BF16)
                d['ssq'] = sb(ph, 'ssqp%d' % par, [128, 1])
                d['rstd'] = sb(ph, 'rstdp%d' % par, [128, 1])
                d['gain'] = W['gain']
                d['B'] = W['B'][4 * par:4 * par + 4] + W['B'][4 * par:4 * par + 4]
                PB.append(d)
            BALL = W['B']

            def state_out(dst, pd):
                for c in range(4):
                    kk.tr(pd[:, c * 128:(c + 1) * 128], hT[:, c * 128:(c + 1) * 128], ident_f)
                kk.cp(hio[:].rearrange('p c n -> p (c n)'), pd[:, :])
                kk.dma(dst.rearrange('(c p) n -> p c n', p=128), hio[:], q='pool')

            def body(i):
                (ti, n, pos0, row0, sb_) = tiles[i]
                P = PB[i % 2]
                pa, pb, pc, pd = BALL[4 * (i % 2)], BALL[4 * (i % 2) + 1], BALL[4 * (i % 2) + 2], BALL[4 * (i % 2) + 3]
                proj, cbuf, c3, ca, cb2, xbcT, xsB = P['proj'], P['cbuf'], P['c3'], P['ca'], P['cb2'], P['xbcT'], P['xsB']
                dt_t, a_t, e_t, d_t, gdec = P['dt_t'], P['a_t'], P['e_t'], P['d_t'], P['gdec']
                xdt, xdtd, GTm, aL, LT, WT, yt = P['xdt'], P['xdtd'], P['GTm'], P['aL'], P['LT'], P['WT'], P['yt']
                y2 = cb2[:, 0:4, :].rearrange('p a b -> p (a b)')
                zs = ca[:, 0:4, :].rearrange('p a b -> p (a b)')
                xt = W['xt'][i % 2]
                src = I['xp'][row0:row0 + n, :] if sb_ is None else I['xs'][sb_:sb_ + 1, :]
                kk.dma(xt[:n], src)
                if ti == 0:
                    kk.memset(cbuf[:, :, 0:3], 0.0)
                    kk.memset(hT[:], 0.0)
                    kk.memset(hT_bf[:], 0.0)
                if sb_ is not None:
                    for j_ in range(3):
                        kk.dma(cbuf[:, :, j_], I['sconv'][sb_, j_].rearrange('(c p) -> p c', p=128), slow=True)
                rstd_of(P['junk'], xt, n, D, P['ssq'], P['rstd'])
                yield
                kk.stt(P['xn'][:n], xt[:n], P['rstd'][:n, 0:1], P['gain'][:n], ALU.mult, ALU.mult)
                yield
                psT = bfv(pa, 8, 128)
                for kc in range(8):
                    kk.tr(psT[:, kc, :n], P['xn'][:n, kc * 128:(kc + 1) * 128], ident_bf[:n, :n])
                yield
                kk.cp(P['xnT'][:, :, :n], psT[:, :, :n], eng='act')
                yield
                xnT = P['xnT']
                for fc in range(8):
                    pbk = pc if fc < 4 else pd
                    for kc in range(8):
                        kk.mm(pbk[:, (fc % 4) * 128:(fc % 4) * 128 + n], WB[:, kc, 520 + fc * 128:520 + (fc + 1) * 128], xnT[:, kc, :n], start=(kc == 0), stop=(kc == 7))
                for kc in range(8):
                    kk.mm(pb[:n, 0:8], xnT[:, kc, :n], WB[:, kc, 512:520], start=(kc == 0), stop=(kc == 7))
                for kc in range(8):
                    kk.mm(pa[:n, 0:512], xnT[:, kc, :n], WB[:, kc, 0:512], start=(kc == 0), stop=(kc == 7))
                yield
                kk.cp(cbuf[:, 0:4, 3:3 + n], pc[:, :].rearrange('p (c t) -> p c t', c=4)[:, :, :n], eng='dve')
                kk.cp(cbuf[:, 4:8, 3:3 + n], pd[:, :].rearrange('p (c t) -> p c t', c=4)[:, :, :n], eng='act')
                kk.tt(dt_t[:n], pb[:n, 0:8], dtb_b[:n], ALU.add)
                kk.cp(proj[:n, 0:512], pa[:n, 0:512], eng='act')
                yield
                kk.act(dt_t[:n], dt_t[:n], AF.Exp)
                kk.act(dt_t[:n], dt_t[:n], AF.Ln, bias=1.0)
                kk.tt(cb2[:, :, :n], cbuf[:, :, 1:1 + n], bc(cw_f[:, :, 1:2], [128, 8, n]), ALU.mult, eng='pool')
                kk.tt(ca[:, :, :n], cbuf[:, :, 0:n], bc(cw_f[:, :, 0:1], [128, 8, n]), ALU.mult)
                yield
                if sb_ is not None or ti == NT - 1:
                    kk.cp(c3[:], cbuf[:, :, n:n + 3], eng='pool')
                    dst = O['sconv_p'] if sb_ is None else O['sconv_s'][sb_]
                    for j_ in range(3):
                        kk.dma(dst[j_].rearrange('(c p) -> p c', p=128), c3[:, :, j_], q='pool', slow=True)
                if sb_ is None and ti < NT - 1:
                    kk.cp(PB[(i + 1) % 2]['cbuf'][:, :, 0:3], cbuf[:, :, n:n + 3], eng='pool')
                kk.tt(a_t[:n], dt_t[:n], Aneg_b[:n], ALU.mult)
                kk.tt(ca[:, :, :n], ca[:, :, :n], cb2[:, :, :n], ALU.add)
                kk.tt(cb2[:, :, :n], cbuf[:, :, 2:2 + n], bc(cw_f[:, :, 2:3], [128, 8, n]), ALU.mult, eng='pool')
                yield
                kk.mm(pb[:n, 8:16], Utri[:n, :n], a_t[:n, :])
                kk.mm(pb[:, 16:24], ones_f[:n, :], a_t[:n, :])
                kk.tt(aL[:n, :, :n], bc(Lstrict[:n, :n].rearrange('p (o l) -> p o l', o=1), [n, 8, n]), bc(a_t[:n].rearrange('p (h o) -> p h o', o=1), [n, 8, n]), ALU.mult)
                kk.tt(ca[:, :, :n], ca[:, :, :n], cb2[:, :, :n], ALU.add)
                kk.tt(cb2[:, :, :n], cbuf[:, :, 3:3 + n], bc(cw_f[:, :, 3:4], [128, 8, n]), ALU.mult, eng='pool')
                yield
                kk.act(e_t[:n], pb[:n, 8:16], AF.Exp)
                kk.act(gdec[:], pb[:, 16:24], AF.Exp)
                kk.cp(d_t[:n], pb[:n, 8:16])
                kk.tt(d_t[:n], pb[:n, 16:24], d_t[:n], ALU.subtract)
                for h in range(8):
                    pbk = pc if h < 4 else pd
                    kk.mm(pbk[:n, (h % 4) * 128:(h % 4) * 128 + n], aL[:n, h, :n], Utri[:n, :n])
                kk.tt(ca[:, :, :n], ca[:, :, :n], cb2[:, :, :n], ALU.add)
                yield
                kk.act(d_t[:n], d_t[:n], AF.Exp)
                kk.act(LT[:n, 0:4, :n], pc[:n, :].rearrange('p (c t) -> p c t', c=4)[:, :, :n], AF.Exp)
                kk.act(LT[:n, 4:8, :n], pd[:n, :].rearrange('p (c t) -> p c t', c=4)[:, :, :n], AF.Exp)
                for fc in range(8):
                    kk.act(xbcT[:, fc, :n], ca[:, fc, :n], AF.Silu, bias=cbias_f[:, fc:fc + 1])
                kk.act(zs[:n], proj[:n, 0:512], AF.Silu)
                yield
                psT2 = bfv(pa, 8, 128)
                for fc in range(6):
                    kk.tr(psT2[:n, fc, :], xbcT[:, fc, :n], ident_bf[:, :])
                for g in range(2):
                    kk.mm(pb[:n, 128 + g * 128:128 + g * 128 + n], xbcT[:, 4 + g, :n], xbcT[:, 6 + g, :n])
                yield
                kk.cp(xsB[:n, :], psT2[:n, 0:6, :].rearrange('p c f -> p (c f)'), eng='act')
                kk.tt(GTm[:n, :, :n], pb[:n, 128:384].rearrange('p (g l) -> p g l', g=2)[:, :, :n], bc(Utri[:n, :n].rearrange('p (o l) -> p o l', o=1), [n, 2, n]), ALU.mult)
                for g in range(2):
                    kk.tt(WT[:n, g * 4:(g + 1) * 4, :n], LT[:n, g * 4:(g + 1) * 4, :n], bc(GTm[:n, g:g + 1, :n], [n, 4, n]), ALU.mult)
                yield
                xs_v = xsB[:n, 0:512].rearrange('p (h d) -> p h d', h=8)
                kk.tt(xdt[:n].rearrange('p (h d) -> p h d', h=8), xs_v, bc(dt_t[:n].rearrange('p (h o) -> p h o', o=1), [n, 8, 64]), ALU.mult)
                kk.tt(xdtd[:n].rearrange('p (h d) -> p h d', h=8), xdt[:n].rearrange('p (h d) -> p h d', h=8), bc(d_t[:n].rearrange('p (h o) -> p h o', o=1), [n, 8, 64]), ALU.mult)
                kk.tt(y2[:n].rearrange('p (h d) -> p h d', h=8), xs_v, bc(D_b[:n].rearrange('p (h o) -> p h o', o=1), [n, 8, 64]), ALU.mult, eng='pool')
                yield
                if sb_ is not None:
                    kk.dma(hio[:], I['sssm'][sb_].rearrange('(c p) n -> p c n', p=128))
                    for c in range(4):
                        kk.tr(pd[:, c * 128:(c + 1) * 128], hio[:, c, :], ident_f)
                    kk.cp(hT[:], pd[:, :])
                    kk.cp(hT_bf[:], pd[:, :], eng='act')
                    yield
                for h in range(8):
                    kk.mm(pa[:n, h * 64:(h + 1) * 64], WT[:n, h, :n], xdt[:n, h * 64:(h + 1) * 64])
                for g in range(2):
                    kk.mm(pc[:n, g * 256:(g + 1) * 256], xbcT[:, 6 + g, :n], hT_bf[:, g * 256:(g + 1) * 256])
                for g in range(2):
                    kk.mm(pd[:, g * 256:(g + 1) * 256], xsB[:n, 512 + g * 128:512 + (g + 1) * 128], xdtd[:n, g * 256:(g + 1) * 256])
                yield
                kk.tt(yt[:n].rearrange('p (h d) -> p h d', h=8), pc[:n, :].rearrange('p (h d) -> p h d', h=8), bc(e_t[:n].rearrange('p (h o) -> p h o', o=1), [n, 8, 64]), ALU.mult)
                kk.tt(yt[:n], yt[:n], pa[:n, :], ALU.add)
                kk.tt(hT[:].rearrange('p (h d) -> p h d', h=8), hT[:].rearrange('p (h d) -> p h d', h=8), bc(gdec[:].rearrange('p (h o) -> p h o', o=1), [128, 8, 64]), ALU.mult)
                kk.tt(hT[:], hT[:], pd[:, :], ALU.add)
                kk.tt(yt[:n], yt[:n], y2[:n], ALU.add)
                kk.tt(yt[:n], yt[:n], zs[:n], ALU.mult)
                yield
                kk.cp(hT_bf[:], hT[:], eng='act')
                rstd_of(P['junk'], yt, n, 512, P['ssq'], P['rstd'])
                if sb_ is not None:
                    state_out(O['ssm_s'][sb_], pd)
                elif ti == NT - 1:
                    state_out(O['ssm_p'], pd)
                yield
                ycd = YC[:n, ti, :] if sb_ is None else YCs[0:1, sb_, :]
                kk.stt(ycd, yt[:n], P['rstd'][:n, 0:1], snorm_b[:n], ALU.mult, ALU.mult)
                yield

            HALF = 9
            cur = body(0)
            for _ in range(HALF):
                next(cur)
            for i in range(1, len(tiles)):
                nxt = body(i)
                rr(cur, nxt) if False else None
                done = False
                while not done:
                    try:
                        next(cur)
                    except StopIteration:
                        done = True
                    if not done:
                        try:
                            next(nxt)
                        except StopIteration:
                            pass
                cur = nxt
            for _ in cur:
                pass
            S.barrier(); S.flush()

        with ExitStack() as ph:
            W = common_work(ph)
            B = W['B']
            kcT = sb(ph, 'kcT', [64, 2, 128], BF16)
            vcO = sb(ph, 'vcO', [128, 2, 96], BF16)
            Wout = sb(ph, 'Wout', [128, 8, D], BF16)
            with ExitStack() as ld:
                stg = [sb(ld, 'stgC0', [128, 8, 512]), sb(ld, 'stgC1', [128, 8, 512])]
                W1 = [sb(ld, 'W1k', [64, 32, 256], BF16), sb(ld, 'W1v', [64, 32, 256], BF16)]
                W2 = [sb(ld, 'W2k', [128, 2, 64], BF16), sb(ld, 'W2v', [128, 2, 64], BF16)]
                peT = sb(ld, 'peT', [64, 2, 32])
                peTb = sb(ld, 'peTb', [64, 2, 32], BF16)
                biasv = sb(ld, 'biasv', [128, 2, 2])
                Hs = sb(ld, 'Hs', [128, 2, 2, 2, 128], BF16)
                ovl_t = sb(ld, 'ovl_t', [128, 32])
                for kv, (w1n, w2n, pen) in enumerate([('w1k', 'w2k', 'pek'), ('w1v', 'w2v', 'pev')]):
                    for hh in range(2):
                        sv = stg[hh].rearrange('p a (b c) -> p (a b) c', b=2)
                        kk.dma(sv[0:64, :, :], I[w1n][hh * 1024:(hh + 1) * 1024, :].rearrange('(hs d) f -> d hs f', d=64))
                        kk.cpr(W1[kv][:, hh * 16:(hh + 1) * 16, :], sv[0:64, :, :])
                    kk.dma(stg[0][:, 0:2, 0:64], I[w2n].rearrange('(k p) w -> p k w', p=128))
                    kk.cpr(W2[kv][:], stg[0][:, 0:2, 0:64])
                    for hs in range(32):
                        kk.dma(peT[:, kv, hs:hs + 1], I[pen][hs].rearrange('(d o) -> d o', o=1), q='pool', slow=True)
                kk.cp(peTb[:], peT[:])
                kk.dma(ovl_t[:], I['ovl'])
                for c0 in range(0, D, 512):
                    load_w(Wout[:, :, c0:c0 + 512], I['wout'][:, c0:c0 + 512], stg[(c0 // 512) % 2], 8, 512)
                kk.memset(Hs[:], 0.0)
                for kv in range(2):
                    HT = KA if kv == 0 else KB
                    for fc in range(2):
                        for hs in range(32):
                            kk.mm(B[0][:, 0:1], W1[kv][:, hs, fc * 128:(fc + 1) * 128], peTb[:, kv, hs:hs + 1], start=(hs == 0), stop=(hs == 31))
                        kk.cp(biasv[:, kv, fc:fc + 1], B[0][:, 0:1])
                        for g in range(2):
                            pb = B[1 + (g % 2)]
                            for hs in range(32):
                                kk.mm(pb[:, 0:127], W1[kv][:, hs, fc * 128:(fc + 1) * 128], HT[0:64, g, hs:hs + 16 * 126 + 1:16], start=(hs == 0), stop=(hs == 31))
                            kk.act(Hs[:, kv, g, fc, 0:127], pb[:, 0:127], AF.Silu, bias=biasv[:, kv, fc:fc + 1])
                kk.memset(vcO[:], 0.0)
                kk.memset(kcT[:], 0.0)
                for g in range(2):
                    for fc in range(2):
                        kk.mm(B[3][0:64, 0:127], W2[0][:, fc, :], Hs[:, 0, g, fc, 0:127], start=(fc == 0), stop=(fc == 1))
                    kk.cp(kcT[:, g, 0:127], B[3][0:64, 0:127])
                    for fc in range(2):
                        kk.mm(B[4][0:127, 0:64], Hs[:, 1, g, fc, 0:127], W2[1][:, fc, :], start=(fc == 0), stop=(fc == 1))
                    kk.cp(vcO[0:127, g, 0:64], B[4][0:127, 0:64])
                    kk.cp(vcO[0:127, g, 64:96], ovl_t[0:127, :])
                S.barrier(); S.flush()

            Mc = sb(ph, 'Mc', [128, 128])
            Vm = sb(ph, 'Vm', [128, 32])
            Bm = sb(ph, 'Bm', [128, 32])
            nB = sb(ph, 'nB', [128, 32])
            Vm1 = sb(ph, 'Vm1', [128, 32])
            impacc = sb(ph, 'impacc', [128, 2, 32])
            imp4 = sb(ph, 'imp4', [128, 4, 32])
            imp2 = sb(ph, 'imp2', [128, 32])
            imp3 = sb(ph, 'imp3', [128, 32])
            m8 = sb(ph, 'm8', [128, 8])
            sel = sb(ph, 'sel', [128, 32])
            negsel = sb(ph, 'negsel', [128, 2, 32])
            Mfull = sb(ph, 'Mfull', [128, 2, T], BF16)
            Mwin = sb(ph, 'Mwin', [128, 640], BF16)
            mxa = [sb(ph, 'mxa0', [128, 4]), sb(ph, 'mxa1', [128, 4])]
            scs = [sb(ph, 'sc0', [128, T]), sb(ph, 'sc1', [128, T])]
            pbfs = [sb(ph, 'pbf0', [128, T], BF16), sb(ph, 'pbf1', [128, T], BF16)]
            pTs = [sb(ph, 'pT0', [128, 16, 128], BF16), sb(ph, 'pT1', [128, 16, 128], BF16)]
            mxs = [sb(ph, 'mx0', [128, 1]), sb(ph, 'mx1', [128, 1])]
            negms = [sb(ph, 'negm0', [128, 1]), sb(ph, 'negm1', [128, 1])]
            rss = [sb(ph, 'rs0', [128, 1]), sb(ph, 'rs1', [128, 1]), sb(ph, 'rs2', [128, 1])]
            coef = sb(ph, 'coef', [128, 1])
            coef4 = sb(ph, 'coef4', [128, 4])
            accs = [sb(ph, 'acc0', [128, 8, 64]), sb(ph, 'acc1', [128, 8, 64])]
            catb = sb(ph, 'catb', [128, D], BF16)
            catT = sb(ph, 'catT', [128, 8, 128], BF16)
            xo = sb(ph, 'xo', [128, D])
            csc = sb(ph, 'csc', [128, 4, 128])
            cpbf = sb(ph, 'cpbf', [128, 4, 128], BF16)
            cpT = sb(ph, 'cpT', [128, 4, 128], BF16)
            cmx = sb(ph, 'cmx', [128, 4])
            cnegm = sb(ph, 'cnegm', [128, 4])
            crs = sb(ph, 'crs', [128, 4])
            qk_i = [0]
            tr_i = [0]
            causal_bf = sb(ph, 'causal_bf', [128, 128], BF16)
            kk.cp(causal_bf[:], causal, eng='act')
            kk.memset(Mwin[:], 0.0)
            kk.cp(Mwin[:, 0:128], wlo, eng='pool')
            kk.cp(Mwin[:, 512:640], causal, eng='pool')
            kk.memset(negsel[:], 0.0)
            kk.memset(pbfs[0][:], 0.0)
            kk.memset(pbfs[1][:], 0.0)
            n = 128

            def pre_stages(t, g):
                pos0 = t * 128
                tok = slice(pos0, pos0 + n)
                acc = accs[t % 2]
                nk = (t + 1) * 128

                def p0():
                    if g == 0:
                        kk.dma(W['xt'][t % 2][:n], I['xp'][pos0:pos0 + n, :])
                        kk.memset(Mc[:], NEG)
                        kk.memset(Mc[:, 0:127], 0.0)
                        S.add('pool', lambda e: e.affine_select(out=Mc[:, 0:127], in_=Mc[:, 0:127], pattern=[[-16, 127]], compare_op=ALU.is_ge, fill=NEG, base=pos0 - 31, channel_multiplier=1), reads=[Mc[:, 0:127]], writes=[Mc[:, 0:127]])
                        if t >= 8:
                            kk.memset(Vm[:], 1.0)
                            S.add('pool', lambda e: e.affine_select(out=Vm[:], in_=Vm[:], pattern=[[-64, 32]], compare_op=ALU.is_ge, fill=0.0, base=pos0, channel_multiplier=1), reads=[Vm[:]], writes=[Vm[:]])
                            S.add('pool', lambda e: e.affine_select(out=Bm[:], in_=Vm[:], pattern=[[64, 32]], compare_op=ALU.is_ge, fill=0.0, base=127 - pos0, channel_multiplier=-1), reads=[Vm[:]], writes=[Bm[:]])
                            kk.memset(Bm[:, 0:1], 1.0)
                            kk.ts(nB[:], Bm[:], -1.0, 1.0, ALU.mult, ALU.add, eng='pool')
                            kk.ts(Vm1[:], Vm[:], -1.0, 1e9, ALU.add, ALU.mult, eng='pool')
                    for r in range(4):
                        kk.mm(B[0][:n, r * 128:(r + 1) * 128], QT[0:64, g * 4 + r, tok], kcT[:, g, 0:128])

                def p1():
                    kk.stt(csc[:n], B[0][:n, :].rearrange('p (h k) -> p h k', h=4), 0.125, bc(Mc[:n, :].rearrange('p (o k) -> p o k', o=1), [n, 4, 128]), ALU.mult, ALU.add)
                    S.add('dve', lambda e: e.tensor_reduce(out=cmx[:n], in_=csc[:n], axis=AX.X, op=ALU.max), reads=[csc[:n]], writes=[cmx[:n]])
                    kk.ts(cnegm[:n], cmx[:n], -30000.0, -1.0, ALU.max, ALU.mult)

                def p2():
                    for r in range(4):
                        kk.act(cpbf[:n, r, :], csc[:n, r, :], AF.Exp, bias=cnegm[:n, r:r + 1], accum=crs[:n, r:r + 1])

                def p3():
                    pst = bfv(B[0], 8, 128)
                    for r in range(4):
                        kk.tr(pst[:, r, :n], cpbf[:n, r, :], ident_bf[:n, :n])
                    kk.cp(cpT[:, :, :n], pst[:, 0:4, :n], eng='act')

                def p4():
                    for r in range(4):
                        kk.mm(B[0][:n, r * 96:(r + 1) * 96], cpT[0:127, r, :n], vcO[0:127, g, :])
                    kk.ts(crs[:n], crs[:n], 1e-30, None, ALU.max)
                    kk.recip(crs[:n], crs[:n])
                    gv = G[:n, t, :].rearrange('p (h r) -> p h r', r=3)
                    kk.tt(coef4[:n].rearrange('p (h o) -> p h o', o=1), crs[:n].rearrange('p (h o) -> p h o', o=1), gv[:, g * 4:(g + 1) * 4, 0:1], ALU.mult)
                    ov = B[0][:n, 0:384].rearrange('p (h c) -> p h c', h=4)
                    kk.tt(acc[:n, g * 4:(g + 1) * 4, :], ov[:, :, 0:64], bc(coef4[:n].rearrange('p (h o) -> p h o', o=1), [n, 4, 64]), ALU.mult)
                    kk.tt(imp4[:n], ov[:, :, 64:96], bc(crs[:n].rearrange('p (h o) -> p h o', o=1), [n, 4, 32]), ALU.mult)
                    S.add('dve', lambda e: e.tensor_reduce(out=impacc[:n, g, :], in_=imp4[:n].rearrange('p h j -> p j h'), axis=AX.X, op=ALU.add), reads=[imp4[:n]], writes=[impacc[:n, g, :]])

                def p5():
                    if t >= 8:
                        kk.tt(imp2[:n], impacc[:n, g, :], nB[:n], ALU.mult)
                        kk.stt(imp2[:n], Bm[:n], 1e9, imp2[:n], ALU.mult, ALU.add)
                        kk.tt(imp3[:n], imp2[:n], Vm[:n], ALU.mult)
                        kk.tt(imp3[:n], imp3[:n], Vm1[:n], ALU.add)
                        S.add('dve', lambda e: e.max(out=m8[:], in_=imp3[:]), reads=[imp3[:]], writes=[m8[:]])
                        S.add('dve', lambda e: e.match_replace(out=imp2[:], in_to_replace=m8[:], in_values=imp3[:], imm_value=-3e9), reads=[m8[:], imp3[:]], writes=[imp2[:]])
                        S.add('dve', lambda e: e.max(out=m8[:], in_=imp2[:]), reads=[imp2[:]], writes=[m8[:]])
                        kk.ts(sel[:n], imp3[:n], m8[:n, 7:8], None, ALU.is_ge)
                        kk.ts(negsel[:n, g, :], sel[:n], -1.0, 1e30, ALU.add, ALU.mult)

                def p6():
                    if t >= 8:
                        nblk = 2 * (t + 1)
                        S.add('act', lambda e: e.activation(out=Mfull[:n, g, 0:nk].rearrange('p (b k) -> p b k', k=64), in_=bc(negsel[:n, g, 0:nblk].rearrange('p (b o) -> p b o', o=1), [n, nblk, 64]), func=AF.Copy),
                              reads=[negsel[:n, g, 0:nblk]], writes=[Mfull[:n, g, 0:nk]])

                return [p0, p1, p2, p3, p4, p5, p6]

            def make_task(t, g, hd, br, k):
                pos0 = t * 128
                tok = slice(pos0, pos0 + n)
                acc = accs[t % 2]
                pp = k % 2
                rs = rss[k % 3]
                sc, pbf, pT, mx, negm = scs[pp], pbfs[pp], pTs[pp], mxs[pp], negms[pp]
                if br == 1:
                    jl = list(range(t + 1))
                    KT, kbase, Msk, mbase, vsel = KA, 0, Mfull[:, g, :], 0, 0
                else:
                    j0w = max(0, t - 4)
                    jl = list(range(j0w, t + 1))
                    KT, kbase, Msk, vsel = KB, j0w * 128, Mwin, 1
                    mbase = 640 - len(jl) * 128
                nkk = len(jl) * 128

                def A1():
                    ma = mxa[pp]
                    nbk = 0
                    for kb in range(0, nkk, 512):
                        w = min(512, nkk - kb)
                        qk_i[0] += 1
                        pb = B[1 + qk_i[0] % 4]
                        extra = []
                        if br == 1:
                            dg = t * 128 - kb
                            if t >= 8:
                                extra.append((pb[:n, 0:w], Msk[:n, mbase + kb:mbase + kb + w]))
                            if 0 <= dg < w:
                                extra.append((pb[:n, dg:dg + 128], causal_bf[:n, :]))
                        else:
                            extra.append((pb[:n, 0:w], Msk[:n, mbase + kb:mbase + kb + w]))
                        kk.mm(pb[:n, 0:w], QT[64:128, hd, tok], KT[64:128, g, kbase + kb:kbase + kb + w], start=True, stop=(len(extra) == 0))
                        for ei, (eo, er) in enumerate(extra):
                            kk.mm(eo, ident_bf[:n, :n], er, start=False, stop=(ei == len(extra) - 1))
                        init = -1e30 if nbk == 0 else ma[:n, nbk - 1:nbk]
                        rd = [pb[:n, 0:w]] + ([] if nbk == 0 else [ma[:n, nbk - 1:nbk]])
                        S.add('dve', lambda e, kb=kb, w=w, pb=pb, init=init, nbk=nbk: e.tensor_scalar(out=sc[:n, kb:kb + w], in0=pb[:n, 0:w], scalar1=0.125, scalar2=init, op0=ALU.mult, op1=ALU.max, accum_out=ma[:n, nbk:nbk + 1]),
                              reads=rd, writes=[sc[:n, kb:kb + w], ma[:n, nbk:nbk + 1]])
                        nbk += 1
                    kk.ts(negm[:n], ma[:n, nbk - 1:nbk], -1.0, None, ALU.mult)

                def A2():
                    kk.act(pbf[:n, 0:nkk], sc[:n, 0:nkk], AF.Exp, bias=negm[:n, 0:1], accum=rs[:n])

                def B1():
                    nb = len(jl)
                    for i0 in range(0, nb, 8):
                        tr_i[0] += 1
                        pst = bfv(B[5 + tr_i[0] % 2], 8, 128)
                        cnt = min(8, nb - i0)
                        for i in range(cnt):
                            kk.tr(pst[:, i, :n], pbf[:n, (i0 + i) * 128:(i0 + i + 1) * 128], ident_bf[:n, :n])
                        kk.cp(pT[:, i0:i0 + cnt, :n], pst[:, 0:cnt, :n], eng='act')

                def B2():
                    nb = len(jl)
                    for i, j in enumerate(jl):
                        kk.mm(B[7][:n, 0:64], pT[:, i, :n], VS[:, j, vsel, g * 64:(g + 1) * 64], start=(i == 0), stop=(i == nb - 1))
                    kk.ts(rs[:n], rs[:n], 1e-30, None, ALU.max)
                    kk.recip(rs[:n], rs[:n])
                    kk.tt(coef[:n], rs[:n], G[:n, t, hd * 3 + br:hd * 3 + br + 1], ALU.mult)
                    kk.stt(acc[:n, hd, :], B[7][:n, 0:64], coef[:n, 0:1], acc[:n, hd, :], ALU.mult, ALU.add)

                return (A1, A2, B1, B2)

            def post_stages(t):
                pos0 = t * 128
                acc = accs[t % 2]
                xt = W['xt'][t % 2]

                def q0():
                    kk.cp(catb[:n, 0:512], acc[:n].rearrange('p h d -> p (h d)'), eng='act')
                    kk.cp(catb[:n, 512:1024], YC[:n, t, :], eng='pool')

                def q1():
                    tr_i[0] += 1
                    pst = bfv(B[5 + tr_i[0] % 2], 8, 128)
                    for kc in range(8):
                        kk.tr(pst[:, kc, :n], catb[:n, kc * 128:(kc + 1) * 128], ident_bf[:n, :n])
                    kk.cp(catT[:, :, :n], pst[:, :, :n], eng='act')

                def q2():
                    for cb in range(2):
                        qk_i[0] += 1
                        pb = B[1 + qk_i[0] % 4]
                        for kc in range(8):
                            kk.mm(pb[:n, :], catT[:, kc, :n], Wout[:, kc, cb * 512:(cb + 1) * 512], start=(kc == 0), stop=(kc == 7))
                        kk.tt(xo[:n, cb * 512:(cb + 1) * 512], pb[:n, :], xt[:n, cb * 512:(cb + 1) * 512], ALU.add)
                    kk.dma(xres[pos0:pos0 + n, :], xo[:n, :], q='pool')

                return [q0, q1, q2]

            groups = [(t, g) for t in range(NT) for g in range(2)]
            NG = len(groups)
            nsteps = 8 * NG + 8
            steps = [[] for _ in range(nsteps)]
            for f_ in pre_stages(0, 0):
                f_()
            alltasks = []
            for gi, (t, g) in enumerate(groups):
                for r in range(4):
                    for br in (1, 2):
                        alltasks.append(make_task(t, g, g * 4 + r, br, len(alltasks)))
            NTK = len(alltasks)
            for k in range(NTK + 1):
                st = steps[k]
                if k < NTK:
                    st.append(alltasks[k][0])
                if k >= 1:
                    st.append(alltasks[k - 1][2])
                if k < NTK:
                    st.append(alltasks[k][1])
                if k >= 1:
                    st.append(alltasks[k - 1][3])
            for gi, (t, g) in enumerate(groups):
                base = 8 * gi
                if gi + 1 < NG:
                    for j_, f_ in enumerate(pre_stages(*groups[gi + 1])):
                        steps[base + j_].append(f_)
                if g == 1:
                    for j_, f_ in enumerate(post_stages(t)):
                        steps[base + 9 + j_].append(f_)
            preC = ffn_precast(1, ph, 'sp', ('act', 'act'))
            preC0 = ffn_precast(0, ph, 'sp', ('act', 'act'))
            for si_, st in enumerate(steps):
                for f_ in st:
                    f_()
                if si_ % 11 == 5:
                    next(preC, None)
                if si_ % 11 == 0:
                    next(preC0, None)
            for _ in preC0:
                pass
            for _ in preC:
                pass
            S.barrier(); S.flush()
        l0.close()

        with ExitStack() as ph:
            _uid[0] += 1
            B = [ph.enter_context(nc.psum_tensor('B%d_%d' % (_uid[0], i), [128, 512], F32)) for i in range(8)]
            Wout = sb(ph, 'WoutS', [128, 8, D], BF16)
            ocmp = sb(ph, 'ocmp', [1, NS, 512])
            osw = sb(ph, 'osw', [1, 2, 512])
            xt1 = sb(ph, 'xt1', [1, D])
            negsel_s = sb(ph, 'negsel_s', [4, NS, 2, 136])
            qSr = sb(ph, 'qSr', [64, 8, NS], BF16)
            kn_s = sb(ph, 'kn_s', [64, 2, NS], BF16)
            kn_w = sb(ph, 'kn_w', [64, 2, NS], BF16)
            sc_s = sb(ph, 'sc_s', [4, 8320])
            p_s = sb(ph, 'p_s', [4, 8320], BF16)
            pT_s = sb(ph, 'pT_s', [128, 65, 4], BF16)
            mx = sb(ph, 'mxs', [4, 1])
            negm = sb(ph, 'negms', [4, 1])
            rs = sb(ph, 'rss', [4, 1])
            ocn = sb(ph, 'ocn', [4, 64])
            with ExitStack() as ld:
                stg = [sb(ld, 'stgS0', [128, 8, 512]), sb(ld, 'stgS1', [128, 8, 512])]
                for c0 in range(0, D, 512):
                    load_w(Wout[:, :, c0:c0 + 512], I['wout'][:, c0:c0 + 512], stg[(c0 // 512) % 2], 8, 512)
                S.barrier(); S.flush()
            kk.dma(qSr[:], QTs[64:128, :, :])
            kk.dma(kn_s[:], KAs[64:128, :, :])
            kk.dma(kn_w[:], KBs[64:128, :, :])
            kk.memset(p_s[:], 0.0)

            def softmax4(nk):
                kk.rmax(mx[:], sc_s[:, 0:nk])
                kk.ts(negm[:], mx[:], -30000.0, -1.0, ALU.max, ALU.mult)
                kk.act(p_s[:, 0:nk], sc_s[:, 0:nk], AF.Exp, bias=negm[:, 0:1], accum=rs[:])
                kk.ts(rs[:], rs[:], 1e-30, None, ALU.max)
                kk.recip(rs[:], rs[:])

            def transposes4(nchunks):
                pst = bfv(B[4], 256, 4)
                for j in range(nchunks):
                    kk.tr(pst[:, j, :], p_s[:, j * 128:(j + 1) * 128], ident_bf[0:4, 0:4])
                kk.cpr(pT_s[:, 0:nchunks, :], pst[:, 0:nchunks, :])

            with ExitStack() as s1:
                W1s = [sb(s1, 'W1sk', [128, 16, 256], BF16), sb(s1, 'W1sv', [128, 16, 256], BF16)]
                W2s = [sb(s1, 'W2sk', [128, 2, 64], BF16), sb(s1, 'W2sv', [128, 2, 64], BF16)]
                pe128 = sb(s1, 'pe128', [128, 2, 16])
                pe128b = sb(s1, 'pe128b', [128, 2, 16], BF16)
                biasv = sb(s1, 'biasvs', [128, 2, 2])
                ovl_f = sb(s1, 'ovl_f', [128, 4, 129])
                vcO_s = sb(s1, 'vcO_s', [128, 4, 2, 200], BF16)
                with ExitStack() as ld:
                    stg = [sb(ld, 'stgT0', [128, 8, 512]), sb(ld, 'stgT1', [128, 8, 512])]
                    for kv, (w1n, w2n, pen) in enumerate([('w1k', 'w2k', 'pek'), ('w1v', 'w2v', 'pev')]):
                        sv = stg[kv].rearrange('p a (b c) -> p (a b) c', b=2)
                        kk.dma(sv, I[w1n].rearrange('(c p) f -> p c f', p=128))
                        kk.cpr(W1s[kv][:], sv)
                        kk.dma(stg[kv][:, 0:2, 0:64], I[w2n].rearrange('(k p) w -> p k w', p=128))
                        kk.cpr(W2s[kv][:], stg[kv][:, 0:2, 0:64])
                        for c_ in range(16):
                            kk.dma(pe128[:, kv, c_:c_ + 1], I[pen][2 * c_:2 * c_ + 2, :].rearrange('r (d o) -> (r d) o', o=1), q='pool', slow=True)
                    kk.cp(pe128b[:], pe128[:])
                    kk.dma(ovl_f[:], I['ovls'].rearrange('(c p) j -> p c j', p=128))
                    kk.memset(vcO_s[:], 0.0)
                    for g in range(2):
                        kk.cp(vcO_s[:, :, g, 64:193], ovl_f[:, :, :])
                    for kv in range(2):
                        for fc in range(2):
                            for c_ in range(16):
                                kk.mm(B[0][:, 0:1], W1s[kv][:, c_, fc * 128:(fc + 1) * 128], pe128b[:, kv, c_:c_ + 1], start=(c_ == 0), stop=(c_ == 15))
                            kk.cp(biasv[:, kv, fc:fc + 1], B[0][:, 0:1])
                    S.barrier(); S.flush()
                idx_i = sb(s1, 'idx_i', [128, NS * 4], I32)
                idx_f = sb(s1, 'idx_f', [128, NS * 4])
                idx_u = sb(s1, 'idx_u', [128, NS * 4], I32)
                Xg = [sb(s1, 'Xg0', [128, 4096]), sb(s1, 'Xg1', [128, 4096])]
                XT = sb(s1, 'XT', [128, 4, 8, 512], BF16)
                Xr = sb(s1, 'Xr', [128, 4, 1024], BF16)
                Hs = sb(s1, 'Hss', [128, 2, 2, 2, 512], BF16)
                kcT_s = sb(s1, 'kcT_s', [64, 2, 512], BF16)
                imph = sb(s1, 'imph', [4, 136])
                impg = sb(s1, 'impg', [4, 136])
                imp2 = sb(s1, 'imp2s', [4, 136])
                m8 = sb(s1, 'm8s', [4, 8])
                sel = sb(s1, 'sels', [4, 136])
                ccv = I['ccmp'].rearrange('(n r) c -> n (r c)', r=16)
                for b in range(NS):
                    for tt in range(4):
                        srcp = bass.AP(I['ptab'].tensor, b * 64 + tt * 16, [[1, 16], [0, 8], [1, 1]])
                        kk.dma(idx_i[:, b * 4 + tt:b * 4 + tt + 1], srcp)
                kk.cp(idx_f[:], idx_i[:])
                kk.ts(idx_f[:], idx_f[:], 8.0, cst2[:, 0:1], ALU.mult, ALU.add)
                kk.cp(idx_u[:], idx_f[:])
                def s1_gath(b):
                    for tt in range(4):
                        xg = Xg[tt % 2]
                        ic = b * 4 + tt
                        S.add('pool', lambda e, xg=xg, ic=ic: e.indirect_dma_start(out=xg[:, :], out_offset=None, in_=ccv, in_offset=bass.IndirectOffsetOnAxis(ap=idx_u[:, ic:ic + 1], axis=0)),
                              reads=[idx_u[:, ic:ic + 1], ccv], writes=[xg[:, :]], dma=True)
                        xv = xg[:, :].rearrange('p (s c d) -> p c s d', s=16, c=4, d=64)
                        for comb in range(4):
                            kk.cpr(Xr[:, comb, :].rearrange('p (s d) -> p s d', d=64), xv[:, comb, :, :])
                        for comb in range(4):
                            pst = bfv(B[1 + comb % 2], 8, 128)
                            for sp in range(8):
                                kk.tr(pst[:, sp, :], Xr[:, comb, sp * 128:(sp + 1) * 128], ident_bf[:, :])
                            kk.cpr(XT[:, comb, :, tt * 128:(tt + 1) * 128], pst[:, :, :])

                def s1_comp(b):
                    for kv in range(2):
                        for g in range(2):
                            comb = kv * 2 + g
                            for fc in range(2):
                                pb = B[3 + (fc % 2)]
                                for c_ in range(16):
                                    rhs = XT[:, comb, c_, 0:511] if c_ < 8 else XT[:, comb, c_ - 8, 1:512]
                                    kk.mm(pb[:, 0:511], W1s[kv][:, c_, fc * 128:(fc + 1) * 128], rhs, start=(c_ == 0), stop=(c_ == 15))
                                kk.act(Hs[:, kv, g, fc, 0:511], pb[:, 0:511], AF.Silu, bias=biasv[:, kv, fc:fc + 1])
                    for g in range(2):
                        for fc in range(2):
                            kk.mm(B[5][0:64, 0:511], W2s[0][:, fc, :], Hs[:, 0, g, fc, 0:511], start=(fc == 0), stop=(fc == 1))
                        kk.cp(kcT_s[:, g, 0:511], B[5][0:64, 0:511])
                        for ch in range(4):
                            m = 128 if ch < 3 else 127
                            for fc in range(2):
                                kk.mm(B[6][0:m, ch * 64:(ch + 1) * 64], Hs[:, 1, g, fc, ch * 128:ch * 128 + m], W2s[1][:, fc, :], start=(fc == 0), stop=(fc == 1))
                            kk.cp(vcO_s[0:m, ch, g, 0:64], B[6][0:m, ch * 64:(ch + 1) * 64])

                def s1_tail(b):
                    for g in range(2):
                        kk.mm(B[0][0:4, 0:511], QTs[0:64, g * 4:(g + 1) * 4, b], kcT_s[:, g, 0:511])
                        kk.ts(sc_s[:, 0:511], B[0][0:4, 0:511], 0.125, None, ALU.mult)
                        kk.memset(p_s[:, 511:512], 0.0)
                        softmax4(511)
                        transposes4(4)
                        for ch in range(4):
                            m = 128 if ch < 3 else 127
                            kk.mm(B[7][0:4, 0:193], pT_s[0:m, ch, :], vcO_s[0:m, ch, g, 0:193], start=(ch == 0), stop=(ch == 3))
                        kk.ts(ocn[:], B[7][0:4, 0:64], rs[:, 0:1], None, ALU.mult)
                        kk.dma(ocmp[0:1, b, g * 256:(g + 1) * 256].rearrange('p (h d) -> p h d', h=4), ocn[:])
                        kk.ts(imph[:, 0:129], B[7][0:4, 64:193], rs[:, 0:1], None, ALU.mult)
                        kk.mm(B[0][0:4, 0:129], ones_f[0:4, 0:4], imph[:, 0:129])
                        kk.cp(impg[:, 0:129], B[0][0:4, 0:129])
                        kk.memset(impg[:, 0:1], 1e9, eng='dve')
                        kk.memset(impg[:, 127:129], 1e9, eng='dve')
                        S.add('dve', lambda e: e.max(out=m8[:], in_=impg[:, 0:129]), reads=[impg[:, 0:129]], writes=[m8[:]])
                        S.add('dve', lambda e: e.match_replace(out=imp2[:, 0:129], in_to_replace=m8[:], in_values=impg[:, 0:129], imm_value=-3e9), reads=[m8[:], impg[:, 0:129]], writes=[imp2[:, 0:129]])
                        S.add('dve', lambda e: e.max(out=m8[:], in_=imp2[:, 0:129]), reads=[imp2[:, 0:129]], writes=[m8[:]])
                        kk.ts(sel[:, 0:129], impg[:, 0:129], m8[:, 7:8], None, ALU.is_ge)
                        kk.ts(negsel_s[:, b, g, 0:129], sel[:, 0:129], -1.0, 1e30, ALU.add, ALU.mult)

                s1_gath(0)
                s1_comp(0)
                for b in range(NS):
                    if b + 1 < NS:
                        s1_gath(b + 1)
                    s1_tail(b)
                    if b + 1 < NS:
                        s1_comp(b + 1)
                S.barrier(); S.flush()

            with ExitStack() as s2:
                idr_i = sb(s2, 'idr_i', [128, NS * 64], I32)
                idr_f = sb(s2, 'idr_f', [128, NS * 64])
                idr_u = sb(s2, 'idr_u', [128, NS * 64], I32)
                Kp = [sb(s2, 'Kp0', [128, 8, 256]), sb(s2, 'Kp1', [128, 8, 256])]
                KsTs = [sb(s2, 'KsT0', [128, 8320], BF16), sb(s2, 'KsT1', [128, 8320], BF16)]
                Vs_ss = [sb(s2, 'Vs_s0', [128, 64, 128], BF16), sb(s2, 'Vs_s1', [128, 64, 128], BF16)]
                Wp = sb(s2, 'Wp', [128, 4, 256])
                KwT = sb(s2, 'KwT', [64, 2, 640], BF16)
                Vw_s = sb(s2, 'Vw_s', [128, 4, 128], BF16)
                acc1 = sb(s2, 'acc1', [1, 512])
                tmp1 = sb(s2, 'tmp1', [1, 512])
                catb = sb(s2, 'catbs', [1, D], BF16)
                catT = sb(s2, 'catTs', [128, 8, 1], BF16)
                xo = xt1
                csv = I['cslc']
                kk.dma(idr_i[:], bass.AP(I['ptab'].tensor, 0, [[0, 128], [1, NS * 64]]))
                kk.cp(idr_f[:], idr_i[:])
                kk.ts(idr_f[:], idr_f[:], 128.0, cst2[:, 1:2], ALU.mult, ALU.add)
                kk.cp(idr_u[:], idr_f[:])
                def gather_gen(b):
                    KsT, Vs_s = KsTs[b % 2], Vs_ss[b % 2]
                    for j0 in range(0, 64, 8):
                        kp = Kp[(j0 // 8) % 2]
                        for jj in range(8):
                            j = b * 64 + j0 + jj
                            S.add('pool', lambda e, kp=kp, jj=jj, j=j: e.indirect_dma_start(out=kp[:, jj, :], out_offset=None, in_=csv, in_offset=bass.IndirectOffsetOnAxis(ap=idr_u[:, j:j + 1], axis=0)),
                                  reads=[idr_u[:, j:j + 1], csv], writes=[kp[:, jj, :]], dma=True)
                        for q_ in range(2):
                            pb = B[1 + q_]
                            for jj in range(4):
                                kk.tr(pb[:, jj * 128:(jj + 1) * 128], kp[:, q_ * 4 + jj, 0:128], ident_f)
                            kk.cpr(KsT[:, (j0 + q_ * 4) * 128:(j0 + q_ * 4 + 4) * 128], pb[:, :])
                        kk.cpr(Vs_s[:, j0:j0 + 8, :], kp[:, :, 128:256])
                        yield
                    kk.cp(KsT[0:64, 8192:8193], kn_s[:, 0, b:b + 1])
                    kk.cp(KsT[64:128, 8192:8193], KAs[64:128, 1, b:b + 1])
                    yield

                def attn_gen(b):
                    KsT, Vs_s = KsTs[b % 2], Vs_ss[b % 2]
                    ti = NT + b
                    row0 = T + b
                    for g in range(2):
                        lq = qSr[:, 0:4, b] if g == 0 else QTs[64:128, 4:8, b]
                        kT = KsT[0:64, :] if g == 0 else KsT[64:128, :]
                        for kb in range(0, 8192, 512):
                            pb = B[3 + (kb // 512) % 2]
                            kk.mm(pb[0:4, :], lq, kT[:, kb:kb + 512])
                            kk.stt(sc_s[:, kb:kb + 512].rearrange('p (a k) -> p a k', k=64), pb[0:4, :].rearrange('p (a k) -> p a k', k=64), 0.125,
                                   bc(negsel_s[:, b, g, kb // 64:kb // 64 + 8].rearrange('p (a o) -> p a o', o=1), [4, 8, 64]), ALU.mult, ALU.add)
                        kk.mm(B[3][0:4, 0:1], lq, kT[:, 8192:8193])
                        kk.ts(sc_s[:, 8192:8193], B[3][0:4, 0:1], 0.125, None, ALU.mult)
                        yield
                        softmax4(8193)
                        yield
                        transposes4(65)
                        yield
                        for j in range(64):
                            kk.mm(B[5][0:4, 0:64], pT_s[:, j, :], Vs_s[:, j, g * 64:(g + 1) * 64], start=(j == 0), stop=False)
                        kk.mm(B[5][0:4, 0:64], pT_s[0:1, 64, :], VSs[0:1, b, 0, g * 64:(g + 1) * 64], start=False, stop=True)
                        kk.ts(ocn[:], B[5][0:4, 0:64], rs[:, 0:1], None, ALU.mult)
                        kk.dma(osw[0:1, 0, g * 256:(g + 1) * 256].rearrange('p (h d) -> p h d', h=4), ocn[:])
                    yield
                    kk.dma(Wp[:], I['cwin'][b].rearrange('(c p) f -> p c f', p=128))
                    for g in range(2):
                        for c_ in range(4):
                            kk.tr(B[1][0:64, c_ * 128:(c_ + 1) * 128], Wp[:, c_, g * 64:(g + 1) * 64], ident_f)
                        kk.cpr(KwT[:, g, 0:512], B[1][0:64, :])
                        kk.cp(KwT[:, g, 512:513], kn_w[:, g, b:b + 1])
                    kk.cpr(Vw_s[:], Wp[:, :, 128:256])
                    kk.memset(p_s[:, 513:640], 0.0)
                    for g in range(2):
                        lq = qSr[:, g * 4:(g + 1) * 4, b]
                        kk.mm(B[3][0:4, :], lq, KwT[:, g, 0:512])
                        kk.ts(sc_s[:, 0:512], B[3][0:4, :], 0.125, None, ALU.mult)
                        kk.mm(B[4][0:4, 0:1], lq, KwT[:, g, 512:513])
                        kk.ts(sc_s[:, 512:513], B[4][0:4, 0:1], 0.125, None, ALU.mult)
                        kk.memset(sc_s[:, 0:1], NEG, eng='dve')
                        yield
                        softmax4(513)
                        yield
                        transposes4(5)
                        for j in range(4):
                            kk.mm(B[5][0:4, 0:64], pT_s[:, j, :], Vw_s[:, j, g * 64:(g + 1) * 64], start=(j == 0), stop=False)
                        kk.mm(B[5][0:4, 0:64], pT_s[0:1, 4, :], VSs[0:1, b, 1, g * 64:(g + 1) * 64], start=False, stop=True)
                        kk.ts(ocn[:], B[5][0:4, 0:64], rs[:, 0:1], None, ALU.mult)
                        kk.dma(osw[0:1, 1, g * 256:(g + 1) * 256].rearrange('p (h d) -> p h d', h=4), ocn[:])
                    yield
                    gv = Gs[0:1, b, :].rearrange('p (h r) -> p h r', r=3)
                    for br in range(3):
                        dst = acc1 if br == 0 else tmp1
                        osrc = ocmp[0:1, b, :] if br == 0 else osw[0:1, br - 1, :]
                        kk.tt(dst[0:1, :].rearrange('p (h d) -> p h d', h=8), osrc.rearrange('p (h d) -> p h d', h=8), bc(gv[:, :, br:br + 1], [1, 8, 64]), ALU.mult)
                        if br > 0:
                            kk.tt(acc1[0:1, :], acc1[0:1, :], tmp1[0:1, :], ALU.add)
                    kk.cp(catb[0:1, 0:512], acc1[0:1, :], eng='act')
                    kk.cp(catb[0:1, 512:1024], YCs[0:1, b, :], eng='dve')
                    pst = bfv(B[6], 8, 128)
                    for kc in range(8):
                        kk.tr(pst[:, kc, 0:1], catb[0:1, kc * 128:(kc + 1) * 128], ident_bf[0:1, 0:1])
                    kk.cpr(catT[:, :, 0:1], pst[:, :, 0:1])
                    xt = xt1
                    kk.dma(xt[0:1], I['xs'][b:b + 1, :])
                    for cb in range(2):
                        pb = B[1 + cb]
                        for kc in range(8):
                            kk.mm(pb[0:1, :], catT[:, kc, 0:1], Wout[:, kc, cb * 512:(cb + 1) * 512], start=(kc == 0), stop=(kc == 7))
                        kk.tt(xo[0:1, cb * 512:(cb + 1) * 512], pb[0:1, :], xt[0:1, cb * 512:(cb + 1) * 512], ALU.add)
                    kk.dma(xres[row0:row0 + 1, :], xo[0:1, :], q='pool')
                    yield

                for _ in gather_gen(0):
                    pass
                for b in range(NS):
                    rr(attn_gen(b), gather_gen(b + 1) if b + 1 < NS else None)
                S.barrier(); S.flush()

        def ffn_phase(l):
            wgs, wus = WGS[l], WUS[l]
            with ExitStack() as ph:
                W = common_work(ph)
                B = W['B']
                WD = sb(ph, 'WD', [128, 22, D], BF16)
                gfin = sb(ph, 'gfin', [128, D])
                stgW = sb(ph, 'stgFW', [128, 4, D])

                def load_wd_chunk(ci):
                    k0 = ci * 4
                    kn = min(4, 22 - k0)
                    kk.dma(stgW[:, 0:kn, :], I['wd'][l][k0 * 128:(k0 + kn) * 128, :].rearrange('(k p) w -> p k w', p=128), q='pool')
                    kk.cp(WD[:, k0:k0 + kn, :], stgW[:, 0:kn, :], eng='act')
                kk.dma(W['gain'][:], dram_bcast(I['norm_ffn'][l], 128, D))
                kk.dma(gfin[:], dram_bcast(I['norm_final'], 128, D))
                xnTbs = [sb(ph, 'xnTb0', [128, 8, 516], BF16), sb(ph, 'xnTb1', [128, 8, 516], BF16)]
                hT = sb(ph, 'hTf', [128, 22, 516], BF16)
                wgf = [sb(ph, 'wgf%d' % q_, [128, 8, 128], BF16) for q_ in range(4)]
                wuf = [sb(ph, 'wuf%d' % q_, [128, 8, 128], BF16) for q_ in range(4)]
                sgs = [sb(ph, 'sg0', [128, 516]), sb(ph, 'sg1', [128, 516])]
                xo = sb(ph, 'xof', [128, D])
                yo = sb(ph, 'yof', [128, D])
                xr = [sb(ph, 'xr0', [128, D]), sb(ph, 'xr1', [128, D])]
                blocks = []
                for blk in range(4):
                    btiles = [tl for tl in tiles if (tl[4] is None and tl[0] // 4 == blk) or (tl[4] is not None and blk == 3)]
                    cols = {}
                    c = 0
                    for tl in btiles:
                        cols[tl[0]] = c
                        c += tl[1]
                    blocks.append((btiles, cols, c))

                def norm_tile(blk, k_):
                    btiles, cols, ntok = blocks[blk]
                    if k_ >= len(btiles):
                        return
                    (ti, n, pos0, row0, sb_) = btiles[k_]
                    xt = W['xt'][ti % 2]
                    kk.dma(xt[:n], xres[row0:row0 + n, :])
                    norm_T(W, xt, n)
                    kk.cpr(xnTbs[blk % 2][:, :, cols[ti]:cols[ti] + n], W['xnT'][:, :, :n])

                def norm_block(blk):
                    for k_ in range(len(blocks[blk][0])):
                        norm_tile(blk, k_)

                def gateup_block(blk):
                    btiles, cols, ntok = blocks[blk]
                    xnTb = xnTbs[blk % 2]
                    segs = [(0, min(512, ntok))] + ([(512, ntok)] if ntok > 512 else [])
                    for fp in range(11):
                        hb = fp % 2
                        for fi in range(2):
                            f_ = fp * 2 + fi
                            kk.dma(wgf[f_ % 4][:].rearrange('p k w -> p (k w)'), wgs[f_])
                            kk.dma(wuf[f_ % 4][:].rearrange('p k w -> p (k w)'), wus[f_])
                        for fi in range(2):
                            f = fp * 2 + fi
                            for si, (s0, s1) in enumerate(segs):
                                w = s1 - s0
                                if si == 0:
                                    pg = B[1 + (f % 2) * 2]
                                    pu = B[2 + (f % 2) * 2]
                                else:
                                    pg = B[5]
                                    pu = B[6]
                                for kc in range(8):
                                    kk.mm(pg[:, 0:w], wgf[f % 4][:, kc, :], xnTb[:, kc, s0:s1], start=(kc == 0), stop=(kc == 7))
                                for kc in range(8):
                                    kk.mm(pu[:, 0:w], wuf[f % 4][:, kc, :], xnTb[:, kc, s0:s1], start=(kc == 0), stop=(kc == 7))
                                sg = sgs[f % 2]
                                kk.act(sg[:, s0:s1], pg[:, 0:w], AF.Silu)
                                kk.tt(hT[:, f, s0:s1], sg[:, s0:s1], pu[:, 0:w], ALU.mult)
                        if blk + 1 < 4:
                            norm_tile(blk + 1, fp)
                        if blk == 0 and fp < 6:
                            load_wd_chunk(fp)
                        if l == 0 and blk in (1, 2):
                            next(pre_next, None)

                def down_block(blk):
                    btiles, cols, ntok = blocks[blk]
                    for (ti, n, pos0, row0, sb_) in btiles:
                        c = cols[ti]
                        xt = xr[ti % 2]
                        kk.dma(xt[:n], xres[row0:row0 + n, :], q='pool')
                        for cb in range(2):
                            pb = B[5 + cb]
                            for f in range(22):
                                kk.mm(pb[:n, :], hT[:, f, c:c + n], WD[:, f, cb * 512:(cb + 1) * 512], start=(f == 0), stop=(f == 21))
                            kk.tt(xo[:n, cb * 512:(cb + 1) * 512], pb[:n, :], xt[:n, cb * 512:(cb + 1) * 512], ALU.add)
                        if l == 0:
                            kk.dma(xres[row0:row0 + n, :], xo[:n, :], q='pool')
                        else:
                            rstd_of(yo, xo, n, D, ssq2, rstd2)
                            kk.stt(yo[:n], xo[:n], rstd2[:n, 0:1], gfin[:n], ALU.mult, ALU.mult)
                            dst = O['y_p'][row0:row0 + n, :] if sb_ is None else O['y_s'][sb_:sb_ + 1, :]
                            kk.dma(dst, yo[:n, :], q='pool')

                ssq2 = sb(ph, 'ssq2', [128, 1])
                rstd2 = sb(ph, 'rstd2', [128, 1])
                pre_next = iter(())
                norm_block(0)
                for blk in range(4):
                    gateup_block(blk)
                    down_block(blk)
                for _ in pre_next:
                    pass
                S.barrier(); S.flush()

        ffn_phase(0)

        with ExitStack() as ph:
            W = common_work(ph)
            B = W['B']
            WC = sb(ph, 'WC', [128, 8, 2 * LRU], BF16)
            WOC = sb(ph, 'WOC', [128, 10, D], BF16)
            LWA = sb(ph, 'LWA', [128, 10, 128], BF16)
            LWX = sb(ph, 'LWX', [128, 10, 128], BF16)
            lcw_f = sb(ph, 'lcw_f', [128, 10, 4])
            lcb_f = sb(ph, 'lcb_f', [128, 10])
            lba_f = sb(ph, 'lba_f', [128, 10])
            lbx_f = sb(ph, 'lbx_f', [128, 10])
            c8 = sb(ph, 'c8', [128, 10])
            dg = sb(ph, 'dgw', [128, 40, 128])
            lcb_row = sb(ph, 'lcb_row', [1, LRU])
            c8x2 = sb(ph, 'c8x2', [128, 10])
            with ExitStack() as ld:
                stg = [sb(ld, 'stgL0', [128, 8, 512]), sb(ld, 'stgL1', [128, 8, 512])]
                for i, c0 in enumerate(range(0, 2 * LRU, 512)):
                    load_w(WC[:, :, c0:c0 + 512], I['winc'][:, c0:c0 + 512], stg[i % 2], 8, 512)
                for i, c0 in enumerate(range(0, D, 512)):
                    kk.dma(stg[i % 2][:, 0:8, :], I['woutc'][0:1024, c0:c0 + 512].rearrange('(k p) w -> p k w', p=128))
                    kk.cpr(WOC[:, 0:8, c0:c0 + 512], stg[i % 2][:, 0:8, :])
                    kk.dma(stg[i % 2][:, 0:2, :], I['woutc'][1024:1280, c0:c0 + 512].rearrange('(k p) w -> p k w', p=128))
                    kk.cpr(WOC[:, 8:10, c0:c0 + 512], stg[i % 2][:, 0:2, :])
                sva = stg[0].rearrange('p a (b c) -> p (a b) c', b=4)[:, 0:10, :]
                svx = stg[1].rearrange('p a (b c) -> p (a b) c', b=4)[:, 0:10, :]
                kk.dma(sva, I['lwa'].rearrange('h i j -> i h j'))
                kk.cpr(LWA[:], sva)
                kk.dma(svx, I['lwx'].rearrange('h i j -> i h j'))
                kk.cpr(LWX[:], svx)
                for k_ in range(4):
                    kk.dma(lcw_f[:, :, k_], I['lcw'][k_].rearrange('(c p) -> p c', p=128), slow=True)
                kk.dma(lcb_f[:], I['lcb'].rearrange('(c p) -> p c', p=128), slow=True)
                kk.dma(lba_f[:], I['lba'].rearrange('(c p) -> p c', p=128), slow=True)
                kk.dma(lbx_f[:], I['lbx'].rearrange('(c p) -> p c', p=128), slow=True)
                kk.dma(c8[:], I['lam'].rearrange('(c p) -> p c', p=128), slow=True)
                kk.dma(W['gain'][:], dram_bcast(I['norm_mix'][1], 128, D))
                kk.dma(lcb_row[:], I['lcb'].rearrange('(o c) -> o c', o=1))
                for h_ in range(10):
                    for k_ in range(4):
                        kk.ts(dg[:, h_ * 4 + k_, :], ident_f, lcw_f[:, h_, k_:k_ + 1], None, ALU.mult, eng=('dve', 'pool')[k_ % 2])
                kk.act(c8[:], c8[:], AF.Exp, scale=-1.0)
                kk.act(c8[:], c8[:], AF.Ln, bias=1.0)
                kk.ts(c8[:], c8[:], -8.0, None, ALU.mult)
                kk.ts(c8x2[:], c8[:], 2.0, None, ALU.mult)
                S.barrier(); S.flush()
            cbuf = sb(ph, 'lcbuf', [128, 10, 131])
            c3 = sb(ph, 'lc3', [128, 10, 3])
            gq = sb(ph, 'gq', [128, 10, 128])
            gsb = sb(ph, 'gsb', [128, 10, 128])
            rg = sb(ph, 'rg', [128, 10, 128])
            ig = sb(ph, 'ig', [128, 10, 128])
            av = sb(ph, 'av', [128, 10, 128])
            hh = sb(ph, 'hh', [128, 10, 128])
            hst = sb(ph, 'hst', [128, 10])
            yT = sb(ph, 'yT', [128, 10, 128], BF16)
            xo = sb(ph, 'xol', [128, D])
            cas = [sb(ph, 'lca0', [128, 10, 128]), sb(ph, 'lca1', [128, 10, 128])]
            gus = [sb(ph, 'gu0', [128, 10, 128]), sb(ph, 'gu1', [128, 10, 128])]
            xcbs = [sb(ph, 'xcb0', [128, 10, 128], BF16), sb(ph, 'xcb1', [128, 10, 128], BF16)]
            ssqs = [sb(ph, 'lssq0', [128, 1]), sb(ph, 'lssq1', [128, 1])]
            rstds = [sb(ph, 'lrstd0', [128, 1]), sb(ph, 'lrstd1', [128, 1])]

            def bodyL(i):
                (ti, n, pos0, row0, sb_) = tiles[i]
                par = i % 2
                pa, pb, pc, pd = [B[4 * par + k_] for k_ in range(4)]
                ca, gu, xcb = cas[par], gus[par], xcbs[par]
                ssq, rstd = ssqs[par], rstds[par]
                xt = W['xt'][par]
                xn, xnT = W['xn'], W['xnT']

                def slot(h):
                    bk = (pb, pc, pd)[h // 4]
                    return bk[:, (h % 4) * 128:(h % 4) * 128 + n]

                def slots3():
                    return [(pb[:, :].rearrange('p (c t) -> p c t', c=4)[:, :, :n], 0, 4),
                            (pc[:, :].rearrange('p (c t) -> p c t', c=4)[:, :, :n], 4, 4),
                            (pd[:, 0:256].rearrange('p (c t) -> p c t', c=2)[:, :, :n], 8, 2)]
                kk.dma(xt[:n], xres[row0:row0 + n, :])
                rstd_of(W['junk'], xt, n, D, ssq, rstd)
                yield
                kk.stt(xn[:n], xt[:n], rstd[:n, 0:1], W['gain'][:n], ALU.mult, ALU.mult)
                yield
                psT = bfv(pa, 8, 128)
                for kc in range(8):
                    kk.tr(psT[:, kc, :n], xn[:n, kc * 128:(kc + 1) * 128], ident_bf[:n, :n])
                yield
                kk.cp(xnT[:, :, :n], psT[:, :, :n], eng='act')
                yield
                for h in range(10):
                    fc = 10 + h
                    for kc in range(8):
                        kk.mm(slot(h), WC[:, kc, fc * 128:(fc + 1) * 128], xnT[:, kc, :n], start=(kc == 0), stop=(kc == 7))
                yield
                if ti == 0:
                    kk.memset(cbuf[:, :, 0:3], 0.0)
                if sb_ is not None:
                    for j_ in range(3):
                        kk.dma(cbuf[:, :, j_], I['slconv'][sb_, j_].rearrange('(c p) -> p c', p=128), slow=True)
                for (pv_, h0, k_) in slots3():
                    kk.cpr(cbuf[:, h0:h0 + k_, 3:3 + n], pv_)
                yield
                if sb_ is not None or ti == NT - 1:
                    kk.cp(c3[:], cbuf[:, :, n:n + 3], eng='act')
                    dst = O['lconv_p'] if sb_ is None else O['lconv_s'][sb_]
                    for j_ in range(3):
                        kk.dma(dst[j_].rearrange('(c p) -> p c', p=128), c3[:, :, j_], q='pool', slow=True)
                for h in range(10):
                    for k_ in range(4):
                        kk.mm(slot(h), dg[:, h * 4 + k_, :], cbuf[:, h, k_:k_ + n], start=(k_ == 0), stop=False)
                    kk.mm(slot(h), lcb_row[0:1, h * 128:(h + 1) * 128], ones_f[0:1, 0:n], start=False, stop=True)
                yield
                for (pv_, h0, k_) in slots3():
                    kk.act(ca[:, h0:h0 + k_, :n], pv_, AF.Copy)
                    kk.act(xcb[:, h0:h0 + k_, :n], pv_, AF.Copy)
                if sb_ is None and ti < NT - 1:
                    kk.cp(c3[:], cbuf[:, :, n:n + 3], eng='act')
                    kk.cp(cbuf[:, :, 0:3], c3[:], eng='act')
                yield
                for h in range(10):
                    for kc in range(8):
                        kk.mm(slot(h), WC[:, kc, h * 128:(h + 1) * 128], xnT[:, kc, :n], start=(kc == 0), stop=(kc == 7))
                yield
                for (pv_, h0, k_) in slots3():
                    kk.act(gsb[:, h0:h0 + k_, :n], pv_, AF.Copy)
                yield
                kk.tt(gq[:, :, :n], gsb[:, :, :n], gsb[:, :, :n], ALU.mult)
                kk.ts(gq[:, :, :n], gq[:, :, :n], 0.044715, 1.0, ALU.mult, ALU.add)
                kk.tt(gq[:, :, :n], gq[:, :, :n], gsb[:, :, :n], ALU.mult)
                yield
                kk.act(gq[:, :, :n], gq[:, :, :n], AF.Sigmoid, scale=1.5957691216057308)
                yield
                kk.tt(gu[:, :, :n], gq[:, :, :n], gsb[:, :, :n], ALU.mult)
                for h in range(10):
                    kk.mm(slot(h), LWA[:, h, :], xcb[:, h, :n])
                yield
                for h in range(10):
                    kk.act(rg[:, h, :n], slot(h), AF.Sigmoid, bias=lba_f[:, h:h + 1])
                yield
                for h in range(10):
                    kk.mm(slot(h), LWX[:, h, :], xcb[:, h, :n])
                kk.tt(rg[:, :, :n], rg[:, :, :n], bc(c8[:].rearrange('p (c o) -> p c o', o=1), [128, 10, n]), ALU.mult)
                yield
                for h in range(10):
                    kk.act(ig[:, h, :n], slot(h), AF.Sigmoid, bias=lbx_f[:, h:h + 1])
                kk.act(av[:, :, :n], rg[:, :, :n], AF.Exp)
                kk.act(rg[:, :, :n], rg[:, :, :n], AF.Exp, scale=2.0)
                kk.act(rg[:, :, :n], rg[:, :, :n], AF.Ln, scale=-1.0, bias=1.0)
                kk.act(rg[:, :, :n], rg[:, :, :n], AF.Exp, scale=0.5)
                yield
                if ti == 0:
                    kk.memset(hst[:], 0.0)
                if sb_ is not None:
                    kk.dma(hst[:], I['slru'][sb_].rearrange('(c p) -> p c', p=128), slow=True)
                kk.tt(ig[:, :, :n], ig[:, :, :n], ca[:, :, :n], ALU.mult)
                kk.tt(ig[:, :, :n], ig[:, :, :n], rg[:, :, :n], ALU.mult)
                for h in range(10):
                    S.add('dve', lambda e, h=h: e.tensor_tensor_scan(out=hh[:, h, :n], data0=av[:, h, :n], data1=ig[:, h, :n], initial=hst[:, h:h + 1], op0=ALU.mult, op1=ALU.add),
                          reads=[av[:, h, :n], ig[:, h, :n], hst[:, h:h + 1]], writes=[hh[:, h, :n]])
                kk.cp(hst[:], hh[:, :, n - 1])
                if sb_ is not None or ti == NT - 1:
                    dst = O['lru_p'] if sb_ is None else O['lru_s'][sb_]
                    kk.dma(dst.rearrange('(c p) -> p c', p=128), hst[:], q='pool', slow=True)
                kk.tt(yT[:, :, :n], gu[:, :, :n], hh[:, :, :n], ALU.mult)
                yield
                for cb in range(2):
                    pbk = (pb, pc)[cb]
                    for h in range(10):
                        kk.mm(pbk[:n, :], yT[:, h, :n], WOC[:, h, cb * 512:(cb + 1) * 512], start=(h == 0), stop=(h == 9))
                yield
                for cb in range(2):
                    pbk = (pb, pc)[cb]
                    kk.tt(xo[:n, cb * 512:(cb + 1) * 512], pbk[:n, :], xt[:n, cb * 512:(cb + 1) * 512], ALU.add)
                kk.dma(xres[row0:row0 + n, :], xo[:n, :], q='pool')
                yield

            zipp([(lambda i=i: bodyL(i)) for i in range(len(tiles))], 13, 'BABBABBABBBBABBBABA')
            S.barrier(); S.flush()

        ffn_phase(1)

        S.barrier(); S.flush()
    return nc


def _consts():
    p = np.arange(128)[:, None]
    j = np.arange(128)[None, :]
    c = np.zeros((128, 768), np.float32)
    c[:, 0:128] = (p == j)
    c[:, 128:256] = (p > j)
    c[:, 256:384] = (p <= j)
    c[:, 384:512] = np.where(j <= p, 0.0, NEG)
    c[:, 512:640] = np.where(j <= p, NEG, 0.0)
    c[:, 640:768] = 1.0
    half = 32
    inv_freq = (np.float32(10000.0) ** (-(np.arange(half, dtype=np.float32)) / np.float32(half))).astype(np.float32)
    pos = np.concatenate([np.arange(T), np.full(NS, 8192)]).astype(np.float32)
    ang = (pos[:, None] * inv_freq[None, :]).astype(np.float32)
    rt = np.concatenate([np.cos(ang), np.sin(ang)], axis=1).astype(np.float32)
    n = np.arange(128)[:, None] * 16
    s = np.arange(32)[None, :] * 64
    ovl = ((n < s + 64) & (n + 32 > s)).astype(np.float32)
    c2 = np.zeros((128, 4), np.float32)
    c2[:, 0] = np.arange(128) % 8
    c2[:, 1] = np.arange(128)
    n5 = np.arange(512)[:, None] * 16
    s5 = np.arange(129)[None, :] * 64
    ovls = ((n5 < s5 + 64) & (n5 + 32 > s5)).astype(np.float32)
    ovls[511, :] = 0.0
    return c, rt, ovl, c2, ovls


_NC_CACHE = {}


def kernel(x_prompt, x_sample, cache_kv_cmp, cache_kv_slc, cache_kv_win, state_ssm, state_ssd_conv,
           state_lru, state_lru_conv, page_table, norm_mix, norm_ffn, norm_final, w_ffn_gate, w_ffn_up,
           w_ffn_down, w_in_a, w_out_a, cmp_pe_k, cmp_w1_k, cmp_w2_k, cmp_pe_v, cmp_w1_v, cmp_w2_v,
           ssd_conv_w, ssd_conv_b, ssd_dt_bias, ssd_a_log, ssd_d, ssd_norm, w_in_c, lru_conv_w, lru_conv_b,
           lru_w_a, lru_b_a, lru_w_x, lru_b_x, lru_lambda, w_out_c):
    f = lambda a: np.ascontiguousarray(np.asarray(a, dtype=np.float32))
    if 'nc' not in _NC_CACHE:
        _NC_CACHE['nc'] = build_program()
    nc = _NC_CACHE['nc']
    cst, rt, ovl, c2, ovls = _consts()
    ccmp = f(cache_kv_cmp).reshape(NPHYS * 128, 256)
    cslc = f(cache_kv_slc).reshape(NPHYS * 128, 256)
    shared = dict(
        ccmp=ccmp, cslc=cslc,
        norm_mix=f(norm_mix), norm_ffn=f(norm_ffn), norm_final=f(norm_final),
        wg=f(w_ffn_gate), wu=f(w_ffn_up), wd=f(w_ffn_down), win=f(w_in_a)[0], wout=f(w_out_a)[0],
        pek=f(cmp_pe_k)[0], w1k=f(cmp_w1_k)[0], w2k=f(cmp_w2_k)[0],
        pev=f(cmp_pe_v)[0], w1v=f(cmp_w1_v)[0], w2v=f(cmp_w2_v)[0],
        cw=f(ssd_conv_w)[0], cb=f(ssd_conv_b)[0], dtb=f(ssd_dt_bias)[0], alog=f(ssd_a_log)[0],
        dsk=f(ssd_d)[0], snorm=f(ssd_norm)[0],
        winc=f(w_in_c)[0], lcw=f(lru_conv_w)[0], lcb=f(lru_conv_b)[0], lwa=f(lru_w_a)[0], lba=f(lru_b_a)[0],
        lwx=f(lru_w_x)[0], lbx=f(lru_b_x)[0], lam=f(lru_lambda)[0], woutc=f(w_out_c)[0],
        cst=cst, ropetab=rt, ovl=ovl, cst2=c2, ovls=ovls,
    )
    xp = f(x_prompt)
    xs = f(x_sample)[:, 0, :]
    pt = np.ascontiguousarray(np.asarray(page_table, dtype=np.int32))
    in_maps = []
    for c in range(8):
        sl = slice(NS * c, NS * (c + 1))
        m = dict(shared)
        m.update(
            xp=xp[c], xs=np.ascontiguousarray(xs[sl]),
            cwin=np.ascontiguousarray(f(cache_kv_win)[0, sl].reshape(NS, 512, 256)),
            sssm=np.ascontiguousarray(f(state_ssm)[0, sl].reshape(NS, 512, 128)),
            sconv=np.ascontiguousarray(f(state_ssd_conv)[0, sl]),
            slru=np.ascontiguousarray(f(state_lru)[0, sl]),
            slconv=np.ascontiguousarray(f(state_lru_conv)[0, sl]),
            ptab=np.ascontiguousarray(pt[sl]),
        )
        in_maps.append(m)
    res = run_bass_kernel_spmd(nc, in_maps, core_ids=list(range(8))).results
    cat = lambda k: np.stack([np.asarray(r[k]) for r in res])
    cats = lambda k: np.concatenate([np.asarray(r[k]) for r in res], 0)
    y_prompt = cat('y_p').reshape(8, T, D)
    y_sample = cats('y_s').reshape(32, 1, D)
    kv_cmp_p = cat('kvc_p').reshape(1, 8, T, 2, 2, 64)
    kv_slc_p = cat('kvs_p').reshape(1, 8, T, 2, 2, 64)
    kv_win_p = cat('kvw_p').reshape(1, 8, 512, 2, 2, 64)
    ssm_p = cat('ssm_p').reshape(1, 8, 8, 64, 128)
    sconv_p = cat('sconv_p').reshape(1, 8, 3, 1024)
    lru_p = cat('lru_p').reshape(1, 8, LRU)
    lconv_p = cat('lconv_p').reshape(1, 8, 3, LRU)
    kv_cmp_s = cats('kvc_s').reshape(1, 32, 1, 2, 2, 64)
    kv_slc_s = cats('kvs_s').reshape(1, 32, 1, 2, 2, 64)
    kv_win_s = cats('kvw_s').reshape(1, 32, 512, 2, 2, 64)
    ssm_s = cats('ssm_s').reshape(1, 32, 8, 64, 128)
    sconv_s = cats('sconv_s').reshape(1, 32, 3, 1024)
    lru_s = cats('lru_s').reshape(1, 32, LRU)
    lconv_s = cats('lconv_s').reshape(1, 32, 3, LRU)
    outs = (y_prompt, y_sample, kv_cmp_p, kv_slc_p, kv_win_p, ssm_p, sconv_p, lru_p, lconv_p,
            kv_cmp_s, kv_slc_s, kv_win_s, ssm_s, sconv_s, lru_s, lconv_s)
    return tuple(np.ascontiguousarray(o, dtype=np.float32) for o in outs)
```
